# Optimizing a Trainium2 kernel written in Bass

```python
import jax, jax.numpy as jnp
from jax import lax
import numpy as np

D_MODEL = 1024
BATCH = 16
SEQ = 2048
DEPTH = 2

CHUNK = 64
N_MIXERS = 2
HEAD_DIM = 64
MEM_TOKENS = 256
MEM_HEADS = 4
MEM_WIDTH = MEM_HEADS * HEAD_DIM
TOK_WIDTH = D_MODEL - MEM_WIDTH
ATT_HEADS = TOK_WIDTH // HEAD_DIM
LEFT_CHUNKS = 8
BAND = (LEFT_CHUNKS + 1) * CHUNK
BAND_PAD = LEFT_CHUNKS * CHUNK
REL_CLIP = 128
N_REL = REL_CLIP + CHUNK
CONV_WIDTH = 31
CONV_CH = TOK_WIDTH
A_IN = 3 * TOK_WIDTH + MEM_WIDTH
B_IN = 2 * CONV_CH + MEM_WIDTH
D_FF = -(-8 * D_MODEL // (3 * 256)) * 256
EPS = 1e-6
NEG_INF = -1e30
ATTN_SCALE = HEAD_DIM ** -0.5

_DIST = np.arange(CHUNK)[:, None] - np.arange(BAND)[None, :] + BAND_PAD
REL_IDX = np.clip(_DIST, -(CHUNK - 1), REL_CLIP) + (CHUNK - 1)
BAND_OFF = np.arange(BAND) - BAND_PAD

kernel_name = "hybrid_chunkattn_conformerconv_memxattn"


def rms_norm(x, g):
    xf = x.astype(jnp.float32)
    y = xf * lax.rsqrt(jnp.mean(xf * xf, axis=-1, keepdims=True) + EPS)
    return (y * g.astype(jnp.float32)).astype(x.dtype)


def layer_norm(x, g, b):
    xf = x.astype(jnp.float32)
    mu = jnp.mean(xf, axis=-1, keepdims=True)
    xc = xf - mu
    y = xc * lax.rsqrt(jnp.mean(xc * xc, axis=-1, keepdims=True) + EPS)
    return (y * g.astype(jnp.float32) + b.astype(jnp.float32)).astype(x.dtype)


def chunk_relpos_attention(q, k, v, rel_bias):
    B, S, H, Dh = q.shape
    nc = S // CHUNK
    kp = jnp.pad(k, ((0, 0), (BAND_PAD, 0), (0, 0), (0, 0)))
    vp = jnp.pad(v, ((0, 0), (BAND_PAD, 0), (0, 0), (0, 0)))
    qc = q.reshape(B, nc, CHUNK, H, Dh).transpose(1, 0, 2, 3, 4)
    bias = rel_bias[:, REL_IDX].astype(jnp.float32)
    band_off = jnp.asarray(BAND_OFF, dtype=jnp.int32)

    def one_chunk(args):
        c, q_c = args
        start = c * CHUNK
        k_b = lax.dynamic_slice_in_dim(kp, start, BAND, axis=1)
        v_b = lax.dynamic_slice_in_dim(vp, start, BAND, axis=1)
        s = jnp.einsum('bqhd,bkhd->bhqk', q_c, k_b).astype(jnp.float32) * ATTN_SCALE + bias
        valid = (start + band_off) >= 0
        s = jnp.where(valid, s, NEG_INF)
        p = jax.nn.softmax(s, axis=-1).astype(v.dtype)
        return jnp.einsum('bhqk,bkhd->bqhd', p, v_b)

    out = lax.map(one_chunk, (jnp.arange(nc, dtype=jnp.int32), qc))
    return out.transpose(1, 0, 2, 3, 4).reshape(B, S, H * Dh)


def conformer_conv(u, conv_w, conv_b, ln_g, ln_b):
    a, gate = jnp.split(u, 2, axis=-1)
    h = a * jax.nn.sigmoid(gate)
    hp = jnp.pad(h, ((0, 0), (CONV_WIDTH - 1, 0), (0, 0)))
    y = lax.conv_general_dilated(
        hp, conv_w[:, None, :].astype(h.dtype), window_strides=(1,), padding='VALID',
        dimension_numbers=('NWC', 'WIO', 'NWC'), feature_group_count=CONV_CH)
    y = y + conv_b
    return jax.nn.silu(layer_norm(y, ln_g, ln_b))


def memory_attention(qm, mem_n, w_mem_kv, mq_g, mk_g):
    B, S, _ = qm.shape
    M = mem_n.shape[1]
    km, vm = jnp.split(mem_n @ w_mem_kv, 2, axis=-1)
    q = rms_norm(qm.reshape(B, S, MEM_HEADS, HEAD_DIM), mq_g)
    k = rms_norm(km.reshape(B, M, MEM_HEADS, HEAD_DIM), mk_g)
    v = vm.reshape(B, M, MEM_HEADS, HEAD_DIM)
    s = jnp.einsum('bshd,bmhd->bhsm', q, k).astype(jnp.float32) * ATTN_SCALE
    p = jax.nn.softmax(s, axis=-1).astype(v.dtype)
    return jnp.einsum('bhsm,bmhd->bshd', p, v).reshape(B, S, MEM_WIDTH)


def setup_inputs(seed: int = 0) -> dict:
    key = jax.random.key(seed)
    ks = jax.random.split(key, 22)
    n_a = (DEPTH + N_MIXERS - 1) // N_MIXERS
    n_b = DEPTH // N_MIXERS
    f32 = jnp.float32

    def w(k, shape, fan_in):
        return jax.random.normal(k, shape, f32) * (fan_in ** -0.5)

    def gain(k, shape):
        return 1.0 + 0.05 * jax.random.normal(k, shape, f32)

    def small(k, shape, s=0.02):
        return s * jax.random.normal(k, shape, f32)

    return {
        "x": jax.random.normal(ks[0], (BATCH, SEQ, D_MODEL), f32),
        "mem": jax.random.normal(ks[1], (BATCH, MEM_TOKENS, D_MODEL), f32),
        "norm1_g": gain(ks[2], (DEPTH, D_MODEL)),
        "mem_norm_g": gain(ks[3], (DEPTH, D_MODEL)),
        "a_w_in": w(ks[4], (n_a, D_MODEL, A_IN), D_MODEL),
        "a_q_g": gain(ks[5], (n_a, HEAD_DIM)),
        "a_k_g": gain(ks[6], (n_a, HEAD_DIM)),
        "a_rel_bias": small(ks[7], (n_a, ATT_HEADS, N_REL), 0.3),
        "b_w_in": w(ks[8], (n_b, D_MODEL, B_IN), D_MODEL),
        "b_b_in": small(ks[9], (n_b, B_IN)),
        "b_conv_w": w(ks[10], (n_b, CONV_WIDTH, CONV_CH), CONV_WIDTH),
        "b_conv_b": small(ks[11], (n_b, CONV_CH)),
        "b_ln_g": gain(ks[12], (n_b, CONV_CH)),
        "b_ln_b": small(ks[13], (n_b, CONV_CH)),
        "mq_g": gain(ks[14], (DEPTH, HEAD_DIM)),
        "mk_g": gain(ks[15], (DEPTH, HEAD_DIM)),
        "w_mem_kv": w(ks[16], (DEPTH, D_MODEL, 2 * MEM_WIDTH), D_MODEL),
        "w_out": w(ks[17], (DEPTH, D_MODEL, D_MODEL), D_MODEL),
        "norm2_g": gain(ks[18], (DEPTH, D_MODEL)),
        "w_gate": w(ks[19], (DEPTH, D_MODEL, D_FF), D_MODEL),
        "w_up": w(ks[20], (DEPTH, D_MODEL, D_FF), D_MODEL),
        "w_down": w(ks[21], (DEPTH, D_FF, D_MODEL), D_FF),
    }


def reference(x, mem, norm1_g, mem_norm_g, a_w_in, a_q_g, a_k_g, a_rel_bias,
              b_w_in, b_b_in, b_conv_w, b_conv_b, b_ln_g, b_ln_b,
              mq_g, mk_g, w_mem_kv, w_out, norm2_g, w_gate, w_up, w_down):
    B, S, _ = x.shape
    for i in range(DEPTH):
        j = i // N_MIXERS
        h = rms_norm(x, norm1_g[i])
        mem_n = rms_norm(mem, mem_norm_g[i])
        if i % N_MIXERS == 0:
            z = h @ a_w_in[j]
            q, k, v, qm = jnp.split(z, [TOK_WIDTH, 2 * TOK_WIDTH, 3 * TOK_WIDTH], axis=-1)
            q = rms_norm(q.reshape(B, S, ATT_HEADS, HEAD_DIM), a_q_g[j])
            k = rms_norm(k.reshape(B, S, ATT_HEADS, HEAD_DIM), a_k_g[j])
            v = v.reshape(B, S, ATT_HEADS, HEAD_DIM)
            tok = chunk_relpos_attention(q, k, v, a_rel_bias[j])
        else:
            z = h @ b_w_in[j] + b_b_in[j]
            u, qm = jnp.split(z, [2 * CONV_CH], axis=-1)
            tok = conformer_conv(u, b_conv_w[j], b_conv_b[j], b_ln_g[j], b_ln_b[j])
        memo = memory_attention(qm, mem_n, w_mem_kv[i], mq_g[i], mk_g[i])
        x = x + jnp.concatenate([tok, memo], axis=-1) @ w_out[i]
        h2 = rms_norm(x, norm2_g[i])
        x = x + (jax.nn.silu(h2 @ w_gate[i]) * (h2 @ w_up[i])) @ w_down[i]
    return x
```

```python
import numpy as np
from contextlib import ExitStack
import concourse.bass as bass
import concourse.mybir as mybir
from concourse.bass_utils import run_bass_kernel_spmd

F32 = mybir.dt.float32
BF16 = mybir.dt.bfloat16
U8 = mybir.dt.uint8
AF = mybir.ActivationFunctionType
ALU = mybir.AluOpType
AX = mybir.AxisListType

PE, ACT, DVE, POOL, SP = range(5)

S = 2048
D = 1024
NT = 16
KC = 8
DFF = 2816
NFC = 22
FG = 4
NSEQ = 2
EPS = 1e-6
SCALE = 0.125
NEG = -30000.0
POOL_THROTTLE = False


class Tile:
    __slots__ = ("name", "excl", "w", "r", "dr")

    def __init__(self, name, excl=False):
        self.name = name
        self.excl = excl
        self.w = None
        self.r = {}
        self.dr = []


class Op:
    __slots__ = ("fn", "waits", "signal", "sigval", "dma")

    def __init__(self, fn):
        self.fn = fn
        self.waits = []
        self.signal = False
        self.sigval = 0
        self.dma = None


class Prog:
    NDS = 24

    def __init__(self, nc, es):
        self.nc = nc
        self.ops = [[] for _ in range(5)]
        self.waited = [dict() for _ in range(5)]
        names = ["pe", "act", "dve", "pool", "sp"]
        self.esem = [es.enter_context(nc.semaphore("e_" + n)) for n in names]
        self.dsem = {}
        self.dcount = {}
        self.dnext = {}
        self.nds = {POOL: 72, SP: 24}
        for q in (POOL, SP):
            self.dsem[q] = [es.enter_context(nc.semaphore(f"d{q}_{i}")) for i in range(self.nds[q])]
            self.dcount[q] = [0] * self.nds[q]
            self.dnext[q] = 0
        self.last_dma_events = []

    def _deps(self, eng, reads, writes, is_dma):
        deps = []
        for t in reads:
            if t.w is not None:
                deps.append(t.w)
            if t.excl:
                for e2, s in t.r.items():
                    if e2 != eng:
                        deps.append(("e", e2, s))
        for t in writes:
            if t.w is not None:
                if is_dma or not (t.w[0] == "e" and t.w[1] == eng):
                    deps.append(t.w)
            for e2, s in t.r.items():
                if is_dma or e2 != eng:
                    deps.append(("e", e2, s))
            deps.extend(t.dr)
        return deps

    def _add_waits(self, eng, op, deps):
        wd = self.waited[eng]
        for d in deps:
            if d[0] == "e":
                key = ("e", d[1])
                if wd.get(key, -1) >= d[2]:
                    continue
                wd[key] = d[2]
                self.ops[d[1]][d[2]].signal = True
                op.waits.append(d)
            else:
                key = ("d", d[1], d[2])
                if wd.get(key, 0) >= d[3]:
                    continue
                wd[key] = d[3]
                op.waits.append(d)

    def emit(self, eng, fn, reads=(), writes=()):
        op = Op(fn)
        self._add_waits(eng, op, self._deps(eng, reads, writes, False))
        seq = len(self.ops[eng])
        self.ops[eng].append(op)
        for t in reads:
            t.r[eng] = seq
        for t in writes:
            t.w = ("e", eng, seq)
            t.r = {}
            t.dr = []
        return op

    def dma(self, q, out, in_, reads=(), writes=(), **kw):
        op = Op(lambda e: e.dma_start(out=out, in_=in_, **kw))
        self._add_waits(q, op, self._deps(q, reads, writes, True))
        if q == POOL:
            self.pool_evs = getattr(self, "pool_evs", [])
            if POOL_THROTTLE and len(self.pool_evs) >= 6:
                self._add_waits(q, op, [self.pool_evs[-6]])
        i = self.dnext[q]
        self.dnext[q] = (i + 1) % self.nds[q]
        if self.dcount[q][i] > 0:
            self._add_waits(q, op, [("d", q, i, 16 * self.dcount[q][i])])
        self.dcount[q][i] += 1
        ev = ("d", q, i, 16 * self.dcount[q][i])
        op.dma = (q, i)
        if q == POOL:
            self.pool_evs.append(ev)
        self.ops[q].append(op)
        for t in writes:
            t.w = ev
            t.r = {}
            t.dr = []
        for t in reads:
            t.dr.append(ev)
        return ev

    def wait_events(self, eng, evs):
        op = Op(None)
        self._add_waits(eng, op, list(evs))
        self.ops[eng].append(op)

    def barrier(self):
        lasts = []
        for e in range(5):
            if self.ops[e]:
                s = len(self.ops[e]) - 1
                while s >= 0 and (self.ops[e][s].fn is None or self.ops[e][s].dma is not None):
                    s -= 1
                if s >= 0:
                    lasts.append(("e", e, s))
        for c in range(5):
            self.wait_events(c, [d for d in lasts if d[1] != c])

    def lower(self, block):
        for e in range(5):
            c = 0
            for op in self.ops[e]:
                if op.signal:
                    c += 1
                    op.sigval = c

        def mk(e):
            def body(eng):
                for op in self.ops[e]:
                    for w in op.waits:
                        if w[0] == "e":
                            eng.wait_ge(self.esem[w[1]], self.ops[w[1]][w[2]].sigval)
                        else:
                            eng.wait_ge(self.dsem[w[1]][w[2]], w[3])
                    if op.fn is None:
                        continue
                    inst = op.fn(eng)
                    if op.dma is not None:
                        inst.then_inc(self.dsem[op.dma[0]][op.dma[1]], 16)
                    elif op.signal:
                        inst.then_inc(self.esem[e], 1)
            return body

        block.tensor(mk(PE))
        block.scalar(mk(ACT))
        block.vector(mk(DVE))
        block.gpsimd(mk(POOL))
        block.sync(mk(SP))


def interleave(gens, totals=None):
    gens = list(gens)
    n = len(gens)
    if totals is None:
        totals = [1] * n
    done = [0] * n
    alive = [True] * n
    while any(alive):
        best = None
        for k in range(n):
            if alive[k]:
                frac = (done[k] + 1) / float(totals[k])
                if best is None or frac < best[0]:
                    best = (frac, k)
        k = best[1]
        try:
            next(gens[k])
            done[k] += 1
        except StopIteration:
            alive[k] = False


def run(gen):
    for _ in gen:
        pass


class Ring:
    def __init__(self, items):
        self.items = list(items)
        self.i = 0
        self.open = set()

    def next(self):
        for _ in range(len(self.items)):
            it = self.items[self.i]
            self.i = (self.i + 1) % len(self.items)
            if it not in self.open:
                self.open.add(it)
                return it
        raise AssertionError("all psum banks of the ring are open")

    def release(self, it):
        self.open.discard(it)


C_AQG, C_AKG, C_MQG0, C_MKG0, C_MQG1, C_MKG1 = 0, 1, 2, 3, 4, 5
C_BIN = 6
C_CONVB = 18
C_LNG = 24
C_LNB = 30
C_BFAR = 36
C_EPS = 48
C_CONVW = 50
C_GA, C_GM0, C_GM1 = 240, 241, 242
C_HBG = 243
NCST = 256
SKIP_FFN = False
DBG_TILE = None

B_N1 = (0, 2)
B_N2 = (1, 3)
B_MEM = (4, 5)

WA_GROUPS_A = [
    (0, 512, [(0, 512)]),
    (512, 512, [(512, 256), (2304, 256)]),
    (1024, 512, [(768, 512)]),
    (1536, 256, [(1280, 256)]),
    (1792, 512, [(1536, 512)]),
    (2304, 256, [(2048, 256)]),
]


def build_program(layers=(0, 1), nseq=NSEQ):
    nc = bass.Bass("TRN2", target_bir_lowering=False)
    dt = nc.dram_tensor
    x_d = dt("x", [nseq, S, D], F32, kind="ExternalInput").ap()
    mem_d = dt("mem", [nseq, 256, D], F32, kind="ExternalInput").ap()
    a_w_in = dt("a_w_in", [1, D, 2560], F32, kind="ExternalInput").ap()
    b_w_in = dt("b_w_in", [1, D, 1792], F32, kind="ExternalInput").ap()
    w_mem_kv = dt("w_mem_kv", [2, D, 512], F32, kind="ExternalInput").ap()
    w_out = dt("w_out", [2, D, D], F32, kind="ExternalInput").ap()
    w_gate = dt("w_gate", [2, D, DFF], F32, kind="ExternalInput").ap()
    w_up = dt("w_up", [2, D, DFF], F32, kind="ExternalInput").ap()
    w_down = dt("w_down", [2, DFF, D], F32, kind="ExternalInput").ap()
    cst_d = dt("cst", [128, NCST], F32, kind="ExternalInput").ap()
    bct_d = dt("bct", [6, 128, D], F32, kind="ExternalInput").ap()
    qmb_d = dt("qmb", [128, 256], F32, kind="ExternalInput").ap()
    bias_d = dt("biasT", [128, 6 * 4 * 128], F32, kind="ExternalInput").ap()
    diag_d = dt("convdiag", [6, 128, 31 * 128], F32, kind="ExternalInput").ap()
    out_d = dt("out", [nseq, S, D], F32, kind="ExternalOutput").ap()
    dbg_d = dt("dbg", [128, 1024], BF16, kind="ExternalOutput").ap() if DBG_TILE is not None else None

    es = ExitStack()
    with es:
        ARENA = 212800
        arena = es.enter_context(nc.sbuf_tensor("arena", [128, ARENA], U8))
        ps = es.enter_context(nc.psum_tensor("ps", [128, 8, 512], F32))
        P = Prog(nc, es)
        block = es.enter_context(nc.Block())

        def view(off, nbytes, dtype, pat=None, **kw):
            ap = arena[:, off:off + nbytes].bitcast(dtype)
            if pat is not None:
                ap = ap.rearrange(pat, **kw)
            return ap

        def mm(out, lhsT, rhs, start, stop, reads, writes):
            P.emit(PE, lambda e: e.matmul(out, lhsT=lhsT, rhs=rhs, start=start, stop=stop), reads, writes)

        def tr(out, in_, ident, reads, writes):
            P.emit(PE, lambda e: e.transpose(out=out, in_=in_, identity=ident), reads, writes)

        def act(out, in_, func, reads, writes, **kw):
            P.emit(ACT, lambda e: e.activation(out=out, in_=in_, func=func, **kw), reads, writes)

        def tt(eng, out, in0, in1, op, reads, writes):
            P.emit(eng, lambda e: e.tensor_tensor(out=out, in0=in0, in1=in1, op=op), reads, writes)

        def stt(out, in0, scalar, in1, op0, op1, reads, writes):
            P.emit(DVE, lambda e: e.scalar_tensor_tensor(out=out, in0=in0, scalar=scalar, in1=in1, op0=op0, op1=op1),
                   reads, writes)

        def ts(eng, out, in0, s1, s2, op0, op1, reads, writes):
            if op1 is None:
                P.emit(eng, lambda e: e.tensor_scalar(out=out, in0=in0, scalar1=s1, scalar2=None, op0=op0), reads, writes)
            else:
                P.emit(eng, lambda e: e.tensor_scalar(out=out, in0=in0, scalar1=s1, scalar2=s2, op0=op0, op1=op1),
                       reads, writes)

        def cp(eng, out, in_, reads, writes):
            P.emit(eng, lambda e: e.tensor_copy(out=out, in_=in_), reads, writes)

        def mset(eng, ap, val, writes):
            P.emit(eng, lambda e: e.memset(ap, val), (), writes)

        X0 = 0
        WA0 = 65536
        R10 = WA0 + 40960
        R20 = R10 + 49152
        G0 = R20 + 32768
        assert G0 + 24384 <= ARENA
        x_sb = view(X0, 65536, F32, "p (t d) -> p t d", t=NT)
        x_t = [Tile(f"x{i}") for i in range(NT)]

        g = G0
        ident_f = view(g, 512, F32); g += 512
        ident_b = view(g, 256, BF16); g += 256
        cst = view(g, 1024, F32); g += 1024
        gbc = view(g, 4096, F32); g += 4096
        xn = view(g, 4096, F32); g += 4096
        kmT = [view(g + 1024 * l, 1024, BF16, "p (c m) -> p c m", c=2) for l in range(2)]; g += 2048
        vm = [view(g + 1040 * l, 1040, BF16, "p (t h e) -> p t h e", t=2, h=4) for l in range(2)]; g += 2080
        stats = view(g, 1024, F32); g += 1024
        ffn_silu = [view(g + 1024 * i, 1024, F32) for i in range(2)]; g += 2048
        ffn_act = [view(g + 512 * i, 512, BF16) for i in range(4)]; g += 2048
        qmb = view(g, 1024, F32); g += 1024
        ptm = view(g, 2048, BF16); g += 2048
        assert g <= G0 + 24384, g - G0
        t_ident, t_cst, t_gbc, t_xn_default = Tile("ident"), Tile("cst"), Tile("gbc"), Tile("xn")
        t_km = [Tile(f"km{l}") for l in range(2)]
        t_vm = [Tile(f"vm{l}") for l in range(2)]
        t_silu = [Tile(f"silu{i}") for i in range(2)]
        t_act = [Tile(f"act{i}") for i in range(4)]
        t_qmb, t_ptm = Tile("qmb"), Tile("ptm")
        ST_SS, ST_LN, ST_RS = 0, 16, 32
        ST_H, ST_HL, ST_HR = 48, 80, 112
        ST_REC = 144
        ST_BN = 176
        t_srms, t_shead, t_srec, t_srecm, t_sbn = Tile("srms"), Tile("shead"), Tile("srec"), Tile("srecm"), Tile("sbn")

        pb = [Tile(f"ps{b}", excl=True) for b in range(8)]

        def bank(b):
            return ps[:, b, :]

        def col(c, n=1):
            return cst[:, c:c + n]

        P.dma(SP, cst, cst_d, writes=[t_cst])
        P.dma(SP, qmb, qmb_d, writes=[t_qmb])
        mset(POOL, ident_f, 0.0, [t_ident])
        P.emit(POOL, lambda e: e.affine_select(out=ident_f, in_=ident_f, pattern=[[-1, 128]],
                                               compare_op=ALU.not_equal, fill=1.0, base=0,
                                               channel_multiplier=1),
               reads=[t_ident], writes=[t_ident])
        cp(POOL, ident_b, ident_f, [t_ident], [t_ident])

        tt(DVE, col(C_GA), col(C_AQG), col(C_AKG), ALU.mult, [t_cst], [t_cst])
        tt(DVE, col(C_GM0), col(C_MQG0), col(C_MKG0), ALU.mult, [t_cst], [t_cst])
        tt(DVE, col(C_GM1), col(C_MQG1), col(C_MKG1), ALU.mult, [t_cst], [t_cst])

        xnb_default = xn.bitcast(BF16)[:, 0:D]

        xnb2 = xn.bitcast(BF16)[:, D:2 * D]
        t_xn2 = Tile("xn2")

        def norm_transpose_g(src_ap, src_tiles, sscol, dst, dst_tiles, pool, evac=DVE, xbuf=None, gain=None):
            ss = stats[:, ST_SS + sscol:ST_SS + sscol + 1]
            ln = stats[:, ST_LN + sscol:ST_LN + sscol + 1]
            rs = stats[:, ST_RS + sscol:ST_RS + sscol + 1]
            xnb, t_xn = xbuf if xbuf is not None else (xnb_default, t_xn_default)
            act(xnb, src_ap, AF.Square, src_tiles, [t_xn, t_srms], accum_out=ss)
            yield
            act(ln, ss, AF.Ln, [t_srms, t_cst], [t_srms], scale=1.0 / D, bias=col(C_EPS))
            act(rs, ln, AF.Exp, [t_srms], [t_srms], scale=-0.5)
            yield
            g_ap, g_t = gain if gain is not None else (gbc, t_gbc)
            stt(xnb, src_ap, rs, g_ap, ALU.mult, ALU.mult, src_tiles + [t_srms, g_t], [t_xn])
            yield
            b = pool.next()
            pbf = bank(b).bitcast(BF16)
            for kc in range(KC):
                tr(pbf[:, kc * 128:(kc + 1) * 128], xnb[:, kc * 128:(kc + 1) * 128], ident_b, [t_xn, t_ident], [pb[b]])
            src3 = pbf.rearrange("p (k t) -> p k t", k=KC)
            if evac == ACT:
                act(dst, src3, AF.Copy, [pb[b]], dst_tiles)
            else:
                cp(DVE, dst, src3, [pb[b]], dst_tiles)
            pool.release(b)
            yield

        def norm_transpose(src_ap, src_tiles, sscol, dsts, dst_tiles, pool):
            raise NotImplementedError

        def head_norm_transpose(nheads, qf, t_qf, qsq, t_qsq, dst_fn, pool):
            w = nheads * 64
            q3 = qf[:, 0:w].rearrange("p (h d) -> p h d", d=64)
            act(qsq[:, 0:w], qf[:, 0:w], AF.Square, [t_qf], [t_qsq])
            hs = stats[:, ST_H:ST_H + nheads]
            hl = stats[:, ST_HL:ST_HL + nheads]
            hr = stats[:, ST_HR:ST_HR + nheads]
            P.emit(DVE, lambda e: e.tensor_reduce(out=hs, in_=qsq[:, 0:w].rearrange("p (h d) -> p h d", d=64),
                                                  axis=AX.X, op=ALU.add), [t_qsq], [t_shead])
            act(hl, hs, AF.Ln, [t_shead, t_cst], [t_shead], scale=1.0 / 64, bias=col(C_EPS))
            act(hr, hl, AF.Exp, [t_shead], [t_shead], scale=-0.5)
            tt(DVE, q3, q3, hr.unsqueeze(2).broadcast_to([128, nheads, 64]), ALU.mult, [t_qf, t_shead], [t_qf])
            npairs = nheads // 2
            p0 = 0
            while p0 < npairs:
                np_ = min(4, npairs - p0)
                b = pool.next()
                for k in range(np_):
                    pp = p0 + k
                    tr(bank(b)[:, k * 128:(k + 1) * 128], qf[:, pp * 128:(pp + 1) * 128], ident_f,
                       [t_qf, t_ident], [pb[b]])
                dst_fn(b, p0, np_)
                pool.release(b)
                p0 += np_

        mk_memt = view(R10, 8192, F32, "p (t d) -> p t d", t=2)
        mk_gb = [view(R10 + 8192 + 4096 * k, 4096, F32) for k in range(2)]
        mk_wkv = [view(R10 + 16384, 8192, BF16, "p (c n) -> p c n", c=8),
                  view(R10 + 24576, 8192, BF16, "p (c n) -> p c n", c=8)]

        def mem_kv_loads(s, extra_writes=(), only_slot0=False):
            t_memt = Tile("memt")
            t_gb = [Tile("mgb0"), Tile("mgb1")]
            t_wkv = [Tile("wkv0"), Tile("wkv1")]
            ew = list(extra_writes)
            P.dma(SP, mk_memt, mem_d[s].rearrange("(t p) d -> p t d", p=128), writes=[t_memt] + ew)
            for k, l in enumerate(layers):
                P.dma(SP, mk_gb[k], bct_d[B_MEM[l]], writes=[t_gb[k]] + ew)
            for k, l in enumerate(layers):
                if only_slot0 and k > 0:
                    continue
                P.dma(POOL, mk_wkv[k], w_mem_kv[l].rearrange("(c p) n -> p c n", p=128), writes=[t_wkv[k]] + ew,
                      max_dma_last_dim=4096)
            return dict(memt=t_memt, gb=t_gb, wkv=t_wkv, have_wkv1=not (only_slot0 and len(layers) > 1))

        def mem_kv(s, pre=None):
            if pre is None:
                pre = mem_kv_loads(s)
            elif not pre["have_wkv1"]:
                P.dma(POOL, mk_wkv[1], w_mem_kv[layers[1]].rearrange("(c p) n -> p c n", p=128), writes=[pre["wkv"][1]],
                      max_dma_last_dim=4096)
            memt, t_memt = mk_memt, pre["memt"]
            pool = Ring([0, 1, 2, 3, 4, 5])

            def one_layer(l, k):
                o = R10 + 32768 + k * 6144
                memT = view(o, 4096, BF16, "p (c m) -> p c m", c=8)
                qf = view(o + 4096, 1024, F32)
                qsq = view(o + 5120, 1024, F32)
                wkv, gb = mk_wkv[k], mk_gb[k]
                t_wkv, t_gb = pre["wkv"][k], pre["gb"][k]
                t_memT, t_qf, t_qsq = Tile("memT"), Tile("mqf"), Tile("mqsq")
                xb = (xnb_default, t_xn_default) if k == 0 else (xnb2, t_xn2)
                for mt in range(2):
                    yield from norm_transpose_g(memt[:, mt, :], [t_memt], 2 * k + mt, memT[:, :, mt * 128:(mt + 1) * 128],
                                                [t_memT], pool, xbuf=xb, gain=(gb, t_gb))
                gc = C_GM0 if l == 0 else C_GM1
                hs = stats[:, ST_H + 4 * k:ST_H + 4 * k + 4]
                hl = stats[:, ST_HL + 4 * k:ST_HL + 4 * k + 4]
                hr = stats[:, ST_HR + 4 * k:ST_HR + 4 * k + 4]
                t_sh = Tile("msh")
                for mt in range(2):
                    b = pool.next()
                    for kc in range(KC):
                        mm(bank(b), memT[:, kc, mt * 128:(mt + 1) * 128], wkv[:, kc, :], kc == 0, kc == KC - 1,
                           [t_memT, t_wkv], [pb[b]])
                    cp(DVE, vm[l][:, mt, :, 0:64], bank(b)[:, 256:512].rearrange("p (h d) -> p h d", d=64),
                       [pb[b]], [t_vm[l]])
                    mset(POOL, vm[l][:, mt, :, 64:65], 1.0, [t_vm[l]])
                    cp(DVE, qf[:, 0:256], bank(b)[:, 0:256], [pb[b]], [t_qf])
                    pool.release(b)
                    yield
                    act(qsq, qf, AF.Square, [t_qf], [t_qsq])
                    yield
                    P.emit(DVE, lambda e, hs=hs, qsq=qsq: e.tensor_reduce(
                        out=hs, in_=qsq.rearrange("p (h d) -> p h d", d=64), axis=AX.X, op=ALU.add), [t_qsq], [t_sh])
                    yield
                    act(hl, hs, AF.Ln, [t_sh, t_cst], [t_sh], scale=1.0 / 64, bias=col(C_EPS))
                    act(hr, hl, AF.Exp, [t_sh], [t_sh], scale=-0.5)
                    yield
                    q3 = qf.rearrange("p (h d) -> p h d", d=64)
                    tt(DVE, q3, q3, hr.unsqueeze(2).broadcast_to([128, 4, 64]), ALU.mult, [t_qf, t_sh], [t_qf])
                    yield
                    b2 = pool.next()
                    for pp in range(2):
                        tr(bank(b2)[:, pp * 128:(pp + 1) * 128], qf[:, pp * 128:(pp + 1) * 128], ident_f,
                           [t_qf, t_ident], [pb[b2]])
                    act(kmT[l][:, :, mt * 128:(mt + 1) * 128], bank(b2)[:, 0:256].rearrange("p (k t) -> p k t", k=2),
                        AF.Copy, [pb[b2], t_cst], [t_km[l]], scale=col(gc))
                    pool.release(b2)
                    yield

            interleave([one_layer(l, k) for k, l in enumerate(layers)])

        wa_A = view(WA0, 40960, BF16, "p (c n) -> p c n", c=8)
        wa_B = view(WA0, 28672, BF16, "p (c n) -> p c n", c=8)
        wo = view(R10, 16384, BF16, "p (c n) -> p c n", c=8)
        t_wa = [[Tile(f"wa{i}_{j}") for j in range(2)] for i in range(6)]
        t_wo = [Tile(f"wo{i}") for i in range(2)]

        def load_wa(l):
            if l == 0:
                for gi, (c0, n, pieces) in enumerate(WA_GROUPS_A):
                    o = c0
                    for pi, (d0, dn) in enumerate(pieces):
                        P.dma(POOL, wa_A[:, :, o:o + dn], a_w_in[0][:, d0:d0 + dn].rearrange("(c p) n -> p c n", p=128),
                              writes=[t_wa[gi][pi]], max_dma_last_dim=4096)
                        o += dn
            else:
                bounds = [0, 512, 1024, 1536, 1792]
                for gi in range(4):
                    a, b_ = bounds[gi], bounds[gi + 1]
                    P.dma(POOL, wa_B[:, :, a:b_], b_w_in[0][:, a:b_].rearrange("(c p) n -> p c n", p=128),
                          writes=[t_wa[gi][0]], max_dma_last_dim=4096)

        def load_wo(l):
            for hf in range(2):
                P.dma(POOL, wo[:, :, hf * 512:(hf + 1) * 512],
                      w_out[l][:, hf * 512:(hf + 1) * 512].rearrange("(c p) n -> p c n", p=128),
                      writes=[t_wo[hf]], max_dma_last_dim=4096)

        mem_pre = {}

        def ffn_phase(s, l, is_last, next_wa):
            P.barrier()
            h2T = view(R20, 32768, BF16, "p (c t) -> p c t", c=8)
            t_h2 = [Tile(f"h2T{i}") for i in range(NT)]
            fs = []
            for i in range(2):
                o = R10 + i * 24576
                fs.append(dict(
                    gate=view(o, 8192, BF16, "p (c n) -> p c n", c=8),
                    up=view(o + 8192, 8192, BF16, "p (c n) -> p c n", c=8),
                    down=view(o + 16384, 8192, BF16, "p (j n) -> p j n", j=4),
                    tg=[Tile(f"fg{i}_{j}") for j in range(FG)], tu=[Tile(f"fu{i}_{j}") for j in range(FG)],
                    td=[Tile(f"fd{i}_{j}") for j in range(FG)]))
            groups = []
            c = 0
            while c < NFC:
                n = min(FG, NFC - c)
                groups.append((c, n))
                c += n

            def load_group(gi):
                c0, n = groups[gi]
                f = fs[gi % 2]
                if False:
                    for j in range(n):
                        cc = c0 + j
                        P.dma(POOL, f["gate"][:, :, j * 128:(j + 1) * 128],
                              w_gate[l][:, cc * 128:(cc + 1) * 128].rearrange("(c p) n -> p c n", p=128),
                              writes=[f["tg"][j]], max_dma_last_dim=4096)
                        P.dma(POOL, f["up"][:, :, j * 128:(j + 1) * 128],
                              w_up[l][:, cc * 128:(cc + 1) * 128].rearrange("(c p) n -> p c n", p=128),
                              writes=[f["tu"][j]], max_dma_last_dim=4096)
                        P.dma(POOL, f["down"][:, j, :], w_down[l][cc * 128:(cc + 1) * 128, :],
                              writes=[f["td"][j]], max_dma_last_dim=4096)
                    return
                P.dma(POOL, f["gate"][:, :, 0:n * 128],
                      w_gate[l][:, c0 * 128:(c0 + n) * 128].rearrange("(c p) n -> p c n", p=128),
                      writes=f["tg"][0:n], max_dma_last_dim=4096)
                P.dma(POOL, f["up"][:, :, 0:n * 128],
                      w_up[l][:, c0 * 128:(c0 + n) * 128].rearrange("(c p) n -> p c n", p=128),
                      writes=f["tu"][0:n], max_dma_last_dim=4096)
                P.dma(POOL, f["down"][:, 0:n, :],
                      w_down[l][c0 * 128:(c0 + n) * 128, :].rearrange("(j p) n -> p j n", p=128),
                      writes=f["td"][0:n], max_dma_last_dim=4096)

            load_group(0)
            load_group(1)
            P.dma(SP, gbc, bct_d[B_N2[l]], writes=[t_gbc])
            gu = Ring([0, 1, 2, 3])

            def h2_gens(tb):
                gens = []
                for k, i in enumerate((2 * tb, 2 * tb + 1)):
                    xb = (xnb_default, t_xn_default) if k == 0 else (xnb2, t_xn2)
                    g_ = norm_transpose_g(x_sb[:, i, :], [x_t[i]], i, h2T[:, :, i * 128:(i + 1) * 128],
                                          [t_h2[i]], gu, evac=(DVE if i % 2 else ACT), xbuf=xb)
                    for _ in range(3):
                        next(g_)
                    gens.append(g_)
                return gens

            for g_ in h2_gens(0):
                run(g_)
            pending_h2 = h2_gens(1)
            silr = Ring([0, 1])
            actr = Ring([0, 1, 2, 3])

            def do_gu(f, tb, j):
                b = gu.next()
                rhs_t = [t_h2[2 * tb], t_h2[2 * tb + 1]]
                for kc in range(KC):
                    mm(bank(b)[:, 0:256], f["gate"][:, kc, j * 128:(j + 1) * 128], h2T[:, kc, tb * 256:(tb + 1) * 256],
                       kc == 0, kc == KC - 1, [f["tg"][j]] + rhs_t, [pb[b]])
                for kc in range(KC):
                    mm(bank(b)[:, 256:512], f["up"][:, kc, j * 128:(j + 1) * 128], h2T[:, kc, tb * 256:(tb + 1) * 256],
                       kc == 0, kc == KC - 1, [f["tu"][j]] + rhs_t, [pb[b]])
                si = silr.next(); silr.release(si)
                ai = actr.next(); actr.release(ai)
                act(ffn_silu[si], bank(b)[:, 0:256], AF.Silu, [pb[b]], [t_silu[si]])
                tt(DVE, ffn_act[ai], bank(b)[:, 256:512], ffn_silu[si], ALU.mult, [pb[b], t_silu[si]], [t_act[ai]])
                gu.release(b)
                return ai

            def do_down(f, n, last_group, tb, j, ai):
                for t2 in range(2):
                    for hf in range(2):
                        b = 4 + t2 * 2 + hf
                        mm(bank(b), ffn_act[ai][:, t2 * 128:(t2 + 1) * 128], f["down"][:, j, hf * 512:(hf + 1) * 512],
                           j == 0, j == n - 1, [t_act[ai], f["td"][j]], [pb[b]])
                if j == n - 1:
                    for t2 in range(2):
                        ti = 2 * tb + t2
                        xv = x_sb[:, ti, :].rearrange("p (h n) -> p h n", h=2)
                        tt(DVE, xv, ps[:, 4 + 2 * t2:6 + 2 * t2, :], xv, ALU.add,
                           [pb[4 + 2 * t2], pb[5 + 2 * t2], x_t[ti]], [x_t[ti]])
                        if is_last and last_group:
                            ev = P.dma(SP, out_d[s, ti * 128:(ti + 1) * 128, :], x_sb[:, ti, :], reads=[x_t[ti]])
                            P.last_dma_events.append(ev)
                            if s + 1 < nseq:
                                P.dma(SP, x_sb[:, ti, :], x_d[s + 1, ti * 128:(ti + 1) * 128, :], writes=[x_t[ti]])

            for gi, (c0, n) in enumerate(groups):
                f = fs[gi % 2]
                pend = []
                if is_last and s + 1 < nseq and gi == len(groups) - 1 and gi % 2 == 1:
                    f0 = fs[0]
                    mem_pre[s + 1] = mem_kv_loads(s + 1, extra_writes=f0["tg"] + f0["tu"] + f0["td"], only_slot0=True)
                for tb in range(8):
                    for j in range(n):
                        ai = do_gu(f, tb, j)
                        pend.append((tb, j, ai))
                        if len(pend) > 2:
                            do_down(f, n, gi == len(groups) - 1, *pend.pop(0))
                    if gi == 0 and tb + 1 < 8:
                        for g_ in pending_h2:
                            run(g_)
                        pending_h2 = h2_gens(tb + 2) if tb + 2 < 8 else []
                while pend:
                    do_down(f, n, gi == len(groups) - 1, *pend.pop(0))
                if gi == 0 and next_wa is not None:
                    load_wa(next_wa)
                if gi + 2 < len(groups):
                    load_group(gi + 2)

        def mem_attn(l, qmTz, t_qmTz, dst, t_cat, b0, b1, pvb):
            for h in range(4):
                for mt in range(2):
                    b = b0 if h < 2 else b1
                    c0 = ((h % 2) * 2 + mt) * 128
                    mm(bank(b)[:, c0:c0 + 128], kmT[l][:, h // 2, mt * 128:(mt + 1) * 128], qmTz[:, h, :], True, True,
                       [t_km[l], t_qmTz], [pb[b]])
            for hb, b in enumerate((b0, b1)):
                act(ptm[:, hb * 512:(hb + 1) * 512], bank(b), AF.Exp, [pb[b]], [t_ptm], scale=SCALE)
            for h in range(4):
                for mt in range(2):
                    c0 = (h * 2 + mt) * 128
                    mm(bank(pvb)[:, h * 65:(h + 1) * 65], ptm[:, c0:c0 + 128], vm[l][:, mt, h, :], mt == 0, mt == 1,
                       [t_ptm, t_vm[l]], [pb[pvb]])
            rec = stats[:, ST_REC + 12:ST_REC + 16]
            pv3 = bank(pvb)[:, 0:260].rearrange("p (h e) -> p h e", e=65)
            P.emit(DVE, lambda e: e.reciprocal(out=rec.unsqueeze(2), in_=pv3[:, :, 64:65]), [pb[pvb]], [t_srecm])
            tt(DVE, dst.rearrange("p (h d) -> p h d", d=64), pv3[:, :, 0:64],
               rec.unsqueeze(2).broadcast_to([128, 4, 64]), ALU.mult, [pb[pvb], t_srecm], [t_cat])

        def out_proj(ti, catT, t_catT, pool):
            for hf in range(2):
                b = pool.next()
                for kc in range(KC):
                    mm(bank(b), catT[:, kc, :], wo[:, kc, hf * 512:(hf + 1) * 512], kc == 0, kc == KC - 1,
                       [t_catT, t_wo[hf]], [pb[b]])
                xv = x_sb[:, ti, hf * 512:(hf + 1) * 512]
                tt(DVE, xv, bank(b), xv, ALU.add, [pb[b], x_t[ti]], [x_t[ti]])
                pool.release(b)

        def mixer_a(s, l, pre_bias=None):
            P.barrier()
            o = R10 + 16384
            hT = [view(o + 2048 * i, 2048, BF16, "p (c t) -> p c t", c=8) for i in range(2)]; o += 4096
            qf = view(o, 3072, F32); o += 3072
            kf = view(o, 3072, F32); o += 3072
            qsq = view(o, 3072, F32); o += 3072
            qmf = view(o, 1024, F32); o += 1024
            qTz = [view(o + 3072 * i, 3072, BF16, "p (h t) -> p h t", h=12) for i in range(2)]; o += 6144
            qmTz = [view(o + 1024 * i, 1024, BF16, "p (h t) -> p h t", h=4) for i in range(2)]; o += 2048
            PT = [view(o + 2560 * i, 2560, BF16) for i in range(2)]; o += 5120
            cat = view(o, 2048, BF16); o += 2048
            catT = view(o, 2048, BF16, "p (c t) -> p c t", c=8); o += 2048
            assert o <= R10 + 49152, o - R10
            qb = qsq.bitcast(BF16)
            kring = view(R20, 9216, BF16, "p (s c t) -> p s c t", s=6, c=6)
            vring = view(R20 + 9216, 9360, BF16, "p (s h e) -> p s h e", s=6, h=12)
            biasT = view(R20 + 18576, 12288, F32, "p (c j q) -> p c j q", c=6, j=4)
            t_hT = [Tile(f"hT{i}") for i in range(2)]
            t_qf, t_kf, t_qsq, t_qmf = Tile("qf"), Tile("kf"), Tile("qsq"), Tile("qmf")
            t_qTz = [Tile(f"qTz{i}") for i in range(2)]
            t_qmTz = [Tile(f"qmTz{i}") for i in range(2)]
            t_PT = [Tile(f"PT{i}") for i in range(2)]
            t_cat, t_catT = Tile("cat"), Tile("catT")
            t_kr = [Tile(f"kr{i}") for i in range(6)]
            t_vr = [Tile(f"vr{i}") for i in range(6)]
            t_bias = pre_bias if pre_bias is not None else Tile("bias")

            if pre_bias is None:
                P.dma(POOL, gbc, bct_d[B_N1[l]], writes=[t_gbc], max_dma_last_dim=4096)
                P.dma(POOL, biasT.rearrange("p c j q -> p (c j q)"), bias_d, writes=[t_bias], max_dma_last_dim=4096)
            load_wo(l)
            for i in range(2):
                mset(POOL, qTz[i].rearrange("p h t -> p (h t)"), 0.0, [t_qTz[i]])
                mset(POOL, qmTz[i].rearrange("p h t -> p (h t)"), 0.0, [t_qmTz[i]])
            for sl in range(6):
                mset(POOL, vring[:, sl, :, 64:65], 1.0, [t_vr[sl]])
            def bias_init():
                for h in range(12):
                    bv = biasT[:, h // 2, :, :].rearrange("p (d a) q -> p d a q", a=2)[:, :, h % 2, :]
                    ts(DVE, bv, bv, col(C_BFAR + h), None, ALU.subtract, None, [t_bias, t_cst], [t_bias])
                mset(DVE, biasT[64:128, :, 0:2, 0:64], NEG, [t_bias])

            gpool = Ring([0, 1, 2])
            BNS, BFS, B4, PVB = (3, 4), (5, 5), 6, 7

            def stage1(i):
                yield from norm_transpose_g(x_sb[:, i, :], [x_t[i]], i % 16, hT[i % 2], [t_hT[i % 2]], gpool, evac=DVE)

            def stage2(i):
                hb = i % 2
                sl = i % 6
                qT4 = qTz[hb].rearrange("p (a b) t -> p a b t", b=2)
                qmT4 = qmTz[hb].rearrange("p (a b) t -> p a b t", b=2)

                def inproj(gi):
                    c0, n, pieces = WA_GROUPS_A[gi]
                    b = gpool.next()
                    for kc in range(KC):
                        mm(bank(b)[:, 0:n], hT[hb][:, kc, :], wa_A[:, kc, c0:c0 + n], kc == 0, kc == KC - 1,
                           [t_hT[hb]] + t_wa[gi][0:len(pieces)], [pb[b]])
                    return b

                hs = stats[:, ST_H:ST_H + 28]
                hl = stats[:, ST_HL:ST_HL + 28]
                hr = stats[:, ST_HR:ST_HR + 28]

                def sq_red(src, t_src, nheads, c0):
                    w = nheads * 64
                    act(qsq[:, 0:w], src[:, 0:w], AF.Square, [t_src], [t_qsq])
                    P.emit(DVE, lambda e: e.tensor_reduce(
                        out=stats[:, ST_H + c0:ST_H + c0 + nheads],
                        in_=qsq[:, 0:w].rearrange("p (h d) -> p h d", d=64), axis=AX.X, op=ALU.add),
                        [t_qsq], [t_shead])

                b = inproj(0)
                act(qf[:, 0:512], bank(b), AF.Copy, [pb[b]], [t_qf])
                gpool.release(b)
                yield
                b = inproj(1)
                act(qf[:, 512:768], bank(b)[:, 0:256], AF.Copy, [pb[b]], [t_qf])
                act(qmf, bank(b)[:, 256:512], AF.Copy, [pb[b]], [t_qmf])
                gpool.release(b)
                yield
                sq_red(qf, t_qf, 12, 0)
                yield
                b = inproj(2)
                act(kf[:, 0:512], bank(b), AF.Copy, [pb[b]], [t_kf])
                gpool.release(b)
                yield
                sq_red(qmf, t_qmf, 4, 24)
                yield
                b = inproj(3)
                act(kf[:, 512:768], bank(b)[:, 0:256], AF.Copy, [pb[b]], [t_kf])
                gpool.release(b)
                yield
                sq_red(kf, t_kf, 12, 12)
                yield
                b = inproj(4)
                cp(DVE, vring[:, sl, 0:8, 0:64], bank(b).rearrange("p (h d) -> p h d", d=64), [pb[b]], [t_vr[sl]])
                gpool.release(b)
                yield
                act(hl, hs, AF.Ln, [t_shead, t_cst], [t_shead], scale=1.0 / 64, bias=col(C_EPS))
                act(hr, hl, AF.Exp, [t_shead], [t_shead], scale=-0.5)
                yield
                b = inproj(5)
                cp(DVE, vring[:, sl, 8:12, 0:64], bank(b)[:, 0:256].rearrange("p (h d) -> p h d", d=64),
                   [pb[b]], [t_vr[sl]])
                gpool.release(b)
                yield
                tt(DVE, qb[:, 0:768].rearrange("p (h d) -> p h d", d=64), qf.rearrange("p (h d) -> p h d", d=64),
                   stats[:, ST_HR:ST_HR + 12].unsqueeze(2).broadcast_to([128, 12, 64]), ALU.mult,
                   [t_qf, t_shead], [t_qsq])
                yield
                b = gpool.next()
                pbf = bank(b).bitcast(BF16)
                for pp in range(6):
                    tr(pbf[:, pp * 128:(pp + 1) * 128], qb[:, pp * 128:(pp + 1) * 128], ident_b, [t_qsq, t_ident], [pb[b]])
                for hp in range(2):
                    cp(DVE, qT4[hp * 64:(hp + 1) * 64, :, hp, :],
                       pbf[hp * 64:(hp + 1) * 64, 0:768].rearrange("p (k t) -> p k t", k=6), [pb[b]], [t_qTz[hb]])
                gpool.release(b)
                yield
                tt(DVE, qb[:, 0:256].rearrange("p (h d) -> p h d", d=64), qmf.rearrange("p (h d) -> p h d", d=64),
                   stats[:, ST_HR + 24:ST_HR + 28].unsqueeze(2).broadcast_to([128, 4, 64]), ALU.mult,
                   [t_qmf, t_shead], [t_qsq])
                yield
                b = gpool.next()
                pbf = bank(b).bitcast(BF16)
                for pp in range(2):
                    tr(pbf[:, pp * 128:(pp + 1) * 128], qb[:, pp * 128:(pp + 1) * 128], ident_b, [t_qsq, t_ident], [pb[b]])
                for hp in range(2):
                    cp(DVE, qmT4[hp * 64:(hp + 1) * 64, :, hp, :],
                       pbf[hp * 64:(hp + 1) * 64, 0:256].rearrange("p (k t) -> p k t", k=2), [pb[b]], [t_qmTz[hb]])
                gpool.release(b)
                yield
                k3 = kf.rearrange("p (h d) -> p h d", d=64)
                tt(DVE, k3, k3, stats[:, ST_HR + 12:ST_HR + 24].unsqueeze(2).broadcast_to([128, 12, 64]), ALU.mult,
                   [t_kf, t_shead], [t_kf])
                yield
                for (p0, np_) in ((0, 4), (4, 2)):
                    b = gpool.next()
                    for k in range(np_):
                        pp = p0 + k
                        tr(bank(b)[:, k * 128:(k + 1) * 128], kf[:, pp * 128:(pp + 1) * 128], ident_f,
                           [t_kf, t_ident], [pb[b]])
                    act(kring[:, sl, p0:p0 + np_, :], bank(b)[:, 0:np_ * 128].rearrange("p (k t) -> p k t", k=np_),
                        AF.Copy, [pb[b], t_cst], [t_kr[sl]], scale=col(C_GA))
                    gpool.release(b)
                    yield

            def scores(i, p, hb):
                dmax = min(i, 4)
                pi = p % 2
                BN, BF = BNS[pi], BFS[pi]
                q2 = qTz[hb][:, 2 * p:2 * p + 2, :]
                for dl in range(dmax + 1):
                    sl = (i - dl) % 6
                    if dl < 2:
                        b, c0 = BN, dl * 256
                    elif dl < 4:
                        b, c0 = BF, (dl - 2) * 256
                    else:
                        b, c0 = B4, 0
                    mm(bank(b)[:, c0:c0 + 256], kring[:, sl, p, :], q2, True, True, [t_kr[sl], t_qTz[hb]], [pb[b]])
                nd = min(dmax + 1, 2)
                nv = bank(BN)[:, 0:nd * 256]
                bv = biasT[:, p, 0:nd * 2, :].rearrange("p j q -> p (j q)")
                stt(nv, nv, SCALE, bv, ALU.mult, ALU.add, [pb[BN], t_bias], [pb[BN]])
                if dmax >= 4:
                    v4 = bank(B4)[:, 0:256].rearrange("p (a q) -> p a q", a=2)
                    ts(DVE, v4[0:64, :, 64:128], v4[0:64, :, 64:128], NEG, None, ALU.add, None, [pb[B4]], [pb[B4]])
                act(PT[pi][:, 0:nd * 256], nv, AF.Exp, [pb[BN]], [t_PT[pi]])
                if dmax >= 4:
                    fv = ps[:, 5:7, :].rearrange("p b n -> p (b n)")[:, 0:768]
                    act(PT[pi][:, 512:1280], fv, AF.Exp, [pb[BF], pb[B4]], [t_PT[pi]], scale=SCALE)
                elif dmax >= 2:
                    nf = dmax - 1
                    act(PT[pi][:, 512:512 + nf * 256], bank(BF)[:, 0:nf * 256], AF.Exp, [pb[BF]], [t_PT[pi]], scale=SCALE)

            def pv(i, p, hb):
                dmax = min(i, 4)
                pi = p % 2
                fo, do_ = 512, 1024
                for hp in range(2):
                    h = 2 * p + hp
                    oc = (h % 6) * 65
                    for dl in range(dmax + 1):
                        sl = (i - dl) % 6
                        if dl < 2:
                            c0 = (dl * 2 + hp) * 128
                        elif dl < 4:
                            c0 = fo + ((dl - 2) * 2 + hp) * 128
                        else:
                            c0 = do_ + hp * 128
                        mm(bank(PVB)[:, oc:oc + 65], PT[pi][:, c0:c0 + 128], vring[:, sl, h, :], dl == 0, dl == dmax,
                           [t_PT[pi], t_vr[sl]], [pb[PVB]])
                if p in (2, 5):
                    rec = stats[:, ST_REC + (p // 3) * 6:ST_REC + (p // 3) * 6 + 6]
                    pv3 = bank(PVB)[:, 0:390].rearrange("p (h e) -> p h e", e=65)
                    P.emit(DVE, lambda e: e.reciprocal(out=rec.unsqueeze(2), in_=pv3[:, :, 64:65]), [pb[PVB]], [t_srec])
                    cc = (p // 3) * 384
                    tt(DVE, cat[:, cc:cc + 384].rearrange("p (h d) -> p h d", d=64), pv3[:, :, 0:64],
                       rec.unsqueeze(2).broadcast_to([128, 6, 64]), ALU.mult, [pb[PVB], t_srec], [t_cat])

            def stage3(i):
                hb = i % 2
                for p in range(7):
                    if p < 6:
                        scores(i, p, hb)
                        yield
                    if p >= 1:
                        pv(i, p - 1, hb)
                        yield
                pvb = gpool.next()
                mem_attn(l, qmTz[hb], t_qmTz[hb], cat[:, 768:1024], t_cat, BNS[0], BNS[1], pvb)
                gpool.release(pvb)
                yield
                b = gpool.next()
                pbf = bank(b).bitcast(BF16)
                for kc in range(KC):
                    tr(pbf[:, kc * 128:(kc + 1) * 128], cat[:, kc * 128:(kc + 1) * 128], ident_b,
                       [t_cat, t_ident], [pb[b]])
                act(catT, pbf.rearrange("p (c t) -> p c t", c=8), AF.Copy, [pb[b]], [t_catT])
                gpool.release(b)
                if DBG_TILE is not None and i == DBG_TILE and s == 0:
                    P.last_dma_events.append(P.dma(SP, dbg_d, cat, reads=[t_cat]))
                yield
                out_proj(i, catT, t_catT, gpool)
                yield

            for it in range(NT + 2):
                gens, tot = [], []
                if it == 2:
                    bias_init()
                if 2 <= it:
                    gens.append(stage3(it - 2)); tot.append(16)
                if 1 <= it <= NT:
                    gens.append(stage2(it - 1)); tot.append(18)
                if it < NT:
                    gens.append(stage1(it)); tot.append(4)
                interleave(gens, tot)

        def mixer_b(s, l):
            P.barrier()
            load_wo(l)
            hT = view(WA0 + 28672, 8192, BF16, "p (c t) -> p c t", c=8)
            o = R10 + 16384
            hglu = view(o, 6504, BF16, "p (j t) -> p j t", j=6); o += 6528
            sig = [view(o + 2048 * i, 2048, F32) for i in range(2)]; o += 4096
            yn = view(o, 3072, F32); o += 3072
            qmf = view(o, 1024, F32); o += 1024
            qsq = view(o, 1024, F32); o += 1024
            qmTz = [view(o + 1024 * i, 1024, BF16, "p (h t) -> p h t", h=4) for i in range(8)]; o += 8192
            catm = [view(o + 512 * i, 512, BF16) for i in range(2)]; o += 1024
            catT = [view(o + 2048 * i, 2048, BF16, "p (c t) -> p c t", c=8) for i in range(2)]; o += 4096
            assert o <= R10 + 49152, o - R10
            qb = qsq.bitcast(BF16)
            yT0 = view(R20, 12288, F32, "p (j t) -> p j t", j=6)
            diag = [view(R20 + 12288 + 7936 * i, 7936, BF16, "p (k c) -> p k c", k=31) for i in range(2)]
            extra = [WA0 + 36864, WA0 + 38912, R20 + 28160, R20 + 30208, G0 + 22304, G0 + 1792 + 4096 + 2048]
            yT = [[yT0[:, j, :] for j in range(6)], [view(extra[j], 2048, F32) for j in range(6)]]
            t_hT = [Tile(f"bhT{i}") for i in range(4)]
            t_hglu = [Tile(f"hglu_{j}") for j in range(6)]
            t_sig = [Tile(f"sig{i}") for i in range(2)]
            t_yn, t_qmf, t_qsq = Tile("yn"), Tile("bqmf"), Tile("bqsq")
            t_qmTz = [Tile(f"bqmTz{i}") for i in range(8)]
            t_catm = [Tile(f"catm{i}") for i in range(2)]
            t_catT = [Tile(f"bcatT{i}") for i in range(2)]
            t_yT = [[Tile(f"yT{i}_{j}") for j in range(6)] for i in range(2)]
            t_diag = [Tile(f"diag{i}") for i in range(2)]
            P.dma(SP, gbc, bct_d[B_N1[l]], writes=[t_gbc])
            for i in range(8):
                mset(POOL, qmTz[i].rearrange("p h t -> p (h t)"), 0.0, [t_qmTz[i]])
            for j in range(6):
                mset(POOL, hglu[:, j, 0:30], 0.0, [t_hglu[j]])
            gpool = Ring([0, 1, 2, 3, 4])
            convw = cst[:, C_CONVW:C_CONVW + 186].rearrange("p (j k) -> p j k", j=6)
            NB = S // 512

            def stage_x(c):
                cb = c % 2
                for t4 in range(4):
                    ti = 4 * c + t4
                    yield from norm_transpose_g(x_sb[:, ti, :], [x_t[ti]], ti, hT[:, :, t4 * 128:(t4 + 1) * 128],
                                                [t_hT[t4]], gpool, evac=(DVE if t4 % 2 else ACT))
                for j in range(6):
                    si = j % 2
                    bg = gpool.next()
                    ng = 6 + j
                    for kc in range(KC):
                        mm(bank(bg), wa_B[:, kc, ng * 128:(ng + 1) * 128], hT[:, kc, :], kc == 0, kc == KC - 1,
                           t_hT + [t_wa[ng // 4][0]], [pb[bg]])
                    act(sig[si], bank(bg), AF.Sigmoid, [pb[bg], t_cst], [t_sig[si]], bias=col(C_BIN + ng))
                    gpool.release(bg)
                    yield
                    ba = gpool.next()
                    for kc in range(KC):
                        mm(bank(ba), wa_B[:, kc, j * 128:(j + 1) * 128], hT[:, kc, :], kc == 0, kc == KC - 1,
                           t_hT + [t_wa[j // 4][0]], [pb[ba]])
                    stt(hglu[:, j, 30:542], bank(ba), col(C_BIN + j), sig[si], ALU.add, ALU.mult,
                        [pb[ba], t_cst, t_sig[si]], [t_hglu[j]])
                    gpool.release(ba)
                    yield

            def stage_yconv(c):
                cb = c % 2
                for j in range(6):
                    db = j % 2
                    P.dma(POOL, diag[db].rearrange("p k c -> p (k c)"), diag_d[j], writes=[t_diag[db]],
                          max_dma_last_dim=4096)
                    b = gpool.next()
                    for k in range(31):
                        mm(bank(b), diag[db][:, k, :], hglu[:, j, k:k + 512], k == 0, k == 30,
                           [t_diag[db], t_hglu[j]], [pb[b]])
                        if k % 8 == 7:
                            yield
                    act(yT[cb][j], bank(b), AF.Identity, [pb[b], t_cst], [t_yT[cb][j]], bias=col(C_CONVB + j))
                    gpool.release(b)
                    cp(DVE, hglu[:, j, 0:30], hglu[:, j, 512:542], [t_hglu[j]], [t_hglu[j]])
                    yield

            def stage_yq(c):
                for t4 in range(4):
                    bq = gpool.next()
                    for kc in range(KC):
                        mm(bank(bq)[:, 0:256], hT[:, kc, t4 * 128:(t4 + 1) * 128], wa_B[:, kc, 1536:1792],
                           kc == 0, kc == KC - 1, [t_hT[t4], t_wa[3][0]], [pb[bq]])
                    tt(DVE, qmf, bank(bq)[:, 0:256], qmb, ALU.add, [pb[bq], t_qmb], [t_qmf])
                    gpool.release(bq)
                    yield
                    act(qsq, qmf, AF.Square, [t_qmf], [t_qsq])
                    yield
                    hs = stats[:, ST_H:ST_H + 4]
                    hl = stats[:, ST_HL:ST_HL + 4]
                    hr = stats[:, ST_HR:ST_HR + 4]
                    P.emit(DVE, lambda e, hs=hs: e.tensor_reduce(out=hs, in_=qsq.rearrange("p (h d) -> p h d", d=64),
                                                                 axis=AX.X, op=ALU.add), [t_qsq], [t_shead])
                    yield
                    act(hl, hs, AF.Ln, [t_shead, t_cst], [t_shead], scale=1.0 / 64, bias=col(C_EPS))
                    act(hr, hl, AF.Exp, [t_shead], [t_shead], scale=-0.5)
                    yield
                    tt(DVE, qb[:, 0:256].rearrange("p (h d) -> p h d", d=64), qmf.rearrange("p (h d) -> p h d", d=64),
                       hr.unsqueeze(2).broadcast_to([128, 4, 64]), ALU.mult, [t_qmf, t_shead], [t_qsq])
                    yield
                    b = gpool.next()
                    pbf = bank(b).bitcast(BF16)
                    for pp in range(2):
                        tr(pbf[:, pp * 128:(pp + 1) * 128], qb[:, pp * 128:(pp + 1) * 128], ident_b,
                           [t_qsq, t_ident], [pb[b]])
                    qi = (c % 2) * 4 + t4
                    qmT4 = qmTz[qi].rearrange("p (a b) t -> p a b t", b=2)
                    for hp in range(2):
                        cp(DVE, qmT4[hp * 64:(hp + 1) * 64, :, hp, :],
                           pbf[hp * 64:(hp + 1) * 64, 0:256].rearrange("p (k t) -> p k t", k=2), [pb[b]], [t_qmTz[qi]])
                    gpool.release(b)
                    yield

            yn2 = view(R10 + 16384 + 6528 + 4096 + 3072 + 1024 + 1024 + 8192 + 1024 + 4096, 3072, F32)
            yns = [yn, yn2]
            t_yns = [t_yn, Tile("yn2")]
            t_sbns = [t_sbn, Tile("sbn2")]

            def z_tile(c, t4):
                ti = 4 * c + t4
                tb = ti % 2
                cb = c % 2
                qi = cb * 4 + t4
                yb, t_yb, t_sb = yns[tb], t_yns[tb], t_sbns[tb]
                sb0 = ST_BN + 16 * tb
                b1, b2_ = gpool.next(), gpool.next()
                for j in range(6):
                    bb = b1 if j < 4 else b2_
                    tr(bank(bb)[:, (j % 4) * 128:(j % 4 + 1) * 128], yT[cb][j][:, t4 * 128:(t4 + 1) * 128], ident_f,
                       [t_yT[cb][j], t_ident], [pb[bb]])
                mv = stats[:, sb0 + 12:sb0 + 14]
                lnr = stats[:, sb0 + 14:sb0 + 16]
                P.emit(DVE, lambda e: e.bn_stats(out=stats[:, sb0:sb0 + 6], in_=bank(b1)), [pb[b1]], [t_sb])
                P.emit(DVE, lambda e: e.bn_stats(out=stats[:, sb0 + 6:sb0 + 12], in_=bank(b2_)[:, 0:256]), [pb[b2_]], [t_sb])
                P.emit(DVE, lambda e: e.bn_aggr(out=stats[:, sb0 + 12:sb0 + 14], in_=stats[:, sb0:sb0 + 12]), [t_sb], [t_sb])
                act(lnr[:, 0:1], mv[:, 1:2], AF.Ln, [t_sb, t_cst], [t_sb], bias=col(C_EPS))
                act(lnr[:, 1:2], lnr[:, 0:1], AF.Exp, [t_sb], [t_sb], scale=-0.5)
                ts(DVE, yb[:, 0:512], bank(b1), mv[:, 0:1], lnr[:, 1:2], ALU.subtract, ALU.mult, [pb[b1], t_sb], [t_yb])
                ts(DVE, yb[:, 512:768], bank(b2_)[:, 0:256], mv[:, 0:1], lnr[:, 1:2], ALU.subtract, ALU.mult,
                   [pb[b2_], t_sb], [t_yb])
                gpool.release(b1); gpool.release(b2_)
                yield
                b3, b4 = gpool.next(), gpool.next()
                for j in range(6):
                    bb = b3 if j < 4 else b4
                    tr(bank(bb)[:, (j % 4) * 128:(j % 4 + 1) * 128], yb[:, j * 128:(j + 1) * 128], ident_f,
                       [t_yb, t_ident], [pb[bb]])
                for j in range(6):
                    bb = b3 if j < 4 else b4
                    act(catT[tb][:, j, :], bank(bb)[:, (j % 4) * 128:(j % 4 + 1) * 128], AF.Silu,
                        [pb[bb], t_cst], [t_catT[tb]], scale=col(C_LNG + j), bias=col(C_LNB + j))
                gpool.release(b3); gpool.release(b4)
                yield
                mem_attn(l, qmTz[qi], t_qmTz[qi], catm[tb], t_catm[tb], 5, 6, 7)
                yield
                b = gpool.next()
                pbf = bank(b).bitcast(BF16)
                for kc in range(2):
                    tr(pbf[:, kc * 128:(kc + 1) * 128], catm[tb][:, kc * 128:(kc + 1) * 128], ident_b,
                       [t_catm[tb], t_ident], [pb[b]])
                act(catT[tb][:, 6:8, :], pbf[:, 0:256].rearrange("p (c t) -> p c t", c=2), AF.Copy,
                    [pb[b]], [t_catT[tb]])
                gpool.release(b)
                yield
                out_proj(ti, catT[tb], t_catT[tb], gpool)
                yield

            def stage_z(c):
                for t0 in (0, 2):
                    sub = [z_tile(c, t0), z_tile(c, t0 + 1)]
                    alive = [True, True]
                    while any(alive):
                        for k in range(2):
                            if alive[k]:
                                try:
                                    next(sub[k])
                                    yield
                                except StopIteration:
                                    alive[k] = False

            def chain(c):
                yield from stage_x(c)
                sub = [stage_yconv(c), stage_yq(c)]
                alive = [True, True]
                while any(alive):
                    for k in range(2):
                        if alive[k]:
                            try:
                                next(sub[k])
                                yield
                            except StopIteration:
                                alive[k] = False

            for c in range(NB + 1):
                gens, tot = [], []
                if c >= 1:
                    gens.append(stage_z(c - 1)); tot.append(20)
                if c < NB:
                    gens.append(chain(c)); tot.append(29 + 48)
                interleave(gens, tot)

        seq_layers = [(s, l) for s in range(nseq) for l in layers]
        for idx, (s, l) in enumerate(seq_layers):
            if l == layers[0]:
                if idx > 0:
                    P.barrier()
                mem_kv(s, mem_pre.get(s))
                pre_bias = None
                if idx == 0 and l == 0:
                    pre_bias = Tile("bias")
                    P.dma(SP, gbc, bct_d[B_N1[l]], writes=[t_gbc])
                    P.dma(SP, view(R20 + 18576, 12288, F32), bias_d, writes=[pre_bias])
                if idx == 0:
                    load_wa(layers[0])
                if idx == 0 or SKIP_FFN:
                    for i in range(NT):
                        P.dma(SP, x_sb[:, i, :], x_d[s, i * 128:(i + 1) * 128, :], writes=[x_t[i]])
            if l == 0:
                mixer_a(s, l, pre_bias if (idx == 0) else None)
            else:
                mixer_b(s, l)
            nxt = seq_layers[idx + 1][1] if idx + 1 < len(seq_layers) else None
            if SKIP_FFN:
                if nxt is not None:
                    load_wa(nxt)
                P.barrier()
                for ti in range(NT):
                    ev = P.dma(SP, out_d[s, ti * 128:(ti + 1) * 128, :], x_sb[:, ti, :], reads=[x_t[ti]])
                    P.last_dma_events.append(ev)
            else:
                ffn_phase(s, l, is_last=(l == layers[-1]), next_wa=nxt)
        P.wait_events(SP, P.last_dma_events)
        P.lower(block)
    return nc


def _pack_consts(inp):
    cst = np.zeros((128, NCST), np.float32)
    p = np.arange(128)
    cst[:, C_AQG] = inp["a_q_g"][0][p % 64]
    cst[:, C_AKG] = inp["a_k_g"][0][p % 64]
    cst[:, C_MQG0] = inp["mq_g"][0][p % 64]
    cst[:, C_MKG0] = inp["mk_g"][0][p % 64]
    cst[:, C_MQG1] = inp["mq_g"][1][p % 64]
    cst[:, C_MKG1] = inp["mk_g"][1][p % 64]
    cst[:, C_BIN:C_BIN + 12] = inp["b_b_in"][0][:1536].reshape(12, 128).T
    cst[:, C_CONVB:C_CONVB + 6] = inp["b_conv_b"][0].reshape(6, 128).T
    cst[:, C_LNG:C_LNG + 6] = inp["b_ln_g"][0].reshape(6, 128).T
    cst[:, C_LNB:C_LNB + 6] = inp["b_ln_b"][0].reshape(6, 128).T
    cst[:, C_BFAR:C_BFAR + 12] = inp["a_rel_bias"][0][:, 191][None, :]
    cst[:, C_EPS] = EPS
    cw = inp["b_conv_w"][0]
    cst[:, C_CONVW:C_CONVW + 186] = cw.reshape(31, 6, 128).transpose(2, 1, 0).reshape(128, 186)
    bct = np.zeros((6, 128, D), np.float32)
    bct[0] = inp["norm1_g"][0][None, :]
    bct[1] = inp["norm2_g"][0][None, :]
    bct[2] = inp["norm1_g"][1][None, :]
    bct[3] = inp["norm2_g"][1][None, :]
    bct[4] = inp["mem_norm_g"][0][None, :]
    bct[5] = inp["mem_norm_g"][1][None, :]
    qmb = np.ascontiguousarray(np.broadcast_to(inp["b_b_in"][0][1536:1792][None, :], (128, 256))).astype(np.float32)
    rb = inp["a_rel_bias"][0]
    k = np.arange(128)[:, None]
    q = np.arange(128)[None, :]
    bias = np.zeros((128, 6, 4, 128), np.float32)
    for h in range(12):
        for dl in range(2):
            idx = np.clip(q - k + 128 * dl, -63, 128) + 63
            bias[:, h // 2, dl * 2 + (h % 2), :] = rb[h][idx]
    dg = np.zeros((6, 128, 31, 128), np.float32)
    ar = np.arange(128)
    for j in range(6):
        dg[j, ar, :, ar] = cw[:, j * 128:(j + 1) * 128].T
    return cst, bct, qmb, bias.reshape(128, 6 * 4 * 128), dg.reshape(6, 128, 31 * 128)


_NC_CACHE = {}


def kernel(**inputs):
    inp = {k: np.ascontiguousarray(np.asarray(v, dtype=np.float32)) for k, v in inputs.items()}
    cst, bct, qmb, bias, dg = _pack_consts(inp)
    key = "full"
    if key not in _NC_CACHE:
        _NC_CACHE[key] = build_program()
    nc = _NC_CACHE[key]
    in_maps = []
    for c in range(8):
        in_maps.append({
            "x": inp["x"][2 * c:2 * c + 2], "mem": inp["mem"][2 * c:2 * c + 2],
            "a_w_in": inp["a_w_in"], "b_w_in": inp["b_w_in"], "w_mem_kv": inp["w_mem_kv"],
            "w_out": inp["w_out"], "w_gate": inp["w_gate"], "w_up": inp["w_up"], "w_down": inp["w_down"],
            "cst": cst, "bct": bct, "qmb": qmb, "biasT": bias, "convdiag": dg,
        })
    res = run_bass_kernel_spmd(nc, in_maps, core_ids=list(range(8)))
    out = np.concatenate([r["out"] for r in res.results], axis=0)
    return out.astype(np.float32)
```

```python
import numpy as np
from contextlib import ExitStack
import concourse.bass as bass
import concourse.mybir as mybir
from concourse.bass_utils import run_bass_kernel_spmd

F32 = mybir.dt.float32
BF16 = mybir.dt.bfloat16
U8 = mybir.dt.uint8
AF = mybir.ActivationFunctionType
ALU = mybir.AluOpType
AX = mybir.AxisListType

PE, ACT, DVE, POOL, SP = range(5)

S = 2048
D = 1024
NT = 16
KC = 8
DFF = 2816
NFC = 22
FG = 4
NSEQ = 2
EPS = 1e-6
SCALE = 0.125
NEG = -30000.0
POOL_THROTTLE = False


class Tile:
    __slots__ = ("name", "excl", "w", "r", "dr")

    def __init__(self, name, excl=False):
        self.name = name
        self.excl = excl
        self.w = None
        self.r = {}
        self.dr = []


class Op:
    __slots__ = ("fn", "waits", "signal", "sigval", "dma")

    def __init__(self, fn):
        self.fn = fn
        self.waits = []
        self.signal = False
        self.sigval = 0
        self.dma = None


class Prog:
    NDS = 24

    def __init__(self, nc, es):
        self.nc = nc
        self.ops = [[] for _ in range(5)]
        self.waited = [dict() for _ in range(5)]
        names = ["pe", "act", "dve", "pool", "sp"]
        self.esem = [es.enter_context(nc.semaphore("e_" + n)) for n in names]
        self.dsem = {}
        self.dcount = {}
        self.dnext = {}
        self.nds = {POOL: 72, SP: 24}
        for q in (POOL, SP):
            self.dsem[q] = [es.enter_context(nc.semaphore(f"d{q}_{i}")) for i in range(self.nds[q])]
            self.dcount[q] = [0] * self.nds[q]
            self.dnext[q] = 0
        self.last_dma_events = []

    def _deps(self, eng, reads, writes, is_dma):
        deps = []
        for t in reads:
            if t.w is not None:
                deps.append(t.w)
            if t.excl:
                for e2, s in t.r.items():
                    if e2 != eng:
                        deps.append(("e", e2, s))
        for t in writes:
            if t.w is not None:
                if is_dma or not (t.w[0] == "e" and t.w[1] == eng):
                    deps.append(t.w)
            for e2, s in t.r.items():
                if is_dma or e2 != eng:
                    deps.append(("e", e2, s))
            deps.extend(t.dr)
        return deps

    def _add_waits(self, eng, op, deps):
        wd = self.waited[eng]
        for d in deps:
            if d[0] == "e":
                key = ("e", d[1])
                if wd.get(key, -1) >= d[2]:
                    continue
                wd[key] = d[2]
                self.ops[d[1]][d[2]].signal = True
                op.waits.append(d)
            else:
                key = ("d", d[1], d[2])
                if wd.get(key, 0) >= d[3]:
                    continue
                wd[key] = d[3]
                op.waits.append(d)

    def emit(self, eng, fn, reads=(), writes=()):
        op = Op(fn)
        self._add_waits(eng, op, self._deps(eng, reads, writes, False))
        seq = len(self.ops[eng])
        self.ops[eng].append(op)
        for t in reads:
            t.r[eng] = seq
        for t in writes:
            t.w = ("e", eng, seq)
            t.r = {}
            t.dr = []
        return op

    def dma(self, q, out, in_, reads=(), writes=(), **kw):
        op = Op(lambda e: e.dma_start(out=out, in_=in_, **kw))
        self._add_waits(q, op, self._deps(q, reads, writes, True))
        if q == POOL:
            self.pool_evs = getattr(self, "pool_evs", [])
            if POOL_THROTTLE and len(self.pool_evs) >= 6:
                self._add_waits(q, op, [self.pool_evs[-6]])
        i = self.dnext[q]
        self.dnext[q] = (i + 1) % self.nds[q]
        if self.dcount[q][i] > 0:
            self._add_waits(q, op, [("d", q, i, 16 * self.dcount[q][i])])
        self.dcount[q][i] += 1
        ev = ("d", q, i, 16 * self.dcount[q][i])
        op.dma = (q, i)
        if q == POOL:
            self.pool_evs.append(ev)
        self.ops[q].append(op)
        for t in writes:
            t.w = ev
            t.r = {}
            t.dr = []
        for t in reads:
            t.dr.append(ev)
        return ev

    def wait_events(self, eng, evs):
        op = Op(None)
        self._add_waits(eng, op, list(evs))
        self.ops[eng].append(op)

    def barrier(self):
        lasts = []
        for e in range(5):
            if self.ops[e]:
                s = len(self.ops[e]) - 1
                while s >= 0 and (self.ops[e][s].fn is None or self.ops[e][s].dma is not None):
                    s -= 1
                if s >= 0:
                    lasts.append(("e", e, s))
        for c in range(5):
            self.wait_events(c, [d for d in lasts if d[1] != c])

    def lower(self, block):
        for e in range(5):
            c = 0
            for op in self.ops[e]:
                if op.signal:
                    c += 1
                    op.sigval = c

        def mk(e):
            def body(eng):
                for op in self.ops[e]:
                    for w in op.waits:
                        if w[0] == "e":
                            eng.wait_ge(self.esem[w[1]], self.ops[w[1]][w[2]].sigval)
                        else:
                            eng.wait_ge(self.dsem[w[1]][w[2]], w[3])
                    if op.fn is None:
                        continue
                    inst = op.fn(eng)
                    if op.dma is not None:
                        inst.then_inc(self.dsem[op.dma[0]][op.dma[1]], 16)
                    elif op.signal:
                        inst.then_inc(self.esem[e], 1)
            return body

        block.tensor(mk(PE))
        block.scalar(mk(ACT))
        block.vector(mk(DVE))
        block.gpsimd(mk(POOL))
        block.sync(mk(SP))


def interleave(gens, totals=None):
    gens = list(gens)
    n = len(gens)
    if totals is None:
        totals = [1] * n
    done = [0] * n
    alive = [True] * n
    while any(alive):
        best = None
        for k in range(n):
            if alive[k]:
                frac = (done[k] + 1) / float(totals[k])
                if best is None or frac < best[0]:
                    best = (frac, k)
        k = best[1]
        try:
            next(gens[k])
            done[k] += 1
        except StopIteration:
            alive[k] = False


def run(gen):
    for _ in gen:
        pass


class Ring:
    def __init__(self, items):
        self.items = list(items)
        self.i = 0
        self.open = set()

    def next(self):
        for _ in range(len(self.items)):
            it = self.items[self.i]
            self.i = (self.i + 1) % len(self.items)
            if it not in self.open:
                self.open.add(it)
                return it
        raise AssertionError("all psum banks of the ring are open")

    def release(self, it):
        self.open.discard(it)


C_AQG, C_AKG, C_MQG0, C_MKG0, C_MQG1, C_MKG1 = 0, 1, 2, 3, 4, 5
C_BIN = 6
C_CONVB = 18
C_LNG = 24
C_LNB = 30
C_BFAR = 36
C_EPS = 48
C_CONVW = 50
C_GA, C_GM0, C_GM1 = 240, 241, 242
C_HBG = 243
NCST = 256
SKIP_FFN = False
DBG_TILE = None

B_N1 = (0, 2)
B_N2 = (1, 3)
B_MEM = (4, 5)

WA_GROUPS_A = [
    (0, 512, [(0, 512)]),
    (512, 512, [(512, 256), (2304, 256)]),
    (1024, 512, [(768, 512)]),
    (1536, 256, [(1280, 256)]),
    (1792, 512, [(1536, 512)]),
    (2304, 256, [(2048, 256)]),
]


def build_program(layers=(0, 1), nseq=NSEQ):
    nc = bass.Bass("TRN2", target_bir_lowering=False)
    dt = nc.dram_tensor
    x_d = dt("x", [nseq, S, D], F32, kind="ExternalInput").ap()
    mem_d = dt("mem", [nseq, 256, D], F32, kind="ExternalInput").ap()
    a_w_in = dt("a_w_in", [1, D, 2560], F32, kind="ExternalInput").ap()
    b_w_in = dt("b_w_in", [1, D, 1792], F32, kind="ExternalInput").ap()
    w_mem_kv = dt("w_mem_kv", [2, D, 512], F32, kind="ExternalInput").ap()
    w_out = dt("w_out", [2, D, D], F32, kind="ExternalInput").ap()
    w_gate = dt("w_gate", [2, D, DFF], F32, kind="ExternalInput").ap()
    w_up = dt("w_up", [2, D, DFF], F32, kind="ExternalInput").ap()
    w_down = dt("w_down", [2, DFF, D], F32, kind="ExternalInput").ap()
    cst_d = dt("cst", [128, NCST], F32, kind="ExternalInput").ap()
    bct_d = dt("bct", [6, 128, D], F32, kind="ExternalInput").ap()
    qmb_d = dt("qmb", [128, 256], F32, kind="ExternalInput").ap()
    bias_d = dt("biasT", [128, 6 * 4 * 128], F32, kind="ExternalInput").ap()
    diag_d = dt("convdiag", [6, 128, 31 * 128], F32, kind="ExternalInput").ap()
    out_d = dt("out", [nseq, S, D], F32, kind="ExternalOutput").ap()
    dbg_d = dt("dbg", [128, 1024], BF16, kind="ExternalOutput").ap() if DBG_TILE is not None else None

    es = ExitStack()
    with es:
        ARENA = 212800
        arena = es.enter_context(nc.sbuf_tensor("arena", [128, ARENA], U8))
        ps = es.enter_context(nc.psum_tensor("ps", [128, 8, 512], F32))
        P = Prog(nc, es)
        block = es.enter_context(nc.Block())

        def view(off, nbytes, dtype, pat=None, **kw):
            ap = arena[:, off:off + nbytes].bitcast(dtype)
            if pat is not None:
                ap = ap.rearrange(pat, **kw)
            return ap

        def mm(out, lhsT, rhs, start, stop, reads, writes):
            P.emit(PE, lambda e: e.matmul(out, lhsT=lhsT, rhs=rhs, start=start, stop=stop), reads, writes)

        def tr(out, in_, ident, reads, writes):
            P.emit(PE, lambda e: e.transpose(out=out, in_=in_, identity=ident), reads, writes)

        def act(out, in_, func, reads, writes, **kw):
            P.emit(ACT, lambda e: e.activation(out=out, in_=in_, func=func, **kw), reads, writes)

        def tt(eng, out, in0, in1, op, reads, writes):
            P.emit(eng, lambda e: e.tensor_tensor(out=out, in0=in0, in1=in1, op=op), reads, writes)

        def stt(out, in0, scalar, in1, op0, op1, reads, writes):
            P.emit(DVE, lambda e: e.scalar_tensor_tensor(out=out, in0=in0, scalar=scalar, in1=in1, op0=op0, op1=op1),
                   reads, writes)

        def ts(eng, out, in0, s1, s2, op0, op1, reads, writes):
            if op1 is None:
                P.emit(eng, lambda e: e.tensor_scalar(out=out, in0=in0, scalar1=s1, scalar2=None, op0=op0), reads, writes)
            else:
                P.emit(eng, lambda e: e.tensor_scalar(out=out, in0=in0, scalar1=s1, scalar2=s2, op0=op0, op1=op1),
                       reads, writes)

        def cp(eng, out, in_, reads, writes):
            P.emit(eng, lambda e: e.tensor_copy(out=out, in_=in_), reads, writes)

        def mset(eng, ap, val, writes):
            P.emit(eng, lambda e: e.memset(ap, val), (), writes)

        X0 = 0
        WA0 = 65536
        R10 = WA0 + 40960
        R20 = R10 + 49152
        G0 = R20 + 32768
        assert G0 + 24384 <= ARENA
        x_sb = view(X0, 65536, F32, "p (t d) -> p t d", t=NT)
        x_t = [Tile(f"x{i}") for i in range(NT)]

        g = G0
        ident_f = view(g, 512, F32); g += 512
        ident_b = view(g, 256, BF16); g += 256
        cst = view(g, 1024, F32); g += 1024
        gbc = view(g, 4096, F32); g += 4096
        xn = view(g, 4096, F32); g += 4096
        kmT = [view(g + 1024 * l, 1024, BF16, "p (c m) -> p c m", c=2) for l in range(2)]; g += 2048
        vm = [view(g + 1040 * l, 1040, BF16, "p (t h e) -> p t h e", t=2, h=4) for l in range(2)]; g += 2080
        stats = view(g, 1024, F32); g += 1024
        ffn_silu = [view(g + 1024 * i, 1024, F32) for i in range(2)]; g += 2048
        ffn_act = [view(g + 512 * i, 512, BF16) for i in range(4)]; g += 2048
        qmb = view(g, 1024, F32); g += 1024
        ptm = view(g, 2048, BF16); g += 2048
        assert g <= G0 + 24384, g - G0
        t_ident, t_cst, t_gbc, t_xn_default = Tile("ident"), Tile("cst"), Tile("gbc"), Tile("xn")
        t_km = [Tile(f"km{l}") for l in range(2)]
        t_vm = [Tile(f"vm{l}") for l in range(2)]
        t_silu = [Tile(f"silu{i}") for i in range(2)]
        t_act = [Tile(f"act{i}") for i in range(4)]
        t_qmb, t_ptm = Tile("qmb"), Tile("ptm")
        ST_SS, ST_LN, ST_RS = 0, 16, 32
        ST_H, ST_HL, ST_HR = 48, 80, 112
        ST_REC = 144
        ST_BN = 176
        t_srms, t_shead, t_srec, t_srecm, t_sbn = Tile("srms"), Tile("shead"), Tile("srec"), Tile("srecm"), Tile("sbn")

        pb = [Tile(f"ps{b}", excl=True) for b in range(8)]

        def bank(b):
            return ps[:, b, :]

        def col(c, n=1):
            return cst[:, c:c + n]

        P.dma(SP, cst, cst_d, writes=[t_cst])
        P.dma(SP, qmb, qmb_d, writes=[t_qmb])
        mset(POOL, ident_f, 0.0, [t_ident])
        P.emit(POOL, lambda e: e.affine_select(out=ident_f, in_=ident_f, pattern=[[-1, 128]],
                                               compare_op=ALU.not_equal, fill=1.0, base=0,
                                               channel_multiplier=1),
               reads=[t_ident], writes=[t_ident])
        cp(POOL, ident_b, ident_f, [t_ident], [t_ident])

        tt(DVE, col(C_GA), col(C_AQG), col(C_AKG), ALU.mult, [t_cst], [t_cst])
        tt(DVE, col(C_GM0), col(C_MQG0), col(C_MKG0), ALU.mult, [t_cst], [t_cst])
        tt(DVE, col(C_GM1), col(C_MQG1), col(C_MKG1), ALU.mult, [t_cst], [t_cst])

        xnb_default = xn.bitcast(BF16)[:, 0:D]

        xnb2 = xn.bitcast(BF16)[:, D:2 * D]
        t_xn2 = Tile("xn2")

        def norm_transpose_g(src_ap, src_tiles, sscol, dst, dst_tiles, pool, evac=DVE, xbuf=None, gain=None):
            ss = stats[:, ST_SS + sscol:ST_SS + sscol + 1]
            ln = stats[:, ST_LN + sscol:ST_LN + sscol + 1]
            rs = stats[:, ST_RS + sscol:ST_RS + sscol + 1]
            xnb, t_xn = xbuf if xbuf is not None else (xnb_default, t_xn_default)
            act(xnb, src_ap, AF.Square, src_tiles, [t_xn, t_srms], accum_out=ss)
            yield
            act(ln, ss, AF.Ln, [t_srms, t_cst], [t_srms], scale=1.0 / D, bias=col(C_EPS))
            act(rs, ln, AF.Exp, [t_srms], [t_srms], scale=-0.5)
            yield
            g_ap, g_t = gain if gain is not None else (gbc, t_gbc)
            stt(xnb, src_ap, rs, g_ap, ALU.mult, ALU.mult, src_tiles + [t_srms, g_t], [t_xn])
            yield
            b = pool.next()
            pbf = bank(b).bitcast(BF16)
            for kc in range(KC):
                tr(pbf[:, kc * 128:(kc + 1) * 128], xnb[:, kc * 128:(kc + 1) * 128], ident_b, [t_xn, t_ident], [pb[b]])
            src3 = pbf.rearrange("p (k t) -> p k t", k=KC)
            if evac == ACT:
                act(dst, src3, AF.Copy, [pb[b]], dst_tiles)
            else:
                cp(DVE, dst, src3, [pb[b]], dst_tiles)
            pool.release(b)
            yield

        def norm_transpose(src_ap, src_tiles, sscol, dsts, dst_tiles, pool):
            raise NotImplementedError

        def head_norm_transpose(nheads, qf, t_qf, qsq, t_qsq, dst_fn, pool):
            w = nheads * 64
            q3 = qf[:, 0:w].rearrange("p (h d) -> p h d", d=64)
            act(qsq[:, 0:w], qf[:, 0:w], AF.Square, [t_qf], [t_qsq])
            hs = stats[:, ST_H:ST_H + nheads]
            hl = stats[:, ST_HL:ST_HL + nheads]
            hr = stats[:, ST_HR:ST_HR + nheads]
            P.emit(DVE, lambda e: e.tensor_reduce(out=hs, in_=qsq[:, 0:w].rearrange("p (h d) -> p h d", d=64),
                                                  axis=AX.X, op=ALU.add), [t_qsq], [t_shead])
            act(hl, hs, AF.Ln, [t_shead, t_cst], [t_shead], scale=1.0 / 64, bias=col(C_EPS))
            act(hr, hl, AF.Exp, [t_shead], [t_shead], scale=-0.5)
            tt(DVE, q3, q3, hr.unsqueeze(2).broadcast_to([128, nheads, 64]), ALU.mult, [t_qf, t_shead], [t_qf])
            npairs = nheads // 2
            p0 = 0
            while p0 < npairs:
                np_ = min(4, npairs - p0)
                b = pool.next()
                for k in range(np_):
                    pp = p0 + k
                    tr(bank(b)[:, k * 128:(k + 1) * 128], qf[:, pp * 128:(pp + 1) * 128], ident_f,
                       [t_qf, t_ident], [pb[b]])
                dst_fn(b, p0, np_)
                pool.release(b)
                p0 += np_

        mk_memt = view(R10, 8192, F32, "p (t d) -> p t d", t=2)
        mk_gb = [view(R10 + 8192 + 4096 * k, 4096, F32) for k in range(2)]
        mk_wkv = [view(R10 + 16384, 8192, BF16, "p (c n) -> p c n", c=8),
                  view(R10 + 24576, 8192, BF16, "p (c n) -> p c n", c=8)]

        def mem_kv_loads(s, extra_writes=(), only_slot0=False):
            t_memt = Tile("memt")
            t_gb = [Tile("mgb0"), Tile("mgb1")]
            t_wkv = [Tile("wkv0"), Tile("wkv1")]
            ew = list(extra_writes)
            P.dma(SP, mk_memt, mem_d[s].rearrange("(t p) d -> p t d", p=128), writes=[t_memt] + ew)
            for k, l in enumerate(layers):
                P.dma(SP, mk_gb[k], bct_d[B_MEM[l]], writes=[t_gb[k]] + ew)
            for k, l in enumerate(layers):
                if only_slot0 and k > 0:
                    continue
                P.dma(POOL, mk_wkv[k], w_mem_kv[l].rearrange("(c p) n -> p c n", p=128), writes=[t_wkv[k]] + ew,
                      max_dma_last_dim=4096)
            return dict(memt=t_memt, gb=t_gb, wkv=t_wkv, have_wkv1=not (only_slot0 and len(layers) > 1))

        def mem_kv(s, pre=None):
            if pre is None:
                pre = mem_kv_loads(s)
            elif not pre["have_wkv1"]:
                P.dma(POOL, mk_wkv[1], w_mem_kv[layers[1]].rearrange("(c p) n -> p c n", p=128), writes=[pre["wkv"][1]],
                      max_dma_last_dim=4096)
            memt, t_memt = mk_memt, pre["memt"]
            pool = Ring([0, 1, 2, 3, 4, 5])

            def one_layer(l, k):
                o = R10 + 32768 + k * 6144
                memT = view(o, 4096, BF16, "p (c m) -> p c m", c=8)
                qf = view(o + 4096, 1024, F32)
                qsq = view(o + 5120, 1024, F32)
                wkv, gb = mk_wkv[k], mk_gb[k]
                t_wkv, t_gb = pre["wkv"][k], pre["gb"][k]
                t_memT, t_qf, t_qsq = Tile("memT"), Tile("mqf"), Tile("mqsq")
                xb = (xnb_default, t_xn_default) if k == 0 else (xnb2, t_xn2)
                for mt in range(2):
                    yield from norm_transpose_g(memt[:, mt, :], [t_memt], 2 * k + mt, memT[:, :, mt * 128:(mt + 1) * 128],
                                                [t_memT], pool, xbuf=xb, gain=(gb, t_gb))
                gc = C_GM0 if l == 0 else C_GM1
                hs = stats[:, ST_H + 4 * k:ST_H + 4 * k + 4]
                hl = stats[:, ST_HL + 4 * k:ST_HL + 4 * k + 4]
                hr = stats[:, ST_HR + 4 * k:ST_HR + 4 * k + 4]
                t_sh = Tile("msh")
                for mt in range(2):
                    b = pool.next()
                    for kc in range(KC):
                        mm(bank(b), memT[:, kc, mt * 128:(mt + 1) * 128], wkv[:, kc, :], kc == 0, kc == KC - 1,
                           [t_memT, t_wkv], [pb[b]])
                    cp(DVE, vm[l][:, mt, :, 0:64], bank(b)[:, 256:512].rearrange("p (h d) -> p h d", d=64),
                       [pb[b]], [t_vm[l]])
                    mset(POOL, vm[l][:, mt, :, 64:65], 1.0, [t_vm[l]])
                    cp(DVE, qf[:, 0:256], bank(b)[:, 0:256], [pb[b]], [t_qf])
                    pool.release(b)
                    yield
                    act(qsq, qf, AF.Square, [t_qf], [t_qsq])
                    yield
                    P.emit(DVE, lambda e, hs=hs, qsq=qsq: e.tensor_reduce(
                        out=hs, in_=qsq.rearrange("p (h d) -> p h d", d=64), axis=AX.X, op=ALU.add), [t_qsq], [t_sh])
                    yield
                    act(hl, hs, AF.Ln, [t_sh, t_cst], [t_sh], scale=1.0 / 64, bias=col(C_EPS))
                    act(hr, hl, AF.Exp, [t_sh], [t_sh], scale=-0.5)
                    yield
                    q3 = qf.rearrange("p (h d) -> p h d", d=64)
                    tt(DVE, q3, q3, hr.unsqueeze(2).broadcast_to([128, 4, 64]), ALU.mult, [t_qf, t_sh], [t_qf])
                    yield
                    b2 = pool.next()
                    for pp in range(2):
                        tr(bank(b2)[:, pp * 128:(pp + 1) * 128], qf[:, pp * 128:(pp + 1) * 128], ident_f,
                           [t_qf, t_ident], [pb[b2]])
                    act(kmT[l][:, :, mt * 128:(mt + 1) * 128], bank(b2)[:, 0:256].rearrange("p (k t) -> p k t", k=2),
                        AF.Copy, [pb[b2], t_cst], [t_km[l]], scale=col(gc))
                    pool.release(b2)
                    yield

            interleave([one_layer(l, k) for k, l in enumerate(layers)])

        wa_A = view(WA0, 40960, BF16, "p (c n) -> p c n", c=8)
        wa_B = view(WA0, 28672, BF16, "p (c n) -> p c n", c=8)
        wo = view(R10, 16384, BF16, "p (c n) -> p c n", c=8)
        t_wa = [[Tile(f"wa{i}_{j}") for j in range(2)] for i in range(6)]
        t_wo = [Tile(f"wo{i}") for i in range(2)]

        def load_wa(l):
            if l == 0:
                for gi, (c0, n, pieces) in enumerate(WA_GROUPS_A):
                    o = c0
                    for pi, (d0, dn) in enumerate(pieces):
                        P.dma(POOL, wa_A[:, :, o:o + dn], a_w_in[0][:, d0:d0 + dn].rearrange("(c p) n -> p c n", p=128),
                              writes=[t_wa[gi][pi]], max_dma_last_dim=4096)
                        o += dn
            else:
                bounds = [0, 512, 1024, 1536, 1792]
                for gi in range(4):
                    a, b_ = bounds[gi], bounds[gi + 1]
                    P.dma(POOL, wa_B[:, :, a:b_], b_w_in[0][:, a:b_].rearrange("(c p) n -> p c n", p=128),
                          writes=[t_wa[gi][0]], max_dma_last_dim=4096)

        def load_wo(l):
            for hf in range(2):
                P.dma(POOL, wo[:, :, hf * 512:(hf + 1) * 512],
                      w_out[l][:, hf * 512:(hf + 1) * 512].rearrange("(c p) n -> p c n", p=128),
                      writes=[t_wo[hf]], max_dma_last_dim=4096)

        mem_pre = {}

        def ffn_phase(s, l, is_last, next_wa):
            P.barrier()
            h2T = view(R20, 32768, BF16, "p (c t) -> p c t", c=8)
            t_h2 = [Tile(f"h2T{i}") for i in range(NT)]
            fs = []
            for i in range(2):
                o = R10 + i * 24576
                fs.append(dict(
                    gate=view(o, 8192, BF16, "p (c n) -> p c n", c=8),
                    up=view(o + 8192, 8192, BF16, "p (c n) -> p c n", c=8),
                    down=view(o + 16384, 8192, BF16, "p (j n) -> p j n", j=4),
                    tg=[Tile(f"fg{i}_{j}") for j in range(FG)], tu=[Tile(f"fu{i}_{j}") for j in range(FG)],
                    td=[Tile(f"fd{i}_{j}") for j in range(FG)]))
            groups = []
            c = 0
            while c < NFC:
                n = min(FG, NFC - c)
                groups.append((c, n))
                c += n

            def load_group(gi):
                c0, n = groups[gi]
                f = fs[gi % 2]
                if False:
                    for j in range(n):
                        cc = c0 + j
                        P.dma(POOL, f["gate"][:, :, j * 128:(j + 1) * 128],
                              w_gate[l][:, cc * 128:(cc + 1) * 128].rearrange("(c p) n -> p c n", p=128),
                              writes=[f["tg"][j]], max_dma_last_dim=4096)
                        P.dma(POOL, f["up"][:, :, j * 128:(j + 1) * 128],
                              w_up[l][:, cc * 128:(cc + 1) * 128].rearrange("(c p) n -> p c n", p=128),
                              writes=[f["tu"][j]], max_dma_last_dim=4096)
                        P.dma(POOL, f["down"][:, j, :], w_down[l][cc * 128:(cc + 1) * 128, :],
                              writes=[f["td"][j]], max_dma_last_dim=4096)
                    return
                P.dma(POOL, f["gate"][:, :, 0:n * 128],
                      w_gate[l][:, c0 * 128:(c0 + n) * 128].rearrange("(c p) n -> p c n", p=128),
                      writes=f["tg"][0:n], max_dma_last_dim=4096)
                P.dma(POOL, f["up"][:, :, 0:n * 128],
                      w_up[l][:, c0 * 128:(c0 + n) * 128].rearrange("(c p) n -> p c n", p=128),
                      writes=f["tu"][0:n], max_dma_last_dim=4096)
                P.dma(POOL, f["down"][:, 0:n, :],
                      w_down[l][c0 * 128:(c0 + n) * 128, :].rearrange("(j p) n -> p j n", p=128),
                      writes=f["td"][0:n], max_dma_last_dim=4096)

            load_group(0)
            load_group(1)
            P.dma(SP, gbc, bct_d[B_N2[l]], writes=[t_gbc])
            gu = Ring([0, 1, 2, 3])

            def h2_gens(tb):
                gens = []
                for k, i in enumerate((2 * tb, 2 * tb + 1)):
                    xb = (xnb_default, t_xn_default) if k == 0 else (xnb2, t_xn2)
                    g_ = norm_transpose_g(x_sb[:, i, :], [x_t[i]], i, h2T[:, :, i * 128:(i + 1) * 128],
                                          [t_h2[i]], gu, evac=(DVE if i % 2 else ACT), xbuf=xb)
                    for _ in range(3):
                        next(g_)
                    gens.append(g_)
                return gens

            for g_ in h2_gens(0):
                run(g_)
            pending_h2 = h2_gens(1)
            silr = Ring([0, 1])
            actr = Ring([0, 1, 2, 3])

            def do_gu(f, tb, j):
                b = gu.next()
                rhs_t = [t_h2[2 * tb], t_h2[2 * tb + 1]]
                for kc in range(KC):
                    mm(bank(b)[:, 0:256], f["gate"][:, kc, j * 128:(j + 1) * 128], h2T[:, kc, tb * 256:(tb + 1) * 256],
                       kc == 0, kc == KC - 1, [f["tg"][j]] + rhs_t, [pb[b]])
                for kc in range(KC):
                    mm(bank(b)[:, 256:512], f["up"][:, kc, j * 128:(j + 1) * 128], h2T[:, kc, tb * 256:(tb + 1) * 256],
                       kc == 0, kc == KC - 1, [f["tu"][j]] + rhs_t, [pb[b]])
                si = silr.next(); silr.release(si)
                ai = actr.next(); actr.release(ai)
                act(ffn_silu[si], bank(b)[:, 0:256], AF.Silu, [pb[b]], [t_silu[si]])
                tt(DVE, ffn_act[ai], bank(b)[:, 256:512], ffn_silu[si], ALU.mult, [pb[b], t_silu[si]], [t_act[ai]])
                gu.release(b)
                return ai

            def do_down(f, n, last_group, tb, j, ai):
                for t2 in range(2):
                    for hf in range(2):
                        b = 4 + t2 * 2 + hf
                        mm(bank(b), ffn_act[ai][:, t2 * 128:(t2 + 1) * 128], f["down"][:, j, hf * 512:(hf + 1) * 512],
                           j == 0, j == n - 1, [t_act[ai], f["td"][j]], [pb[b]])
                if j == n - 1:
                    for t2 in range(2):
                        ti = 2 * tb + t2
                        xv = x_sb[:, ti, :].rearrange("p (h n) -> p h n", h=2)
                        tt(DVE, xv, ps[:, 4 + 2 * t2:6 + 2 * t2, :], xv, ALU.add,
                           [pb[4 + 2 * t2], pb[5 + 2 * t2], x_t[ti]], [x_t[ti]])
                        if is_last and last_group:
                            ev = P.dma(SP, out_d[s, ti * 128:(ti + 1) * 128, :], x_sb[:, ti, :], reads=[x_t[ti]])
                            P.last_dma_events.append(ev)
                            if s + 1 < nseq:
                                P.dma(SP, x_sb[:, ti, :], x_d[s + 1, ti * 128:(ti + 1) * 128, :], writes=[x_t[ti]])

            for gi, (c0, n) in enumerate(groups):
                f = fs[gi % 2]
                pend = []
                if is_last and s + 1 < nseq and gi == len(groups) - 1 and gi % 2 == 1:
                    f0 = fs[0]
                    mem_pre[s + 1] = mem_kv_loads(s + 1, extra_writes=f0["tg"] + f0["tu"] + f0["td"], only_slot0=True)
                for tb in range(8):
                    for j in range(n):
                        ai = do_gu(f, tb, j)
                        pend.append((tb, j, ai))
                        if len(pend) > 2:
                            do_down(f, n, gi == len(groups) - 1, *pend.pop(0))
                    if gi == 0 and tb + 1 < 8:
                        for g_ in pending_h2:
                            run(g_)
                        pending_h2 = h2_gens(tb + 2) if tb + 2 < 8 else []
                while pend:
                    do_down(f, n, gi == len(groups) - 1, *pend.pop(0))
                if gi == 0 and next_wa is not None:
                    load_wa(next_wa)
                if gi + 2 < len(groups):
                    load_group(gi + 2)

        def mem_attn(l, qmTz, t_qmTz, dst, t_cat, b0, b1, pvb):
            for h in range(4):
                for mt in range(2):
                    b = b0 if h < 2 else b1
                    c0 = ((h % 2) * 2 + mt) * 128
                    mm(bank(b)[:, c0:c0 + 128], kmT[l][:, h // 2, mt * 128:(mt + 1) * 128], qmTz[:, h, :], True, True,
                       [t_km[l], t_qmTz], [pb[b]])
            for hb, b in enumerate((b0, b1)):
                act(ptm[:, hb * 512:(hb + 1) * 512], bank(b), AF.Exp, [pb[b]], [t_ptm], scale=SCALE)
            for h in range(4):
                for mt in range(2):
                    c0 = (h * 2 + mt) * 128
                    mm(bank(pvb)[:, h * 65:(h + 1) * 65], ptm[:, c0:c0 + 128], vm[l][:, mt, h, :], mt == 0, mt == 1,
                       [t_ptm, t_vm[l]], [pb[pvb]])
            rec = stats[:, ST_REC + 12:ST_REC + 16]
            pv3 = bank(pvb)[:, 0:260].rearrange("p (h e) -> p h e", e=65)
            P.emit(DVE, lambda e: e.reciprocal(out=rec.unsqueeze(2), in_=pv3[:, :, 64:65]), [pb[pvb]], [t_srecm])
            tt(DVE, dst.rearrange("p (h d) -> p h d", d=64), pv3[:, :, 0:64],
               rec.unsqueeze(2).broadcast_to([128, 4, 64]), ALU.mult, [pb[pvb], t_srecm], [t_cat])

        def out_proj(ti, catT, t_catT, pool):
            for hf in range(2):
                b = pool.next()
                for kc in range(KC):
                    mm(bank(b), catT[:, kc, :], wo[:, kc, hf * 512:(hf + 1) * 512], kc == 0, kc == KC - 1,
                       [t_catT, t_wo[hf]], [pb[b]])
                xv = x_sb[:, ti, hf * 512:(hf + 1) * 512]
                tt(DVE, xv, bank(b), xv, ALU.add, [pb[b], x_t[ti]], [x_t[ti]])
                pool.release(b)

        def mixer_a(s, l, pre_bias=None):
            P.barrier()
            o = R10 + 16384
            hT = [view(o + 2048 * i, 2048, BF16, "p (c t) -> p c t", c=8) for i in range(2)]; o += 4096
            qf = view(o, 3072, F32); o += 3072
            kf = view(o, 3072, F32); o += 3072
            qsq = view(o, 3072, F32); o += 3072
            qmf = view(o, 1024, F32); o += 1024
            qTz = [view(o + 3072 * i, 3072, BF16, "p (h t) -> p h t", h=12) for i in range(2)]; o += 6144
            qmTz = [view(o + 1024 * i, 1024, BF16, "p (h t) -> p h t", h=4) for i in range(2)]; o += 2048
            PT = [view(o + 2560 * i, 2560, BF16) for i in range(2)]; o += 5120
            cat = view(o, 2048, BF16); o += 2048
            catT = view(o, 2048, BF16, "p (c t) -> p c t", c=8); o += 2048
            assert o <= R10 + 49152, o - R10
            qb = qsq.bitcast(BF16)
            kring = view(R20, 9216, BF16, "p (s c t) -> p s c t", s=6, c=6)
            vring = view(R20 + 9216, 9360, BF16, "p (s h e) -> p s h e", s=6, h=12)
            biasT = view(R20 + 18576, 12288, F32, "p (c j q) -> p c j q", c=6, j=4)
            t_hT = [Tile(f"hT{i}") for i in range(2)]
            t_qf, t_kf, t_qsq, t_qmf = Tile("qf"), Tile("kf"), Tile("qsq"), Tile("qmf")
            t_qTz = [Tile(f"qTz{i}") for i in range(2)]
            t_qmTz = [Tile(f"qmTz{i}") for i in range(2)]
            t_PT = [Tile(f"PT{i}") for i in range(2)]
            t_cat, t_catT = Tile("cat"), Tile("catT")
            t_kr = [Tile(f"kr{i}") for i in range(6)]
            t_vr = [Tile(f"vr{i}") for i in range(6)]
            t_bias = pre_bias if pre_bias is not None else Tile("bias")

            if pre_bias is None:
                P.dma(POOL, gbc, bct_d[B_N1[l]], writes=[t_gbc], max_dma_last_dim=4096)
                P.dma(POOL, biasT.rearrange("p c j q -> p (c j q)"), bias_d, writes=[t_bias], max_dma_last_dim=4096)
            load_wo(l)
            for i in range(2):
                mset(POOL, qTz[i].rearrange("p h t -> p (h t)"), 0.0, [t_qTz[i]])
                mset(POOL, qmTz[i].rearrange("p h t -> p (h t)"), 0.0, [t_qmTz[i]])
            for sl in range(6):
                mset(POOL, vring[:, sl, :, 64:65], 1.0, [t_vr[sl]])
            def bias_init():
                for h in range(12):
                    bv = biasT[:, h // 2, :, :].rearrange("p (d a) q -> p d a q", a=2)[:, :, h % 2, :]
                    ts(DVE, bv, bv, col(C_BFAR + h), None, ALU.subtract, None, [t_bias, t_cst], [t_bias])
                mset(DVE, biasT[64:128, :, 0:2, 0:64], NEG, [t_bias])

            gpool = Ring([0, 1, 2, 4])
            BNS, BFS, B4, PVB = (3, 3), (5, 5), 6, 7

            def stage1(i):
                yield from norm_transpose_g(x_sb[:, i, :], [x_t[i]], i % 16, hT[i % 2], [t_hT[i % 2]], gpool, evac=DVE)

            def stage2(i):
                hb = i % 2
                sl = i % 6
                qT4 = qTz[hb].rearrange("p (a b) t -> p a b t", b=2)
                qmT4 = qmTz[hb].rearrange("p (a b) t -> p a b t", b=2)

                def inproj(gi):
                    c0, n, pieces = WA_GROUPS_A[gi]
                    b = gpool.next()
                    for kc in range(KC):
                        mm(bank(b)[:, 0:n], hT[hb][:, kc, :], wa_A[:, kc, c0:c0 + n], kc == 0, kc == KC - 1,
                           [t_hT[hb]] + t_wa[gi][0:len(pieces)], [pb[b]])
                    return b

                hs = stats[:, ST_H:ST_H + 28]
                hl = stats[:, ST_HL:ST_HL + 28]
                hr = stats[:, ST_HR:ST_HR + 28]

                def sq_red(src, t_src, nheads, c0):
                    w = nheads * 64
                    act(qsq[:, 0:w], src[:, 0:w], AF.Square, [t_src], [t_qsq])
                    P.emit(DVE, lambda e: e.tensor_reduce(
                        out=stats[:, ST_H + c0:ST_H + c0 + nheads],
                        in_=qsq[:, 0:w].rearrange("p (h d) -> p h d", d=64), axis=AX.X, op=ALU.add),
                        [t_qsq], [t_shead])

                b = inproj(0)
                act(qf[:, 0:512], bank(b), AF.Copy, [pb[b]], [t_qf])
                gpool.release(b)
                yield
                b = inproj(1)
                act(qf[:, 512:768], bank(b)[:, 0:256], AF.Copy, [pb[b]], [t_qf])
                act(qmf, bank(b)[:, 256:512], AF.Copy, [pb[b]], [t_qmf])
                gpool.release(b)
                yield
                sq_red(qf, t_qf, 12, 0)
                yield
                b = inproj(2)
                act(kf[:, 0:512], bank(b), AF.Copy, [pb[b]], [t_kf])
                gpool.release(b)
                yield
                sq_red(qmf, t_qmf, 4, 24)
                yield
                b = inproj(3)
                act(kf[:, 512:768], bank(b)[:, 0:256], AF.Copy, [pb[b]], [t_kf])
                gpool.release(b)
                yield
                sq_red(kf, t_kf, 12, 12)
                yield
                b = inproj(4)
                cp(DVE, vring[:, sl, 0:8, 0:64], bank(b).rearrange("p (h d) -> p h d", d=64), [pb[b]], [t_vr[sl]])
                gpool.release(b)
                yield
                act(hl, hs, AF.Ln, [t_shead, t_cst], [t_shead], scale=1.0 / 64, bias=col(C_EPS))
                act(hr, hl, AF.Exp, [t_shead], [t_shead], scale=-0.5)
                yield
                b = inproj(5)
                cp(DVE, vring[:, sl, 8:12, 0:64], bank(b)[:, 0:256].rearrange("p (h d) -> p h d", d=64),
                   [pb[b]], [t_vr[sl]])
                gpool.release(b)
                yield
                tt(DVE, qb[:, 0:768].rearrange("p (h d) -> p h d", d=64), qf.rearrange("p (h d) -> p h d", d=64),
                   stats[:, ST_HR:ST_HR + 12].unsqueeze(2).broadcast_to([128, 12, 64]), ALU.mult,
                   [t_qf, t_shead], [t_qsq])
                yield
                b = gpool.next()
                pbf = bank(b).bitcast(BF16)
                for pp in range(6):
                    tr(pbf[:, pp * 128:(pp + 1) * 128], qb[:, pp * 128:(pp + 1) * 128], ident_b, [t_qsq, t_ident], [pb[b]])
                for hp in range(2):
                    cp(DVE, qT4[hp * 64:(hp + 1) * 64, :, hp, :],
                       pbf[hp * 64:(hp + 1) * 64, 0:768].rearrange("p (k t) -> p k t", k=6), [pb[b]], [t_qTz[hb]])
                gpool.release(b)
                yield
                tt(DVE, qb[:, 0:256].rearrange("p (h d) -> p h d", d=64), qmf.rearrange("p (h d) -> p h d", d=64),
                   stats[:, ST_HR + 24:ST_HR + 28].unsqueeze(2).broadcast_to([128, 4, 64]), ALU.mult,
                   [t_qmf, t_shead], [t_qsq])
                yield
                b = gpool.next()
                pbf = bank(b).bitcast(BF16)
                for pp in range(2):
                    tr(pbf[:, pp * 128:(pp + 1) * 128], qb[:, pp * 128:(pp + 1) * 128], ident_b, [t_qsq, t_ident], [pb[b]])
                for hp in range(2):
                    cp(DVE, qmT4[hp * 64:(hp + 1) * 64, :, hp, :],
                       pbf[hp * 64:(hp + 1) * 64, 0:256].rearrange("p (k t) -> p k t", k=2), [pb[b]], [t_qmTz[hb]])
                gpool.release(b)
                yield
                k3 = kf.rearrange("p (h d) -> p h d", d=64)
                tt(DVE, k3, k3, stats[:, ST_HR + 12:ST_HR + 24].unsqueeze(2).broadcast_to([128, 12, 64]), ALU.mult,
                   [t_kf, t_shead], [t_kf])
                yield
                for (p0, np_) in ((0, 4), (4, 2)):
                    b = gpool.next()
                    for k in range(np_):
                        pp = p0 + k
                        tr(bank(b)[:, k * 128:(k + 1) * 128], kf[:, pp * 128:(pp + 1) * 128], ident_f,
                           [t_kf, t_ident], [pb[b]])
                    act(kring[:, sl, p0:p0 + np_, :], bank(b)[:, 0:np_ * 128].rearrange("p (k t) -> p k t", k=np_),
                        AF.Copy, [pb[b], t_cst], [t_kr[sl]], scale=col(C_GA))
                    gpool.release(b)
                    yield

            def scores(i, p, hb):
                dmax = min(i, 4)
                pi = p % 2
                BN, BF = BNS[pi], BFS[pi]
                q2 = qTz[hb][:, 2 * p:2 * p + 2, :]
                for dl in range(dmax + 1):
                    sl = (i - dl) % 6
                    if dl < 2:
                        b, c0 = BN, dl * 256
                    elif dl < 4:
                        b, c0 = BF, (dl - 2) * 256
                    else:
                        b, c0 = B4, 0
                    mm(bank(b)[:, c0:c0 + 256], kring[:, sl, p, :], q2, True, True, [t_kr[sl], t_qTz[hb]], [pb[b]])
                nd = min(dmax + 1, 2)
                nv = bank(BN)[:, 0:nd * 256]
                bv = biasT[:, p, 0:nd * 2, :].rearrange("p j q -> p (j q)")
                stt(nv, nv, SCALE, bv, ALU.mult, ALU.add, [pb[BN], t_bias], [pb[BN]])
                if dmax >= 4:
                    v4 = bank(B4)[:, 0:256].rearrange("p (a q) -> p a q", a=2)
                    ts(DVE, v4[0:64, :, 64:128], v4[0:64, :, 64:128], NEG, None, ALU.add, None, [pb[B4]], [pb[B4]])
                act(PT[pi][:, 0:nd * 256], nv, AF.Exp, [pb[BN]], [t_PT[pi]])
                if dmax >= 4:
                    fv = ps[:, 5:7, :].rearrange("p b n -> p (b n)")[:, 0:768]
                    act(PT[pi][:, 512:1280], fv, AF.Exp, [pb[BF], pb[B4]], [t_PT[pi]], scale=SCALE)
                elif dmax >= 2:
                    nf = dmax - 1
                    act(PT[pi][:, 512:512 + nf * 256], bank(BF)[:, 0:nf * 256], AF.Exp, [pb[BF]], [t_PT[pi]], scale=SCALE)

            def pv(i, p, hb):
                dmax = min(i, 4)
                pi = p % 2
                fo, do_ = 512, 1024
                for hp in range(2):
                    h = 2 * p + hp
                    oc = (h % 6) * 65
                    for dl in range(dmax + 1):
                        sl = (i - dl) % 6
                        if dl < 2:
                            c0 = (dl * 2 + hp) * 128
                        elif dl < 4:
                            c0 = fo + ((dl - 2) * 2 + hp) * 128
                        else:
                            c0 = do_ + hp * 128
                        mm(bank(PVB)[:, oc:oc + 65], PT[pi][:, c0:c0 + 128], vring[:, sl, h, :], dl == 0, dl == dmax,
                           [t_PT[pi], t_vr[sl]], [pb[PVB]])
                if p in (2, 5):
                    rec = stats[:, ST_REC + (p // 3) * 6:ST_REC + (p // 3) * 6 + 6]
                    pv3 = bank(PVB)[:, 0:390].rearrange("p (h e) -> p h e", e=65)
                    P.emit(DVE, lambda e: e.reciprocal(out=rec.unsqueeze(2), in_=pv3[:, :, 64:65]), [pb[PVB]], [t_srec])
                    cc = (p // 3) * 384
                    tt(DVE, cat[:, cc:cc + 384].rearrange("p (h d) -> p h d", d=64), pv3[:, :, 0:64],
                       rec.unsqueeze(2).broadcast_to([128, 6, 64]), ALU.mult, [pb[PVB], t_srec], [t_cat])

            def stage3(i):
                hb = i % 2
                for p in range(7):
                    if p < 6:
                        scores(i, p, hb)
                        yield
                    if p >= 1:
                        pv(i, p - 1, hb)
                        yield
                pvb = gpool.next()
                mem_attn(l, qmTz[hb], t_qmTz[hb], cat[:, 768:1024], t_cat, BNS[0], BFS[0], pvb)
                gpool.release(pvb)
                yield
                b = gpool.next()
                pbf = bank(b).bitcast(BF16)
                for kc in range(KC):
                    tr(pbf[:, kc * 128:(kc + 1) * 128], cat[:, kc * 128:(kc + 1) * 128], ident_b,
                       [t_cat, t_ident], [pb[b]])
                act(catT, pbf.rearrange("p (c t) -> p c t", c=8), AF.Copy, [pb[b]], [t_catT])
                gpool.release(b)
                if DBG_TILE is not None and i == DBG_TILE and s == 0:
                    P.last_dma_events.append(P.dma(SP, dbg_d, cat, reads=[t_cat]))
                yield
                out_proj(i, catT, t_catT, gpool)
                yield

            for it in range(NT + 2):
                gens, tot = [], []
                if it == 2:
                    bias_init()
                if 2 <= it:
                    gens.append(stage3(it - 2)); tot.append(16)
                if 1 <= it <= NT:
                    gens.append(stage2(it - 1)); tot.append(18)
                if it < NT:
                    gens.append(stage1(it)); tot.append(4)
                interleave(gens, tot)

        def mixer_b(s, l):
            P.barrier()
            load_wo(l)
            hT = view(WA0 + 28672, 8192, BF16, "p (c t) -> p c t", c=8)
            o = R10 + 16384
            hglu = view(o, 6504, BF16, "p (j t) -> p j t", j=6); o += 6528
            sig = [view(o + 2048 * i, 2048, F32) for i in range(2)]; o += 4096
            yn = view(o, 3072, F32); o += 3072
            qmf = view(o, 1024, F32); o += 1024
            qsq = view(o, 1024, F32); o += 1024
            qmTz = [view(o + 1024 * i, 1024, BF16, "p (h t) -> p h t", h=4) for i in range(8)]; o += 8192
            catm = [view(o + 512 * i, 512, BF16) for i in range(2)]; o += 1024
            catT = [view(o + 2048 * i, 2048, BF16, "p (c t) -> p c t", c=8) for i in range(2)]; o += 4096
            assert o <= R10 + 49152, o - R10
            qb = qsq.bitcast(BF16)
            yT0 = view(R20, 12288, F32, "p (j t) -> p j t", j=6)
            diag = [view(R20 + 12288 + 7936 * i, 7936, BF16, "p (k c) -> p k c", k=31) for i in range(2)]
            extra = [WA0 + 36864, WA0 + 38912, R20 + 28160, R20 + 30208, G0 + 22304, G0 + 1792 + 4096 + 2048]
            yT = [[yT0[:, j, :] for j in range(6)], [view(extra[j], 2048, F32) for j in range(6)]]
            t_hT = [Tile(f"bhT{i}") for i in range(4)]
            t_hglu = [Tile(f"hglu_{j}") for j in range(6)]
            t_sig = [Tile(f"sig{i}") for i in range(2)]
            t_yn, t_qmf, t_qsq = Tile("yn"), Tile("bqmf"), Tile("bqsq")
            t_qmTz = [Tile(f"bqmTz{i}") for i in range(8)]
            t_catm = [Tile(f"catm{i}") for i in range(2)]
            t_catT = [Tile(f"bcatT{i}") for i in range(2)]
            t_yT = [[Tile(f"yT{i}_{j}") for j in range(6)] for i in range(2)]
            t_diag = [Tile(f"diag{i}") for i in range(2)]
            P.dma(SP, gbc, bct_d[B_N1[l]], writes=[t_gbc])
            for i in range(8):
                mset(POOL, qmTz[i].rearrange("p h t -> p (h t)"), 0.0, [t_qmTz[i]])
            for j in range(6):
                mset(POOL, hglu[:, j, 0:30], 0.0, [t_hglu[j]])
            gpool = Ring([0, 1, 2, 3, 4])
            convw = cst[:, C_CONVW:C_CONVW + 186].rearrange("p (j k) -> p j k", j=6)
            NB = S // 512

            def stage_x(c):
                cb = c % 2
                for t4 in range(4):
                    ti = 4 * c + t4
                    yield from norm_transpose_g(x_sb[:, ti, :], [x_t[ti]], ti, hT[:, :, t4 * 128:(t4 + 1) * 128],
                                                [t_hT[t4]], gpool, evac=(DVE if t4 % 2 else ACT))
                for j in range(6):
                    si = j % 2
                    bg = gpool.next()
                    ng = 6 + j
                    for kc in range(KC):
                        mm(bank(bg), wa_B[:, kc, ng * 128:(ng + 1) * 128], hT[:, kc, :], kc == 0, kc == KC - 1,
                           t_hT + [t_wa[ng // 4][0]], [pb[bg]])
                    act(sig[si], bank(bg), AF.Sigmoid, [pb[bg], t_cst], [t_sig[si]], bias=col(C_BIN + ng))
                    gpool.release(bg)
                    yield
                    ba = gpool.next()
                    for kc in range(KC):
                        mm(bank(ba), wa_B[:, kc, j * 128:(j + 1) * 128], hT[:, kc, :], kc == 0, kc == KC - 1,
                           t_hT + [t_wa[j // 4][0]], [pb[ba]])
                    stt(hglu[:, j, 30:542], bank(ba), col(C_BIN + j), sig[si], ALU.add, ALU.mult,
                        [pb[ba], t_cst, t_sig[si]], [t_hglu[j]])
                    gpool.release(ba)
                    yield

            def stage_yconv(c):
                cb = c % 2
                for j in range(6):
                    db = j % 2
                    P.dma(POOL, diag[db].rearrange("p k c -> p (k c)"), diag_d[j], writes=[t_diag[db]],
                          max_dma_last_dim=4096)
                    b = gpool.next()
                    for k in range(31):
                        mm(bank(b), diag[db][:, k, :], hglu[:, j, k:k + 512], k == 0, k == 30,
                           [t_diag[db], t_hglu[j]], [pb[b]])
                        if k % 8 == 7:
                            yield
                    act(yT[cb][j], bank(b), AF.Identity, [pb[b], t_cst], [t_yT[cb][j]], bias=col(C_CONVB + j))
                    gpool.release(b)
                    cp(DVE, hglu[:, j, 0:30], hglu[:, j, 512:542], [t_hglu[j]], [t_hglu[j]])
                    yield

            def stage_yq(c):
                for t4 in range(4):
                    bq = gpool.next()
                    for kc in range(KC):
                        mm(bank(bq)[:, 0:256], hT[:, kc, t4 * 128:(t4 + 1) * 128], wa_B[:, kc, 1536:1792],
                           kc == 0, kc == KC - 1, [t_hT[t4], t_wa[3][0]], [pb[bq]])
                    tt(DVE, qmf, bank(bq)[:, 0:256], qmb, ALU.add, [pb[bq], t_qmb], [t_qmf])
                    gpool.release(bq)
                    yield
                    act(qsq, qmf, AF.Square, [t_qmf], [t_qsq])
                    yield
                    hs = stats[:, ST_H:ST_H + 4]
                    hl = stats[:, ST_HL:ST_HL + 4]
                    hr = stats[:, ST_HR:ST_HR + 4]
                    P.emit(DVE, lambda e, hs=hs: e.tensor_reduce(out=hs, in_=qsq.rearrange("p (h d) -> p h d", d=64),
                                                                 axis=AX.X, op=ALU.add), [t_qsq], [t_shead])
                    yield
                    act(hl, hs, AF.Ln, [t_shead, t_cst], [t_shead], scale=1.0 / 64, bias=col(C_EPS))
                    act(hr, hl, AF.Exp, [t_shead], [t_shead], scale=-0.5)
                    yield
                    tt(DVE, qb[:, 0:256].rearrange("p (h d) -> p h d", d=64), qmf.rearrange("p (h d) -> p h d", d=64),
                       hr.unsqueeze(2).broadcast_to([128, 4, 64]), ALU.mult, [t_qmf, t_shead], [t_qsq])
                    yield
                    b = gpool.next()
                    pbf = bank(b).bitcast(BF16)
                    for pp in range(2):
                        tr(pbf[:, pp * 128:(pp + 1) * 128], qb[:, pp * 128:(pp + 1) * 128], ident_b,
                           [t_qsq, t_ident], [pb[b]])
                    qi = (c % 2) * 4 + t4
                    qmT4 = qmTz[qi].rearrange("p (a b) t -> p a b t", b=2)
                    for hp in range(2):
                        cp(DVE, qmT4[hp * 64:(hp + 1) * 64, :, hp, :],
                           pbf[hp * 64:(hp + 1) * 64, 0:256].rearrange("p (k t) -> p k t", k=2), [pb[b]], [t_qmTz[qi]])
                    gpool.release(b)
                    yield

            yn2 = view(R10 + 16384 + 6528 + 4096 + 3072 + 1024 + 1024 + 8192 + 1024 + 4096, 3072, F32)
            yns = [yn, yn2]
            t_yns = [t_yn, Tile("yn2")]
            t_sbns = [t_sbn, Tile("sbn2")]

            def z_tile(c, t4):
                ti = 4 * c + t4
                tb = ti % 2
                cb = c % 2
                qi = cb * 4 + t4
                yb, t_yb, t_sb = yns[tb], t_yns[tb], t_sbns[tb]
                sb0 = ST_BN + 16 * tb
                b1, b2_ = gpool.next(), gpool.next()
                for j in range(6):
                    bb = b1 if j < 4 else b2_
                    tr(bank(bb)[:, (j % 4) * 128:(j % 4 + 1) * 128], yT[cb][j][:, t4 * 128:(t4 + 1) * 128], ident_f,
                       [t_yT[cb][j], t_ident], [pb[bb]])
                mv = stats[:, sb0 + 12:sb0 + 14]
                lnr = stats[:, sb0 + 14:sb0 + 16]
                P.emit(DVE, lambda e: e.bn_stats(out=stats[:, sb0:sb0 + 6], in_=bank(b1)), [pb[b1]], [t_sb])
                P.emit(DVE, lambda e: e.bn_stats(out=stats[:, sb0 + 6:sb0 + 12], in_=bank(b2_)[:, 0:256]), [pb[b2_]], [t_sb])
                P.emit(DVE, lambda e: e.bn_aggr(out=stats[:, sb0 + 12:sb0 + 14], in_=stats[:, sb0:sb0 + 12]), [t_sb], [t_sb])
                act(lnr[:, 0:1], mv[:, 1:2], AF.Ln, [t_sb, t_cst], [t_sb], bias=col(C_EPS))
                act(lnr[:, 1:2], lnr[:, 0:1], AF.Exp, [t_sb], [t_sb], scale=-0.5)
                ts(DVE, yb[:, 0:512], bank(b1), mv[:, 0:1], lnr[:, 1:2], ALU.subtract, ALU.mult, [pb[b1], t_sb], [t_yb])
                ts(DVE, yb[:, 512:768], bank(b2_)[:, 0:256], mv[:, 0:1], lnr[:, 1:2], ALU.subtract, ALU.mult,
                   [pb[b2_], t_sb], [t_yb])
                gpool.release(b1); gpool.release(b2_)
                yield
                b3, b4 = gpool.next(), gpool.next()
                for j in range(6):
                    bb = b3 if j < 4 else b4
                    tr(bank(bb)[:, (j % 4) * 128:(j % 4 + 1) * 128], yb[:, j * 128:(j + 1) * 128], ident_f,
                       [t_yb, t_ident], [pb[bb]])
                for j in range(6):
                    bb = b3 if j < 4 else b4
                    act(catT[tb][:, j, :], bank(bb)[:, (j % 4) * 128:(j % 4 + 1) * 128], AF.Silu,
                        [pb[bb], t_cst], [t_catT[tb]], scale=col(C_LNG + j), bias=col(C_LNB + j))
                gpool.release(b3); gpool.release(b4)
                yield
                mem_attn(l, qmTz[qi], t_qmTz[qi], catm[tb], t_catm[tb], 5, 6, 7)
                yield
                b = gpool.next()
                pbf = bank(b).bitcast(BF16)
                for kc in range(2):
                    tr(pbf[:, kc * 128:(kc + 1) * 128], catm[tb][:, kc * 128:(kc + 1) * 128], ident_b,
                       [t_catm[tb], t_ident], [pb[b]])
                act(catT[tb][:, 6:8, :], pbf[:, 0:256].rearrange("p (c t) -> p c t", c=2), AF.Copy,
                    [pb[b]], [t_catT[tb]])
                gpool.release(b)
                yield
                out_proj(ti, catT[tb], t_catT[tb], gpool)
                yield

            def stage_z(c):
                for t0 in (0, 2):
                    sub = [z_tile(c, t0), z_tile(c, t0 + 1)]
                    alive = [True, True]
                    while any(alive):
                        for k in range(2):
                            if alive[k]:
                                try:
                                    next(sub[k])
                                    yield
                                except StopIteration:
                                    alive[k] = False

            def chain(c):
                yield from stage_x(c)
                sub = [stage_yconv(c), stage_yq(c)]
                alive = [True, True]
                while any(alive):
                    for k in range(2):
                        if alive[k]:
                            try:
                                next(sub[k])
                                yield
                            except StopIteration:
                                alive[k] = False

            for c in range(NB + 1):
                gens, tot = [], []
                if c >= 1:
                    gens.append(stage_z(c - 1)); tot.append(20)
                if c < NB:
                    gens.append(chain(c)); tot.append(29 + 48)
                interleave(gens, tot)

        seq_layers = [(s, l) for s in range(nseq) for l in layers]
        for idx, (s, l) in enumerate(seq_layers):
            if l == layers[0]:
                if idx > 0:
                    P.barrier()
                mem_kv(s, mem_pre.get(s))
                pre_bias = None
                if idx == 0 and l == 0:
                    pre_bias = Tile("bias")
                    P.dma(SP, gbc, bct_d[B_N1[l]], writes=[t_gbc])
                    P.dma(SP, view(R20 + 18576, 12288, F32), bias_d, writes=[pre_bias])
                if idx == 0:
                    load_wa(layers[0])
                if idx == 0 or SKIP_FFN:
                    for i in range(NT):
                        P.dma(SP, x_sb[:, i, :], x_d[s, i * 128:(i + 1) * 128, :], writes=[x_t[i]])
            if l == 0:
                mixer_a(s, l, pre_bias if (idx == 0) else None)
            else:
                mixer_b(s, l)
            nxt = seq_layers[idx + 1][1] if idx + 1 < len(seq_layers) else None
            if SKIP_FFN:
                if nxt is not None:
                    load_wa(nxt)
                P.barrier()
                for ti in range(NT):
                    ev = P.dma(SP, out_d[s, ti * 128:(ti + 1) * 128, :], x_sb[:, ti, :], reads=[x_t[ti]])
                    P.last_dma_events.append(ev)
            else:
                ffn_phase(s, l, is_last=(l == layers[-1]), next_wa=nxt)
        P.wait_events(SP, P.last_dma_events)
        P.lower(block)
    return nc


def _pack_consts(inp):
    cst = np.zeros((128, NCST), np.float32)
    p = np.arange(128)
    cst[:, C_AQG] = inp["a_q_g"][0][p % 64]
    cst[:, C_AKG] = inp["a_k_g"][0][p % 64]
    cst[:, C_MQG0] = inp["mq_g"][0][p % 64]
    cst[:, C_MKG0] = inp["mk_g"][0][p % 64]
    cst[:, C_MQG1] = inp["mq_g"][1][p % 64]
    cst[:, C_MKG1] = inp["mk_g"][1][p % 64]
    cst[:, C_BIN:C_BIN + 12] = inp["b_b_in"][0][:1536].reshape(12, 128).T
    cst[:, C_CONVB:C_CONVB + 6] = inp["b_conv_b"][0].reshape(6, 128).T
    cst[:, C_LNG:C_LNG + 6] = inp["b_ln_g"][0].reshape(6, 128).T
    cst[:, C_LNB:C_LNB + 6] = inp["b_ln_b"][0].reshape(6, 128).T
    cst[:, C_BFAR:C_BFAR + 12] = inp["a_rel_bias"][0][:, 191][None, :]
    cst[:, C_EPS] = EPS
    cw = inp["b_conv_w"][0]
    cst[:, C_CONVW:C_CONVW + 186] = cw.reshape(31, 6, 128).transpose(2, 1, 0).reshape(128, 186)
    bct = np.zeros((6, 128, D), np.float32)
    bct[0] = inp["norm1_g"][0][None, :]
    bct[1] = inp["norm2_g"][0][None, :]
    bct[2] = inp["norm1_g"][1][None, :]
    bct[3] = inp["norm2_g"][1][None, :]
    bct[4] = inp["mem_norm_g"][0][None, :]
    bct[5] = inp["mem_norm_g"][1][None, :]
    qmb = np.ascontiguousarray(np.broadcast_to(inp["b_b_in"][0][1536:1792][None, :], (128, 256))).astype(np.float32)
    rb = inp["a_rel_bias"][0]
    k = np.arange(128)[:, None]
    q = np.arange(128)[None, :]
    bias = np.zeros((128, 6, 4, 128), np.float32)
    for h in range(12):
        for dl in range(2):
            idx = np.clip(q - k + 128 * dl, -63, 128) + 63
            bias[:, h // 2, dl * 2 + (h % 2), :] = rb[h][idx]
    dg = np.zeros((6, 128, 31, 128), np.float32)
    ar = np.arange(128)
    for j in range(6):
        dg[j, ar, :, ar] = cw[:, j * 128:(j + 1) * 128].T
    return cst, bct, qmb, bias.reshape(128, 6 * 4 * 128), dg.reshape(6, 128, 31 * 128)


_NC_CACHE = {}


def kernel(**inputs):
    inp = {k: np.ascontiguousarray(np.asarray(v, dtype=np.float32)) for k, v in inputs.items()}
    cst, bct, qmb, bias, dg = _pack_consts(inp)
    key = "full"
    if key not in _NC_CACHE:
        _NC_CACHE[key] = build_program()
    nc = _NC_CACHE[key]
    in_maps = []
    for c in range(8):
        in_maps.append({
            "x": inp["x"][2 * c:2 * c + 2], "mem": inp["mem"][2 * c:2 * c + 2],
            "a_w_in": inp["a_w_in"], "b_w_in": inp["b_w_in"], "w_mem_kv": inp["w_mem_kv"],
            "w_out": inp["w_out"], "w_gate": inp["w_gate"], "w_up": inp["w_up"], "w_down": inp["w_down"],
            "cst": cst, "bct": bct, "qmb": qmb, "biasT": bias, "convdiag": dg,
        })
    res = run_bass_kernel_spmd(nc, in_maps, core_ids=list(range(8)))
    out = np.concatenate([r["out"] for r in res.results], axis=0)
    return out.astype(np.float32)
```

```python
import numpy as np
from contextlib import ExitStack
import concourse.bass as bass
import concourse.mybir as mybir
from concourse.bass_utils import run_bass_kernel_spmd

F32 = mybir.dt.float32
BF16 = mybir.dt.bfloat16
U8 = mybir.dt.uint8
AF = mybir.ActivationFunctionType
ALU = mybir.AluOpType
AX = mybir.AxisListType

PE, ACT, DVE, POOL, SP = range(5)

S = 2048
D = 1024
NT = 16
KC = 8
DFF = 2816
NFC = 22
FG = 4
NSEQ = 2
EPS = 1e-6
SCALE = 0.125
NEG = -30000.0
POOL_THROTTLE = False


class Tile:
    __slots__ = ("name", "excl", "w", "r", "dr")

    def __init__(self, name, excl=False):
        self.name = name
        self.excl = excl
        self.w = None
        self.r = {}
        self.dr = []


class Op:
    __slots__ = ("fn", "waits", "signal", "sigval", "dma")

    def __init__(self, fn):
        self.fn = fn
        self.waits = []
        self.signal = False
        self.sigval = 0
        self.dma = None


class Prog:
    NDS = 24

    def __init__(self, nc, es):
        self.nc = nc
        self.ops = [[] for _ in range(5)]
        self.waited = [dict() for _ in range(5)]
        names = ["pe", "act", "dve", "pool", "sp"]
        self.esem = [es.enter_context(nc.semaphore("e_" + n)) for n in names]
        self.dsem = {}
        self.dcount = {}
        self.dnext = {}
        self.nds = {POOL: 72, SP: 24}
        for q in (POOL, SP):
            self.dsem[q] = [es.enter_context(nc.semaphore(f"d{q}_{i}")) for i in range(self.nds[q])]
            self.dcount[q] = [0] * self.nds[q]
            self.dnext[q] = 0
        self.last_dma_events = []

    def _deps(self, eng, reads, writes, is_dma):
        deps = []
        for t in reads:
            if t.w is not None:
                deps.append(t.w)
            if t.excl:
                for e2, s in t.r.items():
                    if e2 != eng:
                        deps.append(("e", e2, s))
        for t in writes:
            if t.w is not None:
                if is_dma or not (t.w[0] == "e" and t.w[1] == eng):
                    deps.append(t.w)
            for e2, s in t.r.items():
                if is_dma or e2 != eng:
                    deps.append(("e", e2, s))
            deps.extend(t.dr)
        return deps

    def _add_waits(self, eng, op, deps):
        wd = self.waited[eng]
        for d in deps:
            if d[0] == "e":
                key = ("e", d[1])
                if wd.get(key, -1) >= d[2]:
                    continue
                wd[key] = d[2]
                self.ops[d[1]][d[2]].signal = True
                op.waits.append(d)
            else:
                key = ("d", d[1], d[2])
                if wd.get(key, 0) >= d[3]:
                    continue
                wd[key] = d[3]
                op.waits.append(d)

    def emit(self, eng, fn, reads=(), writes=()):
        op = Op(fn)
        self._add_waits(eng, op, self._deps(eng, reads, writes, False))
        seq = len(self.ops[eng])
        self.ops[eng].append(op)
        for t in reads:
            t.r[eng] = seq
        for t in writes:
            t.w = ("e", eng, seq)
            t.r = {}
            t.dr = []
        return op

    def dma(self, q, out, in_, reads=(), writes=(), **kw):
        op = Op(lambda e: e.dma_start(out=out, in_=in_, **kw))
        self._add_waits(q, op, self._deps(q, reads, writes, True))
        if q == POOL:
            self.pool_evs = getattr(self, "pool_evs", [])
            if POOL_THROTTLE and len(self.pool_evs) >= 6:
                self._add_waits(q, op, [self.pool_evs[-6]])
        i = self.dnext[q]
        self.dnext[q] = (i + 1) % self.nds[q]
        if self.dcount[q][i] > 0:
            self._add_waits(q, op, [("d", q, i, 16 * self.dcount[q][i])])
        self.dcount[q][i] += 1
        ev = ("d", q, i, 16 * self.dcount[q][i])
        op.dma = (q, i)
        if q == POOL:
            self.pool_evs.append(ev)
        self.ops[q].append(op)
        for t in writes:
            t.w = ev
            t.r = {}
            t.dr = []
        for t in reads:
            t.dr.append(ev)
        return ev

    def wait_events(self, eng, evs):
        op = Op(None)
        self._add_waits(eng, op, list(evs))
        self.ops[eng].append(op)

    def barrier(self):
        lasts = []
        for e in range(5):
            if self.ops[e]:
                s = len(self.ops[e]) - 1
                while s >= 0 and (self.ops[e][s].fn is None or self.ops[e][s].dma is not None):
                    s -= 1
                if s >= 0:
                    lasts.append(("e", e, s))
        for c in range(5):
            self.wait_events(c, [d for d in lasts if d[1] != c])

    def lower(self, block):
        for e in range(5):
            c = 0
            for op in self.ops[e]:
                if op.signal:
                    c += 1
                    op.sigval = c

        def mk(e):
            def body(eng):
                for op in self.ops[e]:
                    for w in op.waits:
                        if w[0] == "e":
                            eng.wait_ge(self.esem[w[1]], self.ops[w[1]][w[2]].sigval)
                        else:
                            eng.wait_ge(self.dsem[w[1]][w[2]], w[3])
                    if op.fn is None:
                        continue
                    inst = op.fn(eng)
                    if op.dma is not None:
                        inst.then_inc(self.dsem[op.dma[0]][op.dma[1]], 16)
                    elif op.signal:
                        inst.then_inc(self.esem[e], 1)
            return body

        block.tensor(mk(PE))
        block.scalar(mk(ACT))
        block.vector(mk(DVE))
        block.gpsimd(mk(POOL))
        block.sync(mk(SP))


def interleave(gens, totals=None):
    gens = list(gens)
    n = len(gens)
    if totals is None:
        totals = [1] * n
    done = [0] * n
    alive = [True] * n
    while any(alive):
        best = None
        for k in range(n):
            if alive[k]:
                frac = (done[k] + 1) / float(totals[k])
                if best is None or frac < best[0]:
                    best = (frac, k)
        k = best[1]
        try:
            next(gens[k])
            done[k] += 1
        except StopIteration:
            alive[k] = False


def run(gen):
    for _ in gen:
        pass


class Ring:
    def __init__(self, items):
        self.items = list(items)
        self.i = 0
        self.open = set()

    def next(self):
        for _ in range(len(self.items)):
            it = self.items[self.i]
            self.i = (self.i + 1) % len(self.items)
            if it not in self.open:
                self.open.add(it)
                return it
        raise AssertionError("all psum banks of the ring are open")

    def release(self, it):
        self.open.discard(it)


C_AQG, C_AKG, C_MQG0, C_MKG0, C_MQG1, C_MKG1 = 0, 1, 2, 3, 4, 5
C_BIN = 6
C_CONVB = 18
C_LNG = 24
C_LNB = 30
C_BFAR = 36
C_EPS = 48
C_CONVW = 50
C_GA, C_GM0, C_GM1 = 240, 241, 242
C_HBG = 243
NCST = 256
SKIP_FFN = False
DBG_TILE = None

B_N1 = (0, 2)
B_N2 = (1, 3)
B_MEM = (4, 5)

WA_GROUPS_A = [
    (0, 512, [(0, 512)]),
    (512, 512, [(512, 256), (2304, 256)]),
    (1024, 512, [(768, 512)]),
    (1536, 256, [(1280, 256)]),
    (1792, 512, [(1536, 512)]),
    (2304, 256, [(2048, 256)]),
]


def build_program(layers=(0, 1), nseq=NSEQ):
    nc = bass.Bass("TRN2", target_bir_lowering=False)
    dt = nc.dram_tensor
    x_d = dt("x", [nseq, S, D], F32, kind="ExternalInput").ap()
    mem_d = dt("mem", [nseq, 256, D], F32, kind="ExternalInput").ap()
    a_w_in = dt("a_w_in", [1, D, 2560], F32, kind="ExternalInput").ap()
    b_w_in = dt("b_w_in", [1, D, 1792], F32, kind="ExternalInput").ap()
    w_mem_kv = dt("w_mem_kv", [2, D, 512], F32, kind="ExternalInput").ap()
    w_out = dt("w_out", [2, D, D], F32, kind="ExternalInput").ap()
    w_gate = dt("w_gate", [2, D, DFF], F32, kind="ExternalInput").ap()
    w_up = dt("w_up", [2, D, DFF], F32, kind="ExternalInput").ap()
    w_down = dt("w_down", [2, DFF, D], F32, kind="ExternalInput").ap()
    cst_d = dt("cst", [128, NCST], F32, kind="ExternalInput").ap()
    bct_d = dt("bct", [6, 128, D], F32, kind="ExternalInput").ap()
    qmb_d = dt("qmb", [128, 256], F32, kind="ExternalInput").ap()
    bias_d = dt("biasT", [128, 6 * 4 * 128], F32, kind="ExternalInput").ap()
    diag_d = dt("convdiag", [6, 128, 31 * 128], F32, kind="ExternalInput").ap()
    out_d = dt("out", [nseq, S, D], F32, kind="ExternalOutput").ap()
    dbg_d = dt("dbg", [128, 1024], BF16, kind="ExternalOutput").ap() if DBG_TILE is not None else None

    es = ExitStack()
    with es:
        ARENA = 212800
        arena = es.enter_context(nc.sbuf_tensor("arena", [128, ARENA], U8))
        ps = es.enter_context(nc.psum_tensor("ps", [128, 8, 512], F32))
        P = Prog(nc, es)
        block = es.enter_context(nc.Block())

        def view(off, nbytes, dtype, pat=None, **kw):
            ap = arena[:, off:off + nbytes].bitcast(dtype)
            if pat is not None:
                ap = ap.rearrange(pat, **kw)
            return ap

        def mm(out, lhsT, rhs, start, stop, reads, writes):
            P.emit(PE, lambda e: e.matmul(out, lhsT=lhsT, rhs=rhs, start=start, stop=stop), reads, writes)

        def tr(out, in_, ident, reads, writes):
            P.emit(PE, lambda e: e.transpose(out=out, in_=in_, identity=ident), reads, writes)

        def act(out, in_, func, reads, writes, **kw):
            P.emit(ACT, lambda e: e.activation(out=out, in_=in_, func=func, **kw), reads, writes)

        def tt(eng, out, in0, in1, op, reads, writes):
            P.emit(eng, lambda e: e.tensor_tensor(out=out, in0=in0, in1=in1, op=op), reads, writes)

        def stt(out, in0, scalar, in1, op0, op1, reads, writes):
            P.emit(DVE, lambda e: e.scalar_tensor_tensor(out=out, in0=in0, scalar=scalar, in1=in1, op0=op0, op1=op1),
                   reads, writes)

        def ts(eng, out, in0, s1, s2, op0, op1, reads, writes):
            if op1 is None:
                P.emit(eng, lambda e: e.tensor_scalar(out=out, in0=in0, scalar1=s1, scalar2=None, op0=op0), reads, writes)
            else:
                P.emit(eng, lambda e: e.tensor_scalar(out=out, in0=in0, scalar1=s1, scalar2=s2, op0=op0, op1=op1),
                       reads, writes)

        def cp(eng, out, in_, reads, writes):
            P.emit(eng, lambda e: e.tensor_copy(out=out, in_=in_), reads, writes)

        def mset(eng, ap, val, writes):
            P.emit(eng, lambda e: e.memset(ap, val), (), writes)

        X0 = 0
        WA0 = 65536
        R10 = WA0 + 40960
        R20 = R10 + 49152
        G0 = R20 + 32768
        assert G0 + 24384 <= ARENA
        x_sb = view(X0, 65536, F32, "p (t d) -> p t d", t=NT)
        x_t = [Tile(f"x{i}") for i in range(NT)]

        g = G0
        ident_f = view(g, 512, F32); g += 512
        ident_b = view(g, 256, BF16); g += 256
        cst = view(g, 1024, F32); g += 1024
        gbc = view(g, 4096, F32); g += 4096
        xn = view(g, 4096, F32); g += 4096
        kmT = [view(g + 1024 * l, 1024, BF16, "p (c m) -> p c m", c=2) for l in range(2)]; g += 2048
        vm = [view(g + 1040 * l, 1040, BF16, "p (t h e) -> p t h e", t=2, h=4) for l in range(2)]; g += 2080
        stats = view(g, 1024, F32); g += 1024
        ffn_silu = [view(g + 1024 * i, 1024, F32) for i in range(2)]; g += 2048
        ffn_act = [view(g + 512 * i, 512, BF16) for i in range(4)]; g += 2048
        qmb = view(g, 1024, F32); g += 1024
        ptm = view(g, 2048, BF16); g += 2048
        assert g <= G0 + 24384, g - G0
        t_ident, t_cst, t_gbc, t_xn_default = Tile("ident"), Tile("cst"), Tile("gbc"), Tile("xn")
        t_km = [Tile(f"km{l}") for l in range(2)]
        t_vm = [Tile(f"vm{l}") for l in range(2)]
        t_silu = [Tile(f"silu{i}") for i in range(2)]
        t_act = [Tile(f"act{i}") for i in range(4)]
        t_qmb, t_ptm = Tile("qmb"), Tile("ptm")
        ST_SS, ST_LN, ST_RS = 0, 16, 32
        ST_H, ST_HL, ST_HR = 48, 80, 112
        ST_REC = 144
        ST_BN = 176
        t_srms, t_shead, t_srec, t_srecm, t_sbn = Tile("srms"), Tile("shead"), Tile("srec"), Tile("srecm"), Tile("sbn")

        pb = [Tile(f"ps{b}", excl=True) for b in range(8)]

        def bank(b):
            return ps[:, b, :]

        def col(c, n=1):
            return cst[:, c:c + n]

        P.dma(SP, cst, cst_d, writes=[t_cst])
        P.dma(SP, qmb, qmb_d, writes=[t_qmb])
        mset(POOL, ident_f, 0.0, [t_ident])
        P.emit(POOL, lambda e: e.affine_select(out=ident_f, in_=ident_f, pattern=[[-1, 128]],
                                               compare_op=ALU.not_equal, fill=1.0, base=0,
                                               channel_multiplier=1),
               reads=[t_ident], writes=[t_ident])
        cp(POOL, ident_b, ident_f, [t_ident], [t_ident])

        tt(DVE, col(C_GA), col(C_AQG), col(C_AKG), ALU.mult, [t_cst], [t_cst])
        tt(DVE, col(C_GM0), col(C_MQG0), col(C_MKG0), ALU.mult, [t_cst], [t_cst])
        tt(DVE, col(C_GM1), col(C_MQG1), col(C_MKG1), ALU.mult, [t_cst], [t_cst])

        xnb_default = xn.bitcast(BF16)[:, 0:D]

        xnb2 = xn.bitcast(BF16)[:, D:2 * D]
        t_xn2 = Tile("xn2")

        def norm_transpose_g(src_ap, src_tiles, sscol, dst, dst_tiles, pool, evac=DVE, xbuf=None, gain=None):
            ss = stats[:, ST_SS + sscol:ST_SS + sscol + 1]
            ln = stats[:, ST_LN + sscol:ST_LN + sscol + 1]
            rs = stats[:, ST_RS + sscol:ST_RS + sscol + 1]
            xnb, t_xn = xbuf if xbuf is not None else (xnb_default, t_xn_default)
            act(xnb, src_ap, AF.Square, src_tiles, [t_xn, t_srms], accum_out=ss)
            yield
            act(ln, ss, AF.Ln, [t_srms, t_cst], [t_srms], scale=1.0 / D, bias=col(C_EPS))
            act(rs, ln, AF.Exp, [t_srms], [t_srms], scale=-0.5)
            yield
            g_ap, g_t = gain if gain is not None else (gbc, t_gbc)
            stt(xnb, src_ap, rs, g_ap, ALU.mult, ALU.mult, src_tiles + [t_srms, g_t], [t_xn])
            yield
            b = pool.next()
            pbf = bank(b).bitcast(BF16)
            for kc in range(KC):
                tr(pbf[:, kc * 128:(kc + 1) * 128], xnb[:, kc * 128:(kc + 1) * 128], ident_b, [t_xn, t_ident], [pb[b]])
            src3 = pbf.rearrange("p (k t) -> p k t", k=KC)
            if evac == ACT:
                act(dst, src3, AF.Copy, [pb[b]], dst_tiles)
            else:
                cp(DVE, dst, src3, [pb[b]], dst_tiles)
            pool.release(b)
            yield

        def norm_transpose(src_ap, src_tiles, sscol, dsts, dst_tiles, pool):
            raise NotImplementedError

        def head_norm_transpose(nheads, qf, t_qf, qsq, t_qsq, dst_fn, pool):
            w = nheads * 64
            q3 = qf[:, 0:w].rearrange("p (h d) -> p h d", d=64)
            act(qsq[:, 0:w], qf[:, 0:w], AF.Square, [t_qf], [t_qsq])
            hs = stats[:, ST_H:ST_H + nheads]
            hl = stats[:, ST_HL:ST_HL + nheads]
            hr = stats[:, ST_HR:ST_HR + nheads]
            P.emit(DVE, lambda e: e.tensor_reduce(out=hs, in_=qsq[:, 0:w].rearrange("p (h d) -> p h d", d=64),
                                                  axis=AX.X, op=ALU.add), [t_qsq], [t_shead])
            act(hl, hs, AF.Ln, [t_shead, t_cst], [t_shead], scale=1.0 / 64, bias=col(C_EPS))
            act(hr, hl, AF.Exp, [t_shead], [t_shead], scale=-0.5)
            tt(DVE, q3, q3, hr.unsqueeze(2).broadcast_to([128, nheads, 64]), ALU.mult, [t_qf, t_shead], [t_qf])
            npairs = nheads // 2
            p0 = 0
            while p0 < npairs:
                np_ = min(4, npairs - p0)
                b = pool.next()
                for k in range(np_):
                    pp = p0 + k
                    tr(bank(b)[:, k * 128:(k + 1) * 128], qf[:, pp * 128:(pp + 1) * 128], ident_f,
                       [t_qf, t_ident], [pb[b]])
                dst_fn(b, p0, np_)
                pool.release(b)
                p0 += np_

        mk_memt = view(R10, 8192, F32, "p (t d) -> p t d", t=2)
        mk_gb = [view(R10 + 8192 + 4096 * k, 4096, F32) for k in range(2)]
        mk_wkv = [view(R10 + 16384, 8192, BF16, "p (c n) -> p c n", c=8),
                  view(R10 + 24576, 8192, BF16, "p (c n) -> p c n", c=8)]

        def mem_kv_loads(s, extra_writes=(), only_slot0=False):
            t_memt = Tile("memt")
            t_gb = [Tile("mgb0"), Tile("mgb1")]
            t_wkv = [Tile("wkv0"), Tile("wkv1")]
            ew = list(extra_writes)
            P.dma(SP, mk_memt, mem_d[s].rearrange("(t p) d -> p t d", p=128), writes=[t_memt] + ew)
            for k, l in enumerate(layers):
                P.dma(SP, mk_gb[k], bct_d[B_MEM[l]], writes=[t_gb[k]] + ew)
            for k, l in enumerate(layers):
                if only_slot0 and k > 0:
                    continue
                P.dma(POOL, mk_wkv[k], w_mem_kv[l].rearrange("(c p) n -> p c n", p=128), writes=[t_wkv[k]] + ew,
                      max_dma_last_dim=4096)
            return dict(memt=t_memt, gb=t_gb, wkv=t_wkv, have_wkv1=not (only_slot0 and len(layers) > 1))

        def mem_kv(s, pre=None):
            if pre is None:
                pre = mem_kv_loads(s)
            elif not pre["have_wkv1"]:
                P.dma(POOL, mk_wkv[1], w_mem_kv[layers[1]].rearrange("(c p) n -> p c n", p=128), writes=[pre["wkv"][1]],
                      max_dma_last_dim=4096)
            memt, t_memt = mk_memt, pre["memt"]
            pool = Ring([0, 1, 2, 3, 4, 5])

            def one_layer(l, k):
                o = R10 + 32768 + k * 6144
                memT = view(o, 4096, BF16, "p (c m) -> p c m", c=8)
                qf = view(o + 4096, 1024, F32)
                qsq = view(o + 5120, 1024, F32)
                wkv, gb = mk_wkv[k], mk_gb[k]
                t_wkv, t_gb = pre["wkv"][k], pre["gb"][k]
                t_memT, t_qf, t_qsq = Tile("memT"), Tile("mqf"), Tile("mqsq")
                xb = (xnb_default, t_xn_default) if k == 0 else (xnb2, t_xn2)
                for mt in range(2):
                    yield from norm_transpose_g(memt[:, mt, :], [t_memt], 2 * k + mt, memT[:, :, mt * 128:(mt + 1) * 128],
                                                [t_memT], pool, xbuf=xb, gain=(gb, t_gb))
                gc = C_GM0 if l == 0 else C_GM1
                hs = stats[:, ST_H + 4 * k:ST_H + 4 * k + 4]
                hl = stats[:, ST_HL + 4 * k:ST_HL + 4 * k + 4]
                hr = stats[:, ST_HR + 4 * k:ST_HR + 4 * k + 4]
                t_sh = Tile("msh")
                for mt in range(2):
                    b = pool.next()
                    for kc in range(KC):
                        mm(bank(b), memT[:, kc, mt * 128:(mt + 1) * 128], wkv[:, kc, :], kc == 0, kc == KC - 1,
                           [t_memT, t_wkv], [pb[b]])
                    cp(DVE, vm[l][:, mt, :, 0:64], bank(b)[:, 256:512].rearrange("p (h d) -> p h d", d=64),
                       [pb[b]], [t_vm[l]])
                    mset(POOL, vm[l][:, mt, :, 64:65], 1.0, [t_vm[l]])
                    cp(DVE, qf[:, 0:256], bank(b)[:, 0:256], [pb[b]], [t_qf])
                    pool.release(b)
                    yield
                    act(qsq, qf, AF.Square, [t_qf], [t_qsq])
                    yield
                    P.emit(DVE, lambda e, hs=hs, qsq=qsq: e.tensor_reduce(
                        out=hs, in_=qsq.rearrange("p (h d) -> p h d", d=64), axis=AX.X, op=ALU.add), [t_qsq], [t_sh])
                    yield
                    act(hl, hs, AF.Ln, [t_sh, t_cst], [t_sh], scale=1.0 / 64, bias=col(C_EPS))
                    act(hr, hl, AF.Exp, [t_sh], [t_sh], scale=-0.5)
                    yield
                    q3 = qf.rearrange("p (h d) -> p h d", d=64)
                    tt(DVE, q3, q3, hr.unsqueeze(2).broadcast_to([128, 4, 64]), ALU.mult, [t_qf, t_sh], [t_qf])
                    yield
                    b2 = pool.next()
                    for pp in range(2):
                        tr(bank(b2)[:, pp * 128:(pp + 1) * 128], qf[:, pp * 128:(pp + 1) * 128], ident_f,
                           [t_qf, t_ident], [pb[b2]])
                    act(kmT[l][:, :, mt * 128:(mt + 1) * 128], bank(b2)[:, 0:256].rearrange("p (k t) -> p k t", k=2),
                        AF.Copy, [pb[b2], t_cst], [t_km[l]], scale=col(gc))
                    pool.release(b2)
                    yield

            interleave([one_layer(l, k) for k, l in enumerate(layers)])

        wa_A = view(WA0, 40960, BF16, "p (c n) -> p c n", c=8)
        wa_B = view(WA0, 28672, BF16, "p (c n) -> p c n", c=8)
        wo = view(R10, 16384, BF16, "p (c n) -> p c n", c=8)
        t_wa = [[Tile(f"wa{i}_{j}") for j in range(2)] for i in range(6)]
        t_wo = [Tile(f"wo{i}") for i in range(2)]

        def load_wa(l):
            if l == 0:
                for gi, (c0, n, pieces) in enumerate(WA_GROUPS_A):
                    o = c0
                    for pi, (d0, dn) in enumerate(pieces):
                        P.dma(POOL, wa_A[:, :, o:o + dn], a_w_in[0][:, d0:d0 + dn].rearrange("(c p) n -> p c n", p=128),
                              writes=[t_wa[gi][pi]], max_dma_last_dim=4096)
                        o += dn
            else:
                bounds = [0, 512, 1024, 1536, 1792]
                for gi in range(4):
                    a, b_ = bounds[gi], bounds[gi + 1]
                    P.dma(POOL, wa_B[:, :, a:b_], b_w_in[0][:, a:b_].rearrange("(c p) n -> p c n", p=128),
                          writes=[t_wa[gi][0]], max_dma_last_dim=4096)

        def load_wo(l):
            for hf in range(2):
                P.dma(POOL, wo[:, :, hf * 512:(hf + 1) * 512],
                      w_out[l][:, hf * 512:(hf + 1) * 512].rearrange("(c p) n -> p c n", p=128),
                      writes=[t_wo[hf]], max_dma_last_dim=4096)

        mem_pre = {}

        def ffn_phase(s, l, is_last, next_wa):
            P.barrier()
            h2T = view(R20, 32768, BF16, "p (c t) -> p c t", c=8)
            t_h2 = [Tile(f"h2T{i}") for i in range(NT)]
            fs = []
            for i in range(2):
                o = R10 + i * 24576
                fs.append(dict(
                    gate=view(o, 8192, BF16, "p (c n) -> p c n", c=8),
                    up=view(o + 8192, 8192, BF16, "p (c n) -> p c n", c=8),
                    down=view(o + 16384, 8192, BF16, "p (j n) -> p j n", j=4),
                    tg=[Tile(f"fg{i}_{j}") for j in range(FG)], tu=[Tile(f"fu{i}_{j}") for j in range(FG)],
                    td=[Tile(f"fd{i}_{j}") for j in range(FG)]))
            groups = []
            c = 0
            while c < NFC:
                n = min(FG, NFC - c)
                groups.append((c, n))
                c += n

            def load_group(gi):
                c0, n = groups[gi]
                f = fs[gi % 2]
                if False:
                    for j in range(n):
                        cc = c0 + j
                        P.dma(POOL, f["gate"][:, :, j * 128:(j + 1) * 128],
                              w_gate[l][:, cc * 128:(cc + 1) * 128].rearrange("(c p) n -> p c n", p=128),
                              writes=[f["tg"][j]], max_dma_last_dim=4096)
                        P.dma(POOL, f["up"][:, :, j * 128:(j + 1) * 128],
                              w_up[l][:, cc * 128:(cc + 1) * 128].rearrange("(c p) n -> p c n", p=128),
                              writes=[f["tu"][j]], max_dma_last_dim=4096)
                        P.dma(POOL, f["down"][:, j, :], w_down[l][cc * 128:(cc + 1) * 128, :],
                              writes=[f["td"][j]], max_dma_last_dim=4096)
                    return
                P.dma(POOL, f["gate"][:, :, 0:n * 128],
                      w_gate[l][:, c0 * 128:(c0 + n) * 128].rearrange("(c p) n -> p c n", p=128),
                      writes=f["tg"][0:n], max_dma_last_dim=4096)
                P.dma(POOL, f["up"][:, :, 0:n * 128],
                      w_up[l][:, c0 * 128:(c0 + n) * 128].rearrange("(c p) n -> p c n", p=128),
                      writes=f["tu"][0:n], max_dma_last_dim=4096)
                P.dma(POOL, f["down"][:, 0:n, :],
                      w_down[l][c0 * 128:(c0 + n) * 128, :].rearrange("(j p) n -> p j n", p=128),
                      writes=f["td"][0:n], max_dma_last_dim=4096)

            load_group(0)
            load_group(1)
            P.dma(SP, gbc, bct_d[B_N2[l]], writes=[t_gbc])
            gu = Ring([0, 1, 2, 3])

            def h2_gens(tb):
                gens = []
                for k, i in enumerate((2 * tb, 2 * tb + 1)):
                    xb = (xnb_default, t_xn_default) if k == 0 else (xnb2, t_xn2)
                    g_ = norm_transpose_g(x_sb[:, i, :], [x_t[i]], i, h2T[:, :, i * 128:(i + 1) * 128],
                                          [t_h2[i]], gu, evac=(DVE if i % 2 else ACT), xbuf=xb)
                    for _ in range(3):
                        next(g_)
                    gens.append(g_)
                return gens

            for g_ in h2_gens(0):
                run(g_)
            pending_h2 = h2_gens(1)
            silr = Ring([0, 1])
            actr = Ring([0, 1, 2, 3])

            def do_gu(f, tb, j):
                b = gu.next()
                rhs_t = [t_h2[2 * tb], t_h2[2 * tb + 1]]
                for kc in range(KC):
                    mm(bank(b)[:, 0:256], f["gate"][:, kc, j * 128:(j + 1) * 128], h2T[:, kc, tb * 256:(tb + 1) * 256],
                       kc == 0, kc == KC - 1, [f["tg"][j]] + rhs_t, [pb[b]])
                for kc in range(KC):
                    mm(bank(b)[:, 256:512], f["up"][:, kc, j * 128:(j + 1) * 128], h2T[:, kc, tb * 256:(tb + 1) * 256],
                       kc == 0, kc == KC - 1, [f["tu"][j]] + rhs_t, [pb[b]])
                si = silr.next(); silr.release(si)
                ai = actr.next(); actr.release(ai)
                act(ffn_silu[si], bank(b)[:, 0:256], AF.Silu, [pb[b]], [t_silu[si]])
                tt(DVE, ffn_act[ai], bank(b)[:, 256:512], ffn_silu[si], ALU.mult, [pb[b], t_silu[si]], [t_act[ai]])
                gu.release(b)
                return ai

            def do_down(f, n, last_group, tb, j, ai):
                for t2 in range(2):
                    for hf in range(2):
                        b = 4 + t2 * 2 + hf
                        mm(bank(b), ffn_act[ai][:, t2 * 128:(t2 + 1) * 128], f["down"][:, j, hf * 512:(hf + 1) * 512],
                           j == 0, j == n - 1, [t_act[ai], f["td"][j]], [pb[b]])
                if j == n - 1:
                    for t2 in range(2):
                        ti = 2 * tb + t2
                        xv = x_sb[:, ti, :].rearrange("p (h n) -> p h n", h=2)
                        tt(DVE, xv, ps[:, 4 + 2 * t2:6 + 2 * t2, :], xv, ALU.add,
                           [pb[4 + 2 * t2], pb[5 + 2 * t2], x_t[ti]], [x_t[ti]])
                        if is_last and last_group:
                            ev = P.dma(SP, out_d[s, ti * 128:(ti + 1) * 128, :], x_sb[:, ti, :], reads=[x_t[ti]])
                            P.last_dma_events.append(ev)
                            if s + 1 < nseq:
                                P.dma(SP, x_sb[:, ti, :], x_d[s + 1, ti * 128:(ti + 1) * 128, :], writes=[x_t[ti]])

            for gi, (c0, n) in enumerate(groups):
                f = fs[gi % 2]
                pend = []
                if is_last and s + 1 < nseq and gi == len(groups) - 1 and gi % 2 == 1:
                    f0 = fs[0]
                    mem_pre[s + 1] = mem_kv_loads(s + 1, extra_writes=f0["tg"] + f0["tu"] + f0["td"], only_slot0=True)
                for tb in range(8):
                    for j in range(n):
                        ai = do_gu(f, tb, j)
                        pend.append((tb, j, ai))
                        if len(pend) > 3:
                            do_down(f, n, gi == len(groups) - 1, *pend.pop(0))
                    if gi == 0 and tb + 1 < 8:
                        for g_ in pending_h2:
                            run(g_)
                        pending_h2 = h2_gens(tb + 2) if tb + 2 < 8 else []
                while pend:
                    do_down(f, n, gi == len(groups) - 1, *pend.pop(0))
                if gi == 0 and next_wa is not None:
                    load_wa(next_wa)
                if gi + 2 < len(groups):
                    load_group(gi + 2)

        def mem_attn(l, qmTz, t_qmTz, dst, t_cat, b0, b1, pvb):
            for h in range(4):
                for mt in range(2):
                    b = b0 if h < 2 else b1
                    c0 = ((h % 2) * 2 + mt) * 128
                    mm(bank(b)[:, c0:c0 + 128], kmT[l][:, h // 2, mt * 128:(mt + 1) * 128], qmTz[:, h, :], True, True,
                       [t_km[l], t_qmTz], [pb[b]])
            for hb, b in enumerate((b0, b1)):
                act(ptm[:, hb * 512:(hb + 1) * 512], bank(b), AF.Exp, [pb[b]], [t_ptm], scale=SCALE)
            for h in range(4):
                for mt in range(2):
                    c0 = (h * 2 + mt) * 128
                    mm(bank(pvb)[:, h * 65:(h + 1) * 65], ptm[:, c0:c0 + 128], vm[l][:, mt, h, :], mt == 0, mt == 1,
                       [t_ptm, t_vm[l]], [pb[pvb]])
            rec = stats[:, ST_REC + 12:ST_REC + 16]
            pv3 = bank(pvb)[:, 0:260].rearrange("p (h e) -> p h e", e=65)
            P.emit(DVE, lambda e: e.reciprocal(out=rec.unsqueeze(2), in_=pv3[:, :, 64:65]), [pb[pvb]], [t_srecm])
            tt(DVE, dst.rearrange("p (h d) -> p h d", d=64), pv3[:, :, 0:64],
               rec.unsqueeze(2).broadcast_to([128, 4, 64]), ALU.mult, [pb[pvb], t_srecm], [t_cat])

        def out_proj(ti, catT, t_catT, pool):
            for hf in range(2):
                b = pool.next()
                for kc in range(KC):
                    mm(bank(b), catT[:, kc, :], wo[:, kc, hf * 512:(hf + 1) * 512], kc == 0, kc == KC - 1,
                       [t_catT, t_wo[hf]], [pb[b]])
                xv = x_sb[:, ti, hf * 512:(hf + 1) * 512]
                tt(DVE, xv, bank(b), xv, ALU.add, [pb[b], x_t[ti]], [x_t[ti]])
                pool.release(b)

        def mixer_a(s, l, pre_bias=None):
            P.barrier()
            o = R10 + 16384
            hT = [view(o + 2048 * i, 2048, BF16, "p (c t) -> p c t", c=8) for i in range(2)]; o += 4096
            qf = view(o, 3072, F32); o += 3072
            kf = view(o, 3072, F32); o += 3072
            qsq = view(o, 3072, F32); o += 3072
            qmf = view(o, 1024, F32); o += 1024
            qTz = [view(o + 3072 * i, 3072, BF16, "p (h t) -> p h t", h=12) for i in range(2)]; o += 6144
            qmTz = [view(o + 1024 * i, 1024, BF16, "p (h t) -> p h t", h=4) for i in range(2)]; o += 2048
            PT = [view(o + 2560 * i, 2560, BF16) for i in range(2)]; o += 5120
            cat = view(o, 2048, BF16); o += 2048
            catT = view(o, 2048, BF16, "p (c t) -> p c t", c=8); o += 2048
            assert o <= R10 + 49152, o - R10
            qb = qsq.bitcast(BF16)
            kring = view(R20, 9216, BF16, "p (s c t) -> p s c t", s=6, c=6)
            vring = view(R20 + 9216, 9360, BF16, "p (s h e) -> p s h e", s=6, h=12)
            biasT = view(R20 + 18576, 12288, F32, "p (c j q) -> p c j q", c=6, j=4)
            t_hT = [Tile(f"hT{i}") for i in range(2)]
            t_qf, t_kf, t_qsq, t_qmf = Tile("qf"), Tile("kf"), Tile("qsq"), Tile("qmf")
            t_qTz = [Tile(f"qTz{i}") for i in range(2)]
            t_qmTz = [Tile(f"qmTz{i}") for i in range(2)]
            t_PT = [Tile(f"PT{i}") for i in range(2)]
            t_cat, t_catT = Tile("cat"), Tile("catT")
            t_kr = [Tile(f"kr{i}") for i in range(6)]
            t_vr = [Tile(f"vr{i}") for i in range(6)]
            t_bias = pre_bias if pre_bias is not None else Tile("bias")

            if pre_bias is None:
                P.dma(POOL, gbc, bct_d[B_N1[l]], writes=[t_gbc], max_dma_last_dim=4096)
                P.dma(POOL, biasT.rearrange("p c j q -> p (c j q)"), bias_d, writes=[t_bias], max_dma_last_dim=4096)
            load_wo(l)
            for i in range(2):
                mset(POOL, qTz[i].rearrange("p h t -> p (h t)"), 0.0, [t_qTz[i]])
                mset(POOL, qmTz[i].rearrange("p h t -> p (h t)"), 0.0, [t_qmTz[i]])
            for sl in range(6):
                mset(POOL, vring[:, sl, :, 64:65], 1.0, [t_vr[sl]])
            def bias_init():
                for h in range(12):
                    bv = biasT[:, h // 2, :, :].rearrange("p (d a) q -> p d a q", a=2)[:, :, h % 2, :]
                    ts(DVE, bv, bv, col(C_BFAR + h), None, ALU.subtract, None, [t_bias, t_cst], [t_bias])
                mset(DVE, biasT[64:128, :, 0:2, 0:64], NEG, [t_bias])

            gpool = Ring([0, 1, 2])
            BNS, BFS, B4, PVB = (3, 4), (5, 5), 6, 7

            def stage1(i):
                yield from norm_transpose_g(x_sb[:, i, :], [x_t[i]], i % 16, hT[i % 2], [t_hT[i % 2]], gpool, evac=DVE)

            def stage2(i):
                hb = i % 2
                sl = i % 6
                qT4 = qTz[hb].rearrange("p (a b) t -> p a b t", b=2)
                qmT4 = qmTz[hb].rearrange("p (a b) t -> p a b t", b=2)

                def inproj(gi):
                    c0, n, pieces = WA_GROUPS_A[gi]
                    b = gpool.next()
                    for kc in range(KC):
                        mm(bank(b)[:, 0:n], hT[hb][:, kc, :], wa_A[:, kc, c0:c0 + n], kc == 0, kc == KC - 1,
                           [t_hT[hb]] + t_wa[gi][0:len(pieces)], [pb[b]])
                    return b

                hs = stats[:, ST_H:ST_H + 28]
                hl = stats[:, ST_HL:ST_HL + 28]
                hr = stats[:, ST_HR:ST_HR + 28]

                def sq_red(src, t_src, nheads, c0):
                    w = nheads * 64
                    act(qsq[:, 0:w], src[:, 0:w], AF.Square, [t_src], [t_qsq])
                    P.emit(DVE, lambda e: e.tensor_reduce(
                        out=stats[:, ST_H + c0:ST_H + c0 + nheads],
                        in_=qsq[:, 0:w].rearrange("p (h d) -> p h d", d=64), axis=AX.X, op=ALU.add),
                        [t_qsq], [t_shead])

                b = inproj(0)
                act(qf[:, 0:512], bank(b), AF.Copy, [pb[b]], [t_qf])
                gpool.release(b)
                yield
                b = inproj(1)
                act(qf[:, 512:768], bank(b)[:, 0:256], AF.Copy, [pb[b]], [t_qf])
                act(qmf, bank(b)[:, 256:512], AF.Copy, [pb[b]], [t_qmf])
                gpool.release(b)
                yield
                sq_red(qf, t_qf, 12, 0)
                yield
                b = inproj(2)
                act(kf[:, 0:512], bank(b), AF.Copy, [pb[b]], [t_kf])
                gpool.release(b)
                yield
                sq_red(qmf, t_qmf, 4, 24)
                yield
                b = inproj(3)
                act(kf[:, 512:768], bank(b)[:, 0:256], AF.Copy, [pb[b]], [t_kf])
                gpool.release(b)
                yield
                sq_red(kf, t_kf, 12, 12)
                yield
                b = inproj(4)
                cp(DVE, vring[:, sl, 0:8, 0:64], bank(b).rearrange("p (h d) -> p h d", d=64), [pb[b]], [t_vr[sl]])
                gpool.release(b)
                yield
                act(hl, hs, AF.Ln, [t_shead, t_cst], [t_shead], scale=1.0 / 64, bias=col(C_EPS))
                act(hr, hl, AF.Exp, [t_shead], [t_shead], scale=-0.5)
                yield
                b = inproj(5)
                cp(DVE, vring[:, sl, 8:12, 0:64], bank(b)[:, 0:256].rearrange("p (h d) -> p h d", d=64),
                   [pb[b]], [t_vr[sl]])
                gpool.release(b)
                yield
                tt(DVE, qb[:, 0:768].rearrange("p (h d) -> p h d", d=64), qf.rearrange("p (h d) -> p h d", d=64),
                   stats[:, ST_HR:ST_HR + 12].unsqueeze(2).broadcast_to([128, 12, 64]), ALU.mult,
                   [t_qf, t_shead], [t_qsq])
                yield
                b = gpool.next()
                pbf = bank(b).bitcast(BF16)
                for pp in range(6):
                    tr(pbf[:, pp * 128:(pp + 1) * 128], qb[:, pp * 128:(pp + 1) * 128], ident_b, [t_qsq, t_ident], [pb[b]])
                for hp in range(2):
                    cp(DVE, qT4[hp * 64:(hp + 1) * 64, :, hp, :],
                       pbf[hp * 64:(hp + 1) * 64, 0:768].rearrange("p (k t) -> p k t", k=6), [pb[b]], [t_qTz[hb]])
                gpool.release(b)
                yield
                tt(DVE, qb[:, 0:256].rearrange("p (h d) -> p h d", d=64), qmf.rearrange("p (h d) -> p h d", d=64),
                   stats[:, ST_HR + 24:ST_HR + 28].unsqueeze(2).broadcast_to([128, 4, 64]), ALU.mult,
                   [t_qmf, t_shead], [t_qsq])
                yield
                b = gpool.next()
                pbf = bank(b).bitcast(BF16)
                for pp in range(2):
                    tr(pbf[:, pp * 128:(pp + 1) * 128], qb[:, pp * 128:(pp + 1) * 128], ident_b, [t_qsq, t_ident], [pb[b]])
                for hp in range(2):
                    cp(DVE, qmT4[hp * 64:(hp + 1) * 64, :, hp, :],
                       pbf[hp * 64:(hp + 1) * 64, 0:256].rearrange("p (k t) -> p k t", k=2), [pb[b]], [t_qmTz[hb]])
                gpool.release(b)
                yield
                k3 = kf.rearrange("p (h d) -> p h d", d=64)
                tt(DVE, k3, k3, stats[:, ST_HR + 12:ST_HR + 24].unsqueeze(2).broadcast_to([128, 12, 64]), ALU.mult,
                   [t_kf, t_shead], [t_kf])
                yield
                for (p0, np_) in ((0, 4), (4, 2)):
                    b = gpool.next()
                    for k in range(np_):
                        pp = p0 + k
                        tr(bank(b)[:, k * 128:(k + 1) * 128], kf[:, pp * 128:(pp + 1) * 128], ident_f,
                           [t_kf, t_ident], [pb[b]])
                    act(kring[:, sl, p0:p0 + np_, :], bank(b)[:, 0:np_ * 128].rearrange("p (k t) -> p k t", k=np_),
                        AF.Copy, [pb[b], t_cst], [t_kr[sl]], scale=col(C_GA))
                    gpool.release(b)
                    yield

            def scores(i, p, hb):
                dmax = min(i, 4)
                pi = p % 2
                BN, BF = BNS[pi], BFS[pi]
                q2 = qTz[hb][:, 2 * p:2 * p + 2, :]
                for dl in range(dmax + 1):
                    sl = (i - dl) % 6
                    if dl < 2:
                        b, c0 = BN, dl * 256
                    elif dl < 4:
                        b, c0 = BF, (dl - 2) * 256
                    else:
                        b, c0 = B4, 0
                    mm(bank(b)[:, c0:c0 + 256], kring[:, sl, p, :], q2, True, True, [t_kr[sl], t_qTz[hb]], [pb[b]])
                nd = min(dmax + 1, 2)
                nv = bank(BN)[:, 0:nd * 256]
                bv = biasT[:, p, 0:nd * 2, :].rearrange("p j q -> p (j q)")
                stt(nv, nv, SCALE, bv, ALU.mult, ALU.add, [pb[BN], t_bias], [pb[BN]])
                if dmax >= 4:
                    v4 = bank(B4)[:, 0:256].rearrange("p (a q) -> p a q", a=2)
                    ts(DVE, v4[0:64, :, 64:128], v4[0:64, :, 64:128], NEG, None, ALU.add, None, [pb[B4]], [pb[B4]])
                act(PT[pi][:, 0:nd * 256], nv, AF.Exp, [pb[BN]], [t_PT[pi]])
                if dmax >= 4:
                    fv = ps[:, 5:7, :].rearrange("p b n -> p (b n)")[:, 0:768]
                    act(PT[pi][:, 512:1280], fv, AF.Exp, [pb[BF], pb[B4]], [t_PT[pi]], scale=SCALE)
                elif dmax >= 2:
                    nf = dmax - 1
                    act(PT[pi][:, 512:512 + nf * 256], bank(BF)[:, 0:nf * 256], AF.Exp, [pb[BF]], [t_PT[pi]], scale=SCALE)

            def pv(i, p, hb):
                dmax = min(i, 4)
                pi = p % 2
                fo, do_ = 512, 1024
                for hp in range(2):
                    h = 2 * p + hp
                    oc = (h % 6) * 65
                    for dl in range(dmax + 1):
                        sl = (i - dl) % 6
                        if dl < 2:
                            c0 = (dl * 2 + hp) * 128
                        elif dl < 4:
                            c0 = fo + ((dl - 2) * 2 + hp) * 128
                        else:
                            c0 = do_ + hp * 128
                        mm(bank(PVB)[:, oc:oc + 65], PT[pi][:, c0:c0 + 128], vring[:, sl, h, :], dl == 0, dl == dmax,
                           [t_PT[pi], t_vr[sl]], [pb[PVB]])
                if p in (2, 5):
                    rec = stats[:, ST_REC + (p // 3) * 6:ST_REC + (p // 3) * 6 + 6]
                    pv3 = bank(PVB)[:, 0:390].rearrange("p (h e) -> p h e", e=65)
                    P.emit(DVE, lambda e: e.reciprocal(out=rec.unsqueeze(2), in_=pv3[:, :, 64:65]), [pb[PVB]], [t_srec])
                    cc = (p // 3) * 384
                    tt(DVE, cat[:, cc:cc + 384].rearrange("p (h d) -> p h d", d=64), pv3[:, :, 0:64],
                       rec.unsqueeze(2).broadcast_to([128, 6, 64]), ALU.mult, [pb[PVB], t_srec], [t_cat])

            def stage3(i):
                hb = i % 2
                for p in range(7):
                    if p < 6:
                        scores(i, p, hb)
                        yield
                    if p >= 1:
                        pv(i, p - 1, hb)
                        yield
                pvb = gpool.next()
                mem_attn(l, qmTz[hb], t_qmTz[hb], cat[:, 768:1024], t_cat, BNS[0], BNS[1], pvb)
                gpool.release(pvb)
                yield
                b = gpool.next()
                pbf = bank(b).bitcast(BF16)
                for kc in range(KC):
                    tr(pbf[:, kc * 128:(kc + 1) * 128], cat[:, kc * 128:(kc + 1) * 128], ident_b,
                       [t_cat, t_ident], [pb[b]])
                act(catT, pbf.rearrange("p (c t) -> p c t", c=8), AF.Copy, [pb[b]], [t_catT])
                gpool.release(b)
                if DBG_TILE is not None and i == DBG_TILE and s == 0:
                    P.last_dma_events.append(P.dma(SP, dbg_d, cat, reads=[t_cat]))
                yield
                out_proj(i, catT, t_catT, gpool)
                yield

            for it in range(NT + 2):
                gens, tot = [], []
                if it == 2:
                    bias_init()
                if 2 <= it:
                    gens.append(stage3(it - 2)); tot.append(16)
                if 1 <= it <= NT:
                    gens.append(stage2(it - 1)); tot.append(18)
                if it < NT:
                    gens.append(stage1(it)); tot.append(4)
                interleave(gens, tot)

        def mixer_b(s, l):
            P.barrier()
            load_wo(l)
            hT = view(WA0 + 28672, 8192, BF16, "p (c t) -> p c t", c=8)
            o = R10 + 16384
            hglu = view(o, 6504, BF16, "p (j t) -> p j t", j=6); o += 6528
            sig = [view(o + 2048 * i, 2048, F32) for i in range(2)]; o += 4096
            yn = view(o, 3072, F32); o += 3072
            qmf = view(o, 1024, F32); o += 1024
            qsq = view(o, 1024, F32); o += 1024
            qmTz = [view(o + 1024 * i, 1024, BF16, "p (h t) -> p h t", h=4) for i in range(8)]; o += 8192
            catm = [view(o + 512 * i, 512, BF16) for i in range(2)]; o += 1024
            catT = [view(o + 2048 * i, 2048, BF16, "p (c t) -> p c t", c=8) for i in range(2)]; o += 4096
            assert o <= R10 + 49152, o - R10
            qb = qsq.bitcast(BF16)
            yT0 = view(R20, 12288, F32, "p (j t) -> p j t", j=6)
            diag = [view(R20 + 12288 + 7936 * i, 7936, BF16, "p (k c) -> p k c", k=31) for i in range(2)]
            extra = [WA0 + 36864, WA0 + 38912, R20 + 28160, R20 + 30208, G0 + 22304, G0 + 1792 + 4096 + 2048]
            yT = [[yT0[:, j, :] for j in range(6)], [view(extra[j], 2048, F32) for j in range(6)]]
            t_hT = [Tile(f"bhT{i}") for i in range(4)]
            t_hglu = [Tile(f"hglu_{j}") for j in range(6)]
            t_sig = [Tile(f"sig{i}") for i in range(2)]
            t_yn, t_qmf, t_qsq = Tile("yn"), Tile("bqmf"), Tile("bqsq")
            t_qmTz = [Tile(f"bqmTz{i}") for i in range(8)]
            t_catm = [Tile(f"catm{i}") for i in range(2)]
            t_catT = [Tile(f"bcatT{i}") for i in range(2)]
            t_yT = [[Tile(f"yT{i}_{j}") for j in range(6)] for i in range(2)]
            t_diag = [Tile(f"diag{i}") for i in range(2)]
            P.dma(SP, gbc, bct_d[B_N1[l]], writes=[t_gbc])
            for i in range(8):
                mset(POOL, qmTz[i].rearrange("p h t -> p (h t)"), 0.0, [t_qmTz[i]])
            for j in range(6):
                mset(POOL, hglu[:, j, 0:30], 0.0, [t_hglu[j]])
            gpool = Ring([0, 1, 2, 3, 4])
            convw = cst[:, C_CONVW:C_CONVW + 186].rearrange("p (j k) -> p j k", j=6)
            NB = S // 512

            def stage_x(c):
                cb = c % 2
                for t4 in range(4):
                    ti = 4 * c + t4
                    yield from norm_transpose_g(x_sb[:, ti, :], [x_t[ti]], ti, hT[:, :, t4 * 128:(t4 + 1) * 128],
                                                [t_hT[t4]], gpool, evac=(DVE if t4 % 2 else ACT))
                for j in range(6):
                    si = j % 2
                    bg = gpool.next()
                    ng = 6 + j
                    for kc in range(KC):
                        mm(bank(bg), wa_B[:, kc, ng * 128:(ng + 1) * 128], hT[:, kc, :], kc == 0, kc == KC - 1,
                           t_hT + [t_wa[ng // 4][0]], [pb[bg]])
                    act(sig[si], bank(bg), AF.Sigmoid, [pb[bg], t_cst], [t_sig[si]], bias=col(C_BIN + ng))
                    gpool.release(bg)
                    yield
                    ba = gpool.next()
                    for kc in range(KC):
                        mm(bank(ba), wa_B[:, kc, j * 128:(j + 1) * 128], hT[:, kc, :], kc == 0, kc == KC - 1,
                           t_hT + [t_wa[j // 4][0]], [pb[ba]])
                    stt(hglu[:, j, 30:542], bank(ba), col(C_BIN + j), sig[si], ALU.add, ALU.mult,
                        [pb[ba], t_cst, t_sig[si]], [t_hglu[j]])
                    gpool.release(ba)
                    yield

            def stage_yconv(c):
                cb = c % 2
                for j in range(6):
                    db = j % 2
                    P.dma(POOL, diag[db].rearrange("p k c -> p (k c)"), diag_d[j], writes=[t_diag[db]],
                          max_dma_last_dim=4096)
                    b = gpool.next()
                    for k in range(31):
                        mm(bank(b), diag[db][:, k, :], hglu[:, j, k:k + 512], k == 0, k == 30,
                           [t_diag[db], t_hglu[j]], [pb[b]])
                        if k % 8 == 7:
                            yield
                    act(yT[cb][j], bank(b), AF.Identity, [pb[b], t_cst], [t_yT[cb][j]], bias=col(C_CONVB + j))
                    gpool.release(b)
                    cp(DVE, hglu[:, j, 0:30], hglu[:, j, 512:542], [t_hglu[j]], [t_hglu[j]])
                    yield

            def stage_yq(c):
                for t4 in range(4):
                    bq = gpool.next()
                    for kc in range(KC):
                        mm(bank(bq)[:, 0:256], hT[:, kc, t4 * 128:(t4 + 1) * 128], wa_B[:, kc, 1536:1792],
                           kc == 0, kc == KC - 1, [t_hT[t4], t_wa[3][0]], [pb[bq]])
                    tt(DVE, qmf, bank(bq)[:, 0:256], qmb, ALU.add, [pb[bq], t_qmb], [t_qmf])
                    gpool.release(bq)
                    yield
                    act(qsq, qmf, AF.Square, [t_qmf], [t_qsq])
                    yield
                    hs = stats[:, ST_H:ST_H + 4]
                    hl = stats[:, ST_HL:ST_HL + 4]
                    hr = stats[:, ST_HR:ST_HR + 4]
                    P.emit(DVE, lambda e, hs=hs: e.tensor_reduce(out=hs, in_=qsq.rearrange("p (h d) -> p h d", d=64),
                                                                 axis=AX.X, op=ALU.add), [t_qsq], [t_shead])
                    yield
                    act(hl, hs, AF.Ln, [t_shead, t_cst], [t_shead], scale=1.0 / 64, bias=col(C_EPS))
                    act(hr, hl, AF.Exp, [t_shead], [t_shead], scale=-0.5)
                    yield
                    tt(DVE, qb[:, 0:256].rearrange("p (h d) -> p h d", d=64), qmf.rearrange("p (h d) -> p h d", d=64),
                       hr.unsqueeze(2).broadcast_to([128, 4, 64]), ALU.mult, [t_qmf, t_shead], [t_qsq])
                    yield
                    b = gpool.next()
                    pbf = bank(b).bitcast(BF16)
                    for pp in range(2):
                        tr(pbf[:, pp * 128:(pp + 1) * 128], qb[:, pp * 128:(pp + 1) * 128], ident_b,
                           [t_qsq, t_ident], [pb[b]])
                    qi = (c % 2) * 4 + t4
                    qmT4 = qmTz[qi].rearrange("p (a b) t -> p a b t", b=2)
                    for hp in range(2):
                        cp(DVE, qmT4[hp * 64:(hp + 1) * 64, :, hp, :],
                           pbf[hp * 64:(hp + 1) * 64, 0:256].rearrange("p (k t) -> p k t", k=2), [pb[b]], [t_qmTz[qi]])
                    gpool.release(b)
                    yield

            yn2 = view(R10 + 16384 + 6528 + 4096 + 3072 + 1024 + 1024 + 8192 + 1024 + 4096, 3072, F32)
            yns = [yn, yn2]
            t_yns = [t_yn, Tile("yn2")]
            t_sbns = [t_sbn, Tile("sbn2")]

            def z_tile(c, t4):
                ti = 4 * c + t4
                tb = ti % 2
                cb = c % 2
                qi = cb * 4 + t4
                yb, t_yb, t_sb = yns[tb], t_yns[tb], t_sbns[tb]
                sb0 = ST_BN + 16 * tb
                b1, b2_ = gpool.next(), gpool.next()
                for j in range(6):
                    bb = b1 if j < 4 else b2_
                    tr(bank(bb)[:, (j % 4) * 128:(j % 4 + 1) * 128], yT[cb][j][:, t4 * 128:(t4 + 1) * 128], ident_f,
                       [t_yT[cb][j], t_ident], [pb[bb]])
                mv = stats[:, sb0 + 12:sb0 + 14]
                lnr = stats[:, sb0 + 14:sb0 + 16]
                P.emit(DVE, lambda e: e.bn_stats(out=stats[:, sb0:sb0 + 6], in_=bank(b1)), [pb[b1]], [t_sb])
                P.emit(DVE, lambda e: e.bn_stats(out=stats[:, sb0 + 6:sb0 + 12], in_=bank(b2_)[:, 0:256]), [pb[b2_]], [t_sb])
                P.emit(DVE, lambda e: e.bn_aggr(out=stats[:, sb0 + 12:sb0 + 14], in_=stats[:, sb0:sb0 + 12]), [t_sb], [t_sb])
                act(lnr[:, 0:1], mv[:, 1:2], AF.Ln, [t_sb, t_cst], [t_sb], bias=col(C_EPS))
                act(lnr[:, 1:2], lnr[:, 0:1], AF.Exp, [t_sb], [t_sb], scale=-0.5)
                ts(DVE, yb[:, 0:512], bank(b1), mv[:, 0:1], lnr[:, 1:2], ALU.subtract, ALU.mult, [pb[b1], t_sb], [t_yb])
                ts(DVE, yb[:, 512:768], bank(b2_)[:, 0:256], mv[:, 0:1], lnr[:, 1:2], ALU.subtract, ALU.mult,
                   [pb[b2_], t_sb], [t_yb])
                gpool.release(b1); gpool.release(b2_)
                yield
                b3, b4 = gpool.next(), gpool.next()
                for j in range(6):
                    bb = b3 if j < 4 else b4
                    tr(bank(bb)[:, (j % 4) * 128:(j % 4 + 1) * 128], yb[:, j * 128:(j + 1) * 128], ident_f,
                       [t_yb, t_ident], [pb[bb]])
                for j in range(6):
                    bb = b3 if j < 4 else b4
                    act(catT[tb][:, j, :], bank(bb)[:, (j % 4) * 128:(j % 4 + 1) * 128], AF.Silu,
                        [pb[bb], t_cst], [t_catT[tb]], scale=col(C_LNG + j), bias=col(C_LNB + j))
                gpool.release(b3); gpool.release(b4)
                yield
                mem_attn(l, qmTz[qi], t_qmTz[qi], catm[tb], t_catm[tb], 5, 6, 7)
                yield
                b = gpool.next()
                pbf = bank(b).bitcast(BF16)
                for kc in range(2):
                    tr(pbf[:, kc * 128:(kc + 1) * 128], catm[tb][:, kc * 128:(kc + 1) * 128], ident_b,
                       [t_catm[tb], t_ident], [pb[b]])
                act(catT[tb][:, 6:8, :], pbf[:, 0:256].rearrange("p (c t) -> p c t", c=2), AF.Copy,
                    [pb[b]], [t_catT[tb]])
                gpool.release(b)
                yield
                out_proj(ti, catT[tb], t_catT[tb], gpool)
                yield

            def stage_z(c):
                for t0 in (0, 2):
                    sub = [z_tile(c, t0), z_tile(c, t0 + 1)]
                    alive = [True, True]
                    while any(alive):
                        for k in range(2):
                            if alive[k]:
                                try:
                                    next(sub[k])
                                    yield
                                except StopIteration:
                                    alive[k] = False

            def chain(c):
                yield from stage_x(c)
                sub = [stage_yconv(c), stage_yq(c)]
                alive = [True, True]
                while any(alive):
                    for k in range(2):
                        if alive[k]:
                            try:
                                next(sub[k])
                                yield
                            except StopIteration:
                                alive[k] = False

            for c in range(NB + 1):
                gens, tot = [], []
                if c >= 1:
                    gens.append(stage_z(c - 1)); tot.append(20)
                if c < NB:
                    gens.append(chain(c)); tot.append(29 + 48)
                interleave(gens, tot)

        seq_layers = [(s, l) for s in range(nseq) for l in layers]
        for idx, (s, l) in enumerate(seq_layers):
            if l == layers[0]:
                if idx > 0:
                    P.barrier()
                mem_kv(s, mem_pre.get(s))
                pre_bias = None
                if idx == 0 and l == 0:
                    pre_bias = Tile("bias")
                    P.dma(SP, gbc, bct_d[B_N1[l]], writes=[t_gbc])
                    P.dma(SP, view(R20 + 18576, 12288, F32), bias_d, writes=[pre_bias])
                if idx == 0:
                    load_wa(layers[0])
                if idx == 0 or SKIP_FFN:
                    for i in range(NT):
                        P.dma(SP, x_sb[:, i, :], x_d[s, i * 128:(i + 1) * 128, :], writes=[x_t[i]])
            if l == 0:
                mixer_a(s, l, pre_bias if (idx == 0) else None)
            else:
                mixer_b(s, l)
            nxt = seq_layers[idx + 1][1] if idx + 1 < len(seq_layers) else None
            if SKIP_FFN:
                if nxt is not None:
                    load_wa(nxt)
                P.barrier()
                for ti in range(NT):
                    ev = P.dma(SP, out_d[s, ti * 128:(ti + 1) * 128, :], x_sb[:, ti, :], reads=[x_t[ti]])
                    P.last_dma_events.append(ev)
            else:
                ffn_phase(s, l, is_last=(l == layers[-1]), next_wa=nxt)
        P.wait_events(SP, P.last_dma_events)
        P.lower(block)
    return nc


def _pack_consts(inp):
    cst = np.zeros((128, NCST), np.float32)
    p = np.arange(128)
    cst[:, C_AQG] = inp["a_q_g"][0][p % 64]
    cst[:, C_AKG] = inp["a_k_g"][0][p % 64]
    cst[:, C_MQG0] = inp["mq_g"][0][p % 64]
    cst[:, C_MKG0] = inp["mk_g"][0][p % 64]
    cst[:, C_MQG1] = inp["mq_g"][1][p % 64]
    cst[:, C_MKG1] = inp["mk_g"][1][p % 64]
    cst[:, C_BIN:C_BIN + 12] = inp["b_b_in"][0][:1536].reshape(12, 128).T
    cst[:, C_CONVB:C_CONVB + 6] = inp["b_conv_b"][0].reshape(6, 128).T
    cst[:, C_LNG:C_LNG + 6] = inp["b_ln_g"][0].reshape(6, 128).T
    cst[:, C_LNB:C_LNB + 6] = inp["b_ln_b"][0].reshape(6, 128).T
    cst[:, C_BFAR:C_BFAR + 12] = inp["a_rel_bias"][0][:, 191][None, :]
    cst[:, C_EPS] = EPS
    cw = inp["b_conv_w"][0]
    cst[:, C_CONVW:C_CONVW + 186] = cw.reshape(31, 6, 128).transpose(2, 1, 0).reshape(128, 186)
    bct = np.zeros((6, 128, D), np.float32)
    bct[0] = inp["norm1_g"][0][None, :]
    bct[1] = inp["norm2_g"][0][None, :]
    bct[2] = inp["norm1_g"][1][None, :]
    bct[3] = inp["norm2_g"][1][None, :]
    bct[4] = inp["mem_norm_g"][0][None, :]
    bct[5] = inp["mem_norm_g"][1][None, :]
    qmb = np.ascontiguousarray(np.broadcast_to(inp["b_b_in"][0][1536:1792][None, :], (128, 256))).astype(np.float32)
    rb = inp["a_rel_bias"][0]
    k = np.arange(128)[:, None]
    q = np.arange(128)[None, :]
    bias = np.zeros((128, 6, 4, 128), np.float32)
    for h in range(12):
        for dl in range(2):
            idx = np.clip(q - k + 128 * dl, -63, 128) + 63
            bias[:, h // 2, dl * 2 + (h % 2), :] = rb[h][idx]
    dg = np.zeros((6, 128, 31, 128), np.float32)
    ar = np.arange(128)
    for j in range(6):
        dg[j, ar, :, ar] = cw[:, j * 128:(j + 1) * 128].T
    return cst, bct, qmb, bias.reshape(128, 6 * 4 * 128), dg.reshape(6, 128, 31 * 128)


_NC_CACHE = {}


def kernel(**inputs):
    inp = {k: np.ascontiguousarray(np.asarray(v, dtype=np.float32)) for k, v in inputs.items()}
    cst, bct, qmb, bias, dg = _pack_consts(inp)
    key = "full"
    if key not in _NC_CACHE:
        _NC_CACHE[key] = build_program()
    nc = _NC_CACHE[key]
    in_maps = []
    for c in range(8):
        in_maps.append({
            "x": inp["x"][2 * c:2 * c + 2], "mem": inp["mem"][2 * c:2 * c + 2],
            "a_w_in": inp["a_w_in"], "b_w_in": inp["b_w_in"], "w_mem_kv": inp["w_mem_kv"],
            "w_out": inp["w_out"], "w_gate": inp["w_gate"], "w_up": inp["w_up"], "w_down": inp["w_down"],
            "cst": cst, "bct": bct, "qmb": qmb, "biasT": bias, "convdiag": dg,
        })
    res = run_bass_kernel_spmd(nc, in_maps, core_ids=list(range(8)))
    out = np.concatenate([r["out"] for r in res.results], axis=0)
    return out.astype(np.float32)
```
